# Optimizing a Trainium2 kernel written in Bass

```python
import jax, jax.numpy as jnp
from jax import lax
import numpy as np

D_MODEL = 1024
BATCH = 4
SEQ = 4096
DEPTH = 2
DEC_BATCH = 8
DEC_SEQ = 16
PAST_LEN = 2048

CHUNK = 64
Q_BLOCK = 128
N_MEM = 256

MLA_HEADS = 4
MLA_NOPE = 128
MLA_ROPE = 64
MLA_V = 128
MLA_Q_RANK = 384
MLA_KV_RANK = 256
MLA_WIDTH = MLA_HEADS * MLA_V
FOX_HEADS = 4
FOX_DIM = 64
FOX_WIDTH = FOX_HEADS * FOX_DIM
MEM_HEADS = 4
MEM_DIM = 64
MEM_WIDTH = MEM_HEADS * MEM_DIM
D_MIX = MLA_WIDTH + FOX_WIDTH + MEM_WIDTH

ROPE_THETA = 10000.0
NORM_EPS = 1e-6
NEG_INF = -1e30
MLA_SCALE = (MLA_NOPE + MLA_ROPE) ** -0.5
FOX_SCALE = FOX_DIM ** -0.5
MEM_SCALE = MEM_DIM ** -0.5
DEEPNORM_ALPHA = (2 * DEPTH) ** 0.25
DEEPNORM_BETA = (8 * DEPTH) ** -0.25

IN_SPLITS = (MLA_Q_RANK, MLA_KV_RANK, MLA_ROPE, MLA_WIDTH,
             FOX_WIDTH, FOX_WIDTH, FOX_WIDTH, FOX_HEADS, FOX_WIDTH,
             MEM_WIDTH, MEM_WIDTH)
D_IN = sum(IN_SPLITS)

kernel_name = 'hybrid_mla_fox_mem_streaming_step'


def _rms_norm(x, g):
    xf = x.astype(jnp.float32)
    y = xf * lax.rsqrt(jnp.mean(xf * xf, axis=-1, keepdims=True) + NORM_EPS)
    return (y * g.astype(jnp.float32)).astype(x.dtype)


def _layer_norm(x, g, b):
    xf = x.astype(jnp.float32)
    mu = jnp.mean(xf, axis=-1, keepdims=True)
    xc = xf - mu
    var = jnp.mean(xc * xc, axis=-1, keepdims=True)
    y = xc * lax.rsqrt(var + NORM_EPS) * g.astype(jnp.float32) + b.astype(jnp.float32)
    return y.astype(x.dtype)


def _rope(x, pos):
    half = x.shape[-1] // 2
    inv = ROPE_THETA ** (-jnp.arange(half, dtype=jnp.float32) / half)
    ang = pos.astype(jnp.float32)[:, None] * inv[None, :]
    shape = (1, pos.shape[0]) + (1,) * (x.ndim - 3) + (half,)
    cos = jnp.cos(ang).reshape(shape)
    sin = jnp.sin(ang).reshape(shape)
    x1 = x[..., :half].astype(jnp.float32)
    x2 = x[..., half:].astype(jnp.float32)
    return jnp.concatenate([x1 * cos - x2 * sin, x1 * sin + x2 * cos], axis=-1).astype(x.dtype)


def _attend(q, k, v, q_pos, k_pos, scale, chunk_causal, cq=None, ck=None):
    B, Tq, H, _ = q.shape
    Dv = v.shape[-1]
    k_idx = k_pos // CHUNK if chunk_causal else k_pos
    ck_t = None if ck is None else jnp.transpose(ck, (0, 2, 1))

    def block(args):
        qb, pb, cqb = args
        s = jnp.einsum('bqhd,bkhd->bhqk', qb, k).astype(jnp.float32) * scale
        if cqb is not None:
            s = s + (jnp.transpose(cqb, (0, 2, 1))[:, :, :, None] - ck_t[:, :, None, :])
        q_idx = pb // CHUNK if chunk_causal else pb
        allowed = k_idx[None, :] <= q_idx[:, None]
        s = jnp.where(allowed[None, None], s, NEG_INF)
        p = jax.nn.softmax(s, axis=-1).astype(v.dtype)
        return jnp.einsum('bhqk,bkhd->bqhd', p, v)

    if Tq > Q_BLOCK and Tq % Q_BLOCK == 0:
        nb = Tq // Q_BLOCK
        qs = jnp.transpose(q.reshape(B, nb, Q_BLOCK, H, q.shape[-1]), (1, 0, 2, 3, 4))
        ps = q_pos.reshape(nb, Q_BLOCK)
        cs = None if cq is None else jnp.transpose(cq.reshape(B, nb, Q_BLOCK, H), (1, 0, 2, 3))
        out = lax.map(block, (qs, ps, cs))
        return jnp.transpose(out, (1, 0, 2, 3, 4)).reshape(B, Tq, H, Dv)
    return block((q, q_pos, cq))


def _mem_kv(mem, w_mem_kv):
    B, M, _ = mem.shape
    kv = jnp.einsum('bmd,de->bme', mem, w_mem_kv)
    mk = kv[..., :MEM_WIDTH].reshape(B, M, MEM_HEADS, MEM_DIM)
    mv = kv[..., MEM_WIDTH:].reshape(B, M, MEM_HEADS, MEM_DIM)
    return mk, mv


def _layer(x, pos, mem_k, mem_v, past, w_in, b_f, q_norm_g, kv_norm_g, w_uq, w_ukv, w_out, ln_g, ln_b):
    B, T, _ = x.shape
    h = jnp.einsum('btd,de->bte', x, w_in)
    split_idx = np.cumsum(IN_SPLITS)[:-1].tolist()
    (c_q, c_kv, k_rope, g_mla, f_q, f_k, f_v, f_logit, g_fox, m_q, g_mem) = jnp.split(h, split_idx, axis=-1)

    q = jnp.einsum('btr,re->bte', _rms_norm(c_q, q_norm_g), w_uq).reshape(B, T, MLA_HEADS, MLA_NOPE + MLA_ROPE)
    q = jnp.concatenate([q[..., :MLA_NOPE], _rope(q[..., MLA_NOPE:], pos)], axis=-1)
    lat_new = _rms_norm(c_kv, kv_norm_g)
    kr_new = _rope(k_rope, pos)

    fq = f_q.reshape(B, T, FOX_HEADS, FOX_DIM)
    fk_new = f_k.reshape(B, T, FOX_HEADS, FOX_DIM)
    fv_new = f_v.reshape(B, T, FOX_HEADS, FOX_DIM)
    logf_new = jax.nn.log_sigmoid(f_logit.astype(jnp.float32) + b_f.astype(jnp.float32))

    if past is None:
        lat, kr, fk, fv, logf, k_pos = lat_new, kr_new, fk_new, fv_new, logf_new, pos
    else:
        p_lat, p_kr, p_fk, p_fv, p_logf = past
        lat = jnp.concatenate([p_lat, lat_new], axis=1)
        kr = jnp.concatenate([p_kr, kr_new], axis=1)
        fk = jnp.concatenate([p_fk, fk_new], axis=1)
        fv = jnp.concatenate([p_fv, fv_new], axis=1)
        logf = jnp.concatenate([p_logf.astype(jnp.float32), logf_new], axis=1)
        k_pos = jnp.arange(lat.shape[1])
    Tk = lat.shape[1]

    kv = jnp.einsum('btr,re->bte', lat, w_ukv).reshape(B, Tk, MLA_HEADS, MLA_NOPE + MLA_V)
    k_mla = jnp.concatenate([kv[..., :MLA_NOPE],
                             jnp.broadcast_to(kr[:, :, None, :], (B, Tk, MLA_HEADS, MLA_ROPE))], axis=-1)
    o_mla = _attend(q, k_mla, kv[..., MLA_NOPE:], pos, k_pos, MLA_SCALE, True)

    cum = jnp.cumsum(logf, axis=1)
    o_fox = _attend(fq, fk, fv, pos, k_pos, FOX_SCALE, False, cum[:, Tk - T:], cum)

    mq = m_q.reshape(B, T, MEM_HEADS, MEM_DIM)
    s_mem = jnp.einsum('bthd,bmhd->bhtm', mq, mem_k).astype(jnp.float32) * MEM_SCALE
    o_mem = jnp.einsum('bhtm,bmhd->bthd', jax.nn.softmax(s_mem, axis=-1).astype(mem_v.dtype), mem_v)

    mixed = jnp.concatenate([o_mla.reshape(B, T, MLA_WIDTH) * jax.nn.silu(g_mla),
                             o_fox.reshape(B, T, FOX_WIDTH) * jax.nn.silu(g_fox),
                             o_mem.reshape(B, T, MEM_WIDTH) * jax.nn.silu(g_mem)], axis=-1)
    out = jnp.einsum('bte,ed->btd', mixed, w_out)
    y = _layer_norm(DEEPNORM_ALPHA * x + out, ln_g, ln_b)
    return y, (lat_new, kr_new, fk_new, fv_new, logf_new)


def setup_inputs(seed: int = 0) -> dict:
    key = jax.random.key(seed)
    ks = jax.random.split(key, 24)
    f32 = jnp.float32
    nrm = lambda k, shape: jax.random.normal(k, shape, dtype=f32)
    return {
        'x_prompt': nrm(ks[0], (BATCH, SEQ, D_MODEL)),
        'x_sample': nrm(ks[1], (DEC_BATCH, DEC_SEQ, D_MODEL)),
        'cache_mla_latent': nrm(ks[2], (DEPTH, DEC_BATCH, PAST_LEN, MLA_KV_RANK)),
        'cache_mla_krope': nrm(ks[3], (DEPTH, DEC_BATCH, PAST_LEN, MLA_ROPE)),
        'cache_fox_k': nrm(ks[4], (DEPTH, DEC_BATCH, PAST_LEN, FOX_HEADS, FOX_DIM)),
        'cache_fox_v': nrm(ks[5], (DEPTH, DEC_BATCH, PAST_LEN, FOX_HEADS, FOX_DIM)),
        'cache_fox_logf': jax.nn.log_sigmoid(3.0 + nrm(ks[6], (DEPTH, DEC_BATCH, PAST_LEN, FOX_HEADS))),
        'cache_mem_k': nrm(ks[7], (DEPTH, DEC_BATCH, N_MEM, MEM_HEADS, MEM_DIM)),
        'cache_mem_v': nrm(ks[8], (DEPTH, DEC_BATCH, N_MEM, MEM_HEADS, MEM_DIM)),
        'mem_prompt': nrm(ks[9], (BATCH, N_MEM, D_MODEL)),
        'w_in': nrm(ks[10], (DEPTH, D_MODEL, D_IN)) * D_MODEL ** -0.5,
        'b_fox_f': 3.0 + 0.5 * nrm(ks[11], (DEPTH, FOX_HEADS)),
        'mla_q_norm': 1.0 + 0.01 * nrm(ks[12], (DEPTH, MLA_Q_RANK)),
        'mla_kv_norm': 1.0 + 0.01 * nrm(ks[13], (DEPTH, MLA_KV_RANK)),
        'w_uq': nrm(ks[14], (DEPTH, MLA_Q_RANK, MLA_HEADS * (MLA_NOPE + MLA_ROPE))) * MLA_Q_RANK ** -0.5,
        'w_ukv': nrm(ks[15], (DEPTH, MLA_KV_RANK, MLA_HEADS * (MLA_NOPE + MLA_V))) * MLA_KV_RANK ** -0.5,
        'w_mem_kv': nrm(ks[16], (DEPTH, D_MODEL, 2 * MEM_WIDTH)) * D_MODEL ** -0.5,
        'w_out': nrm(ks[17], (DEPTH, D_MIX, D_MODEL)) * (D_MIX ** -0.5 * DEEPNORM_BETA),
        'ln_g': 1.0 + 0.01 * nrm(ks[18], (DEPTH, D_MODEL)),
        'ln_b': 0.01 * nrm(ks[19], (DEPTH, D_MODEL)),
    }


def reference(x_prompt, x_sample, cache_mla_latent, cache_mla_krope, cache_fox_k, cache_fox_v, cache_fox_logf,
              cache_mem_k, cache_mem_v, mem_prompt, w_in, b_fox_f, mla_q_norm, mla_kv_norm, w_uq, w_ukv,
              w_mem_kv, w_out, ln_g, ln_b):
    past_len = cache_mla_latent.shape[2]
    pos_p = jnp.arange(x_prompt.shape[1])
    pos_s = past_len + jnp.arange(x_sample.shape[1])
    yp, ys = x_prompt, x_sample
    p_rows, s_rows, p_mk, p_mv = [], [], [], []
    for l in range(DEPTH):
        weights = (w_in[l], b_fox_f[l], mla_q_norm[l], mla_kv_norm[l], w_uq[l], w_ukv[l], w_out[l], ln_g[l], ln_b[l])
        mk, mv = _mem_kv(mem_prompt, w_mem_kv[l])
        yp, rp = _layer(yp, pos_p, mk, mv, None, *weights)
        past = (cache_mla_latent[l], cache_mla_krope[l], cache_fox_k[l], cache_fox_v[l], cache_fox_logf[l])
        ys, rs = _layer(ys, pos_s, cache_mem_k[l], cache_mem_v[l], past, *weights)
        p_rows.append(rp)
        s_rows.append(rs)
        p_mk.append(mk)
        p_mv.append(mv)
    p_lat, p_kr, p_fk, p_fv, p_logf = [jnp.stack([r[i] for r in p_rows], axis=0) for i in range(5)]
    s_lat, s_kr, s_fk, s_fv, s_logf = [jnp.stack([r[i] for r in s_rows], axis=0) for i in range(5)]
    p_mem_k = jnp.stack(p_mk, axis=0)
    p_mem_v = jnp.stack(p_mv, axis=0)
    return (yp, ys, p_lat, p_kr, p_fk, p_fv, p_logf, p_mem_k, p_mem_v, s_lat, s_kr, s_fk, s_fv, s_logf)
```

```python
import os
import numpy as np
from contextlib import ExitStack
import concourse.bass as bass
import concourse.mybir as mybir
from concourse.bass_utils import run_bass_kernel_spmd

F32 = mybir.dt.float32
BF16 = mybir.dt.bfloat16
AF = mybir.ActivationFunctionType
ALU = mybir.AluOpType

NCORES = 8
D = 1024
NDT = 8
DEPTH = 2
SEQ = 4096
TOWN = 2048
PAST = 2048
DEC = 16
D_IN = 2756
MLA_SCALE = 192.0 ** -0.5
FOX_SCALE = 0.125
MEM_SCALE = 0.125
ALPHA = (2.0 * DEPTH) ** 0.25
EPS = 1e-6
NEG = -30000.0
GROUPS = [[0, 1], [2, 3], [4, 5], [6, 7]]
EX_ROWS = 488

ENGS = ['pe', 'act', 'dve', 'pool', 'sp']
ALIAS_GROUPS = ('S', 'KV', 'LT')


class Op:
    __slots__ = ('eng', 'fn', 'deps', 'signal', 'seq', 'is_dma', 'qn', 'idx', 'is_cc', 'ccsem')


class Prog:
    def __init__(self, nc, es, nslot):
        self.nc = nc
        self.es = es
        self.nslot = nslot
        self.ops = {e: [] for e in ENGS}
        self.last_w = {}
        self.readers = {}
        self.dma_w = {}
        self.epoch_deps = {}
        self.dma_count = {e: 0 for e in ENGS}
        self.nops = 0

    def op(self, eng, fn, reads=(), writes=(), dma=False, cc=False):
        o = Op()
        o.eng = eng; o.fn = fn; o.is_dma = dma; o.signal = False; o.deps = set(); o.seq = None
        o.is_cc = cc; o.ccsem = None
        o.idx = self.nops; self.nops += 1
        reads = list(reads); writes = list(writes)
        extra = set()
        for t in reads + writes:
            if isinstance(t, tuple) and t[0] in ALIAS_GROUPS:
                extra.add(('BAR', t[0]))
        for t in extra:
            if t not in writes:
                reads.append(t)
        for t in reads:
            w = self.last_w.get(t)
            if w is not None:
                o.deps.add(w)
            if isinstance(t, tuple) and t[0] == 'ps':
                for r in self.readers.get(t, ()):
                    if r.eng != eng:
                        o.deps.add(r)
        for t in reads:
            for d in self.dma_w.get(t, ()):
                o.deps.add(d)
        for t in writes:
            w = self.last_w.get(t)
            R = self.readers.get(t, [])
            if dma:
                if w is not None and w.is_dma and not R:
                    for d in self.epoch_deps.get(t, ()):
                        o.deps.add(d)
                else:
                    base = list(R)
                    if w is not None:
                        base.append(w)
                    base += self.dma_w.get(t, [])
                    for d in base:
                        o.deps.add(d)
                    self.epoch_deps[t] = base
                    self.dma_w[t] = []
            else:
                if w is not None:
                    o.deps.add(w)
                for r in R:
                    o.deps.add(r)
                for d in self.dma_w.get(t, ()):
                    o.deps.add(d)
        for t in reads:
            self.readers.setdefault(t, []).append(o)
        for t in writes:
            self.last_w[t] = o
            self.readers[t] = []
            if dma:
                self.dma_w.setdefault(t, []).append(o)
            else:
                self.dma_w[t] = []
        if dma:
            o.qn = self.dma_count[eng]
            self.dma_count[eng] += 1
        o.deps.discard(o)
        self.ops[eng].append(o)
        return o

    def dma(self, q, out, in_, reads=(), writes=()):
        return self.op(q, lambda e: e.dma_start(out=out, in_=in_), reads, writes, dma=True)

    def finalize(self):
        nc = self.nc
        for e in ENGS:
            for o in self.ops[e]:
                nd = set()
                for d in o.deps:
                    if d.is_dma or d.is_cc:
                        nd.add(d)
                    elif d.eng == 'pe' and o.eng == 'pe' and not o.is_dma:
                        continue
                    else:
                        d.signal = True
                        nd.add(d)
                o.deps = nd
        self.esem = {e: self.es.enter_context(nc.semaphore('sem_' + e)) for e in ENGS}
        for e in ENGS:
            c = 0
            for o in self.ops[e]:
                if o.is_cc:
                    o.ccsem = self.es.enter_context(nc.semaphore('cc_%d' % o.idx))
                if (not o.is_dma) and (not o.is_cc) and o.signal:
                    c += 1
                    o.seq = c
        self.ring = {e: [self.es.enter_context(nc.semaphore('ring_%s_%d' % (e, i))) for i in range(self.nslot[e])]
                     for e in ENGS if self.dma_count[e] > 0}

    def _semval(self, d):
        if d.is_cc:
            return d.ccsem, 1
        if d.is_dma:
            ns = self.nslot[d.eng]
            return self.ring[d.eng][d.qn % ns], 16 * (d.qn // ns + 1)
        return self.esem[d.eng], d.seq

    def emit_engine(self, e, eng):
        waited = {}

        def wait(sem, val):
            k = id(sem)
            if waited.get(k, 0) < val:
                eng.wait_ge(sem, val)
                waited[k] = val

        for o in self.ops[e]:
            need = {}
            for d in o.deps:
                sem, val = self._semval(d)
                k = id(sem)
                if k not in need or need[k][1] < val:
                    need[k] = (sem, val)
            for sem, val in need.values():
                wait(sem, val)
            if o.is_dma:
                ns = self.nslot[e]
                slot = o.qn % ns
                if o.qn >= ns:
                    wait(self.ring[e][slot], 16 * (o.qn // ns))
                ins = o.fn(eng)
                ins.then_inc(self.ring[e][slot], 16)
            elif o.is_cc:
                ins = o.fn(eng)
                ins.then_inc(o.ccsem)
            else:
                ins = o.fn(eng)
                if o.signal:
                    ins.then_inc(self.esem[e], 1)
        if e == 'sp':
            for q, n in self.dma_count.items():
                if n == 0:
                    continue
                ns = self.nslot[q]
                for slot in range(min(n, ns)):
                    cnt = (n - 1 - slot) // ns + 1
                    wait(self.ring[q][slot], 16 * cnt)

    def run_block(self):
        nc = self.nc
        self.finalize()
        with nc.Block() as block:
            @block.tensor
            def _(eng):
                self.emit_engine('pe', eng)

            @block.scalar
            def _(eng):
                self.emit_engine('act', eng)

            @block.vector
            def _(eng):
                self.emit_engine('dve', eng)

            @block.gpsimd
            def _(eng):
                self.emit_engine('pool', eng)

            @block.sync
            def _(eng):
                self.emit_engine('sp', eng)


def MM(P, out, lhsT, rhs, start, stop, reads, writes):
    return P.op('pe', lambda e: e.matmul(out, lhsT, rhs, start=start, stop=stop), reads, writes)


def TR(P, out, in_, ident, reads, writes):
    return P.op('pe', lambda e: e.transpose(out, in_, ident), reads, writes)


def ACTF(P, out, in_, func, reads, writes, bias=None, scale=1.0, accum=None):
    def f(e):
        kw = {}
        if bias is not None:
            kw['bias'] = bias
        if accum is not None:
            kw['accum_out'] = accum
        return e.activation(out, in_, func, scale=scale, **kw)
    return P.op('act', f, reads, writes)


def CP(P, eng, out, in_, reads, writes):
    if eng == 'act':
        return P.op('act', lambda e: e.copy(out, in_), reads, writes)
    return P.op(eng, lambda e: e.tensor_copy(out, in_), reads, writes)


def TT(P, eng, out, a, b, op, reads, writes):
    return P.op(eng, lambda e: e.tensor_tensor(out, a, b, op), reads, writes)


def TS(P, eng, out, a, s1, s2, op0, op1, reads, writes):
    if op1 is None:
        return P.op(eng, lambda e: e.tensor_scalar(out, a, s1, None, op0=op0), reads, writes)
    return P.op(eng, lambda e: e.tensor_scalar(out, a, s1, s2, op0=op0, op1=op1), reads, writes)


def STT(P, eng, out, a, s, b, op0, op1, reads, writes):
    return P.op(eng, lambda e: e.scalar_tensor_tensor(out, a, s, b, op0=op0, op1=op1), reads, writes)


def MEMSET(P, eng, ap, val, reads, writes):
    return P.op(eng, lambda e: e.memset(ap, val), reads, writes)


class Carve:
    def __init__(self, scr, nelem):
        self.scr = scr
        self.nelem = nelem
        self.off = 0

    def take(self, shape, dtype):
        n = 1
        for s in shape:
            n *= s
        nel = n * (2 if dtype == F32 else 1)
        nel += nel % 2
        assert self.off + nel <= self.nelem, ("scratch overflow", self.off + nel, self.nelem)
        ap = self.scr[:, self.off:self.off + nel]
        self.off += nel
        if dtype == F32:
            ap = ap.bitcast(F32)
        ap = ap[:, 0:n]
        if len(shape) == 2:
            return ap.rearrange("p (a b) -> p a b", b=shape[1])
        if len(shape) == 3:
            return ap.rearrange("p (a b c) -> p a b c", b=shape[1], c=shape[2])
        return ap


class Ctx:
    pass


def build_program():
    nc = bass.Bass("TRN2", target_bir_lowering=False)

    def dI(n, s, dt=F32):
        return nc.dram_tensor(n, s, dt, kind="ExternalInput").ap()

    def dO(n, s):
        return nc.dram_tensor(n, s, F32, kind="ExternalOutput").ap()

    X = Ctx()
    X.xp = dI("xp", [TOWN, D]); X.xs = dI("xs", [DEC, D])
    X.c_lat = dI("c_lat", [DEPTH, PAST, 256]); X.c_kr = dI("c_kr", [DEPTH, PAST, 64])
    X.c_fk = dI("c_fk", [DEPTH, PAST, 256]); X.c_fv = dI("c_fv", [DEPTH, PAST, 256])
    X.c_logf = dI("c_logf", [DEPTH, PAST, 4])
    X.c_mk = dI("c_mk", [DEPTH, 256, 256]); X.c_mv = dI("c_mv", [DEPTH, 256, 256])
    X.memp = dI("memp", [256, D])
    X.w_in = dI("w_in", [DEPTH, D, D_IN]); X.b_f = dI("b_fox_f", [DEPTH, 4])
    X.qnorm = dI("mla_q_norm", [DEPTH, 384]); X.kvnorm = dI("mla_kv_norm", [DEPTH, 256])
    X.w_uq = dI("w_uq", [DEPTH, 384, 768]); X.w_ukv = dI("w_ukv", [DEPTH, 256, 1024])
    X.w_mem = dI("w_mem_kv", [DEPTH, D, 512]); X.w_out = dI("w_out", [DEPTH, D, D])
    X.ln_g = dI("ln_g", [DEPTH, D]); X.ln_b = dI("ln_b", [DEPTH, D])
    X.ropeP = dI("ropeP", [128, 16, 128]); X.ropeS = dI("ropeS", [16, 1, 128])
    X.masks = dI("masks", [128, 5, 128])

    X.yp = dO("yp", [TOWN, D]); X.ys = dO("ys", [DEC, D])
    X.o_lat = dO("o_lat", [DEPTH, TOWN, 256]); X.o_kr = dO("o_kr", [DEPTH, TOWN, 64])
    X.o_fk = dO("o_fk", [DEPTH, TOWN, 256]); X.o_fv = dO("o_fv", [DEPTH, TOWN, 256])
    X.o_logf = dO("o_logf", [DEPTH, TOWN, 4])
    X.o_mk = dO("o_mk", [DEPTH, 256, 256]); X.o_mv = dO("o_mv", [DEPTH, 256, 256])
    X.s_lat = dO("s_lat", [DEPTH, DEC, 256]); X.s_kr = dO("s_kr", [DEPTH, DEC, 64])
    X.s_fk = dO("s_fk", [DEPTH, DEC, 256]); X.s_fv = dO("s_fv", [DEPTH, DEC, 256])
    X.s_logf = dO("s_logf", [DEPTH, DEC, 4])

    X.wb = {
        'in': nc.dram_tensor("wb_in", [DEPTH, D, D_IN], BF16).ap(),
        'uq': nc.dram_tensor("wb_uq", [DEPTH, 384, 768], BF16).ap(),
        'ukv': nc.dram_tensor("wb_ukv", [DEPTH, 256, 1024], BF16).ap(),
        'mem': nc.dram_tensor("wb_mem", [DEPTH, D, 512], BF16).ap(),
        'out': nc.dram_tensor("wb_out", [DEPTH, D, D], BF16).ap(),
    }
    X.cb = {
        'lat': nc.dram_tensor("cb_lat", [DEPTH, PAST, 256], BF16).ap(),
        'kr': nc.dram_tensor("cb_kr", [DEPTH, PAST, 64], BF16).ap(),
        'fk': nc.dram_tensor("cb_fk", [DEPTH, PAST, 256], BF16).ap(),
        'fv': nc.dram_tensor("cb_fv", [DEPTH, PAST, 256], BF16).ap(),
    }
    X.csrc32 = {'lat': X.c_lat, 'kr': X.c_kr, 'fk': X.c_fk, 'fv': X.c_fv}
    X.cb_ready = set()
    X.bg = []
    X.bg_np = {}
    X.l0_done = False
    X.wsrc32 = {'in': X.w_in, 'uq': X.w_uq, 'ukv': X.w_ukv, 'mem': X.w_mem, 'out': X.w_out}
    X.wb_done = set()
    X.wb_later = []
    X.y1 = nc.dram_tensor("y1_scr", [TOWN, D], F32).ap()
    X.ys1 = nc.dram_tensor("ys1_scr", [DEC, D], F32).ap()
    X.exs = [[nc.dram_tensor("exs_%d_%d" % (l, c), [EX_ROWS, 1024], BF16) for c in range(4)] for l in range(DEPTH)]
    X.exd = [[nc.dram_tensor("exd_%d_%d" % (l, c), [2 * EX_ROWS, 1024], BF16) for c in range(4)] for l in range(DEPTH)]

    with ExitStack() as es:
        P = Prog(nc, es, {'pe': 1, 'act': 8, 'dve': 1, 'pool': 8, 'sp': 16})
        X.P = P

        def sb(n, s, d):
            return es.enter_context(nc.sbuf_tensor(n, s, d))

        X.identb = sb("identb", [128, 128], BF16); X.identf = sb("identf", [128, 128], F32)
        X.onesb = sb("onesb", [128, 128], BF16); X.onesf = sb("onesf", [128, 128], F32)
        X.trif = sb("trif", [128, 128], F32); X.maskb = sb("maskb", [128, 5, 128], BF16)
        X.epsb = sb("epsb", [128, 1], F32); X.oneb = sb("oneb", [128, 1], F32)
        X.dummy = sb("bar_dummy_t", [128, 2], F32)
        X.XT = sb("XT", [128, NDT, TOWN], BF16)
        X.MIXT = sb("MIXT", [128, NDT, TOWN], BF16)
        X.latT = sb("latT", [128, 2, SEQ], BF16)
        X.KV = sb("KV24", [128, 12288], BF16)
        X.cqnT = sb("cqnT", [128, 3, TOWN], BF16)
        X.qrT = sb("qrT", [128, 2, TOWN], BF16)
        X.memT = sb("memT", [128, NDT, 256], BF16)
        X.mkT = sb("mkT", [128, 2, 256], BF16)
        X.mvA = sb("mvA", [128, 2, 384], BF16)
        X.sfkT = sb("sfkT", [128, 2, DEC], BF16)
        X.sfv = sb("sfv", [DEC, 256], BF16)
        X.slogf = sb("slogf", [DEC, 4], F32)
        X.slat = sb("slat", [DEC, 256], BF16)
        nscr = (nc.sbuf_bytes_remaining - 2048) // 2
        nscr -= nscr % 2
        X.nscr = nscr
        X.scr = sb("scr_big_t", [128, nscr], BF16)
        X.pb = [es.enter_context(nc.psum_tensor("pb%d" % i, [128, 512], F32)) for i in range(8)]
        X.KTh = X.KV[:, 0:4096]
        X.Vh = X.KV[:, 4096:8192].rearrange("p (k d) -> p k d", d=128)
        X.krT = X.KV[:, 8192:12288]
        X.krT3 = X.krT.rearrange("p (k t) -> p k t", t=128)
        X.fvA = X.KV[:, :].rearrange("p (k c) -> p k c", c=384)
        X.latT4 = X.latT[:, :, :].rearrange("p r (k t) -> p r k t", t=128)
        X.stageL = X.KV[:, 0:4096].rearrange("p (k c) -> p k c", c=256)
        X.stageK = X.KV[:, 4096:6144].rearrange("p (k c) -> p k c", c=128)

        emit_consts(X)
        pr = Ctx(); pr.kind = 'p'; pr.R = 128; pr.NTB = 16; pr.NT = TOWN
        pr.groups = [(g * 512, 512) for g in range(4)]
        pr.NKB = 32
        sa = Ctx(); sa.kind = 's'; sa.R = DEC; sa.NTB = 1; sa.NT = DEC
        sa.groups = [(0, DEC)]
        sa.NKB = 17
        stop = int(os.environ.get('KDBG_STOP', '99'))
        only_sample = bool(os.environ.get('KDBG_SAMPLE'))
        st = 0
        for pa in ((sa,) if only_sample else (pr, sa)):
            for L in range(DEPTH):
                for ph in ((phase0,) if L == 0 else ()) + (phase1, phase2, phase3):
                    if st < stop:
                        if ph is phase0:
                            ph(X, pa)
                        else:
                            ph(X, pa, L)
                    st += 1
        if os.environ.get('KDBG_STATS'):
            for e in ENGS:
                print('ENG', e, 'ops', len(P.ops[e]), 'dmas', P.dma_count[e])
        P.run_block()
        if os.environ.get('KDBG_STATS'):
            for e in ENGS:
                print('ENG', e, 'signals', max([o.seq or 0 for o in P.ops[e]] + [0]))
    return nc


def bg_add(X, kind, nm, L, npieces):
    src = (X.wsrc32 if kind == 'wb' else X.csrc32)[nm][L]
    dst = (X.wb if kind == 'wb' else X.cb)[nm][L]
    rows = src.shape[0]
    step = rows // npieces
    for i in range(npieces):
        X.bg.append((kind, nm, L, i, npieces, dst[i * step:(i + 1) * step, :], src[i * step:(i + 1) * step, :]))
    X.bg_np[(kind, nm, L)] = npieces


def bg_pump(X, n=1):
    P = X.P
    for _ in range(n):
        if not X.bg:
            return
        kind, nm, L, i, npieces, dst, src = X.bg.pop(0)
        P.dma('pool', dst, src, [], [(kind, nm, L, i)])
        if i == npieces - 1:
            if kind == 'cb':
                X.cb_ready.add((nm, L))
            elif L == 1 or nm == 'out' or X.l0_done:
                X.wb_done.add((nm, L))
            else:
                X.wb_later.append((nm, L))


def emit_weight_cache(X):
    for (nm, L, k) in (('out', 0, 4), ('in', 1, 8), ('uq', 1, 1), ('ukv', 1, 1), ('mem', 1, 2), ('out', 1, 4),
                       ('in', 0, 8), ('uq', 0, 1), ('ukv', 0, 1), ('mem', 0, 2)):
        if ('wb', nm, L) not in X.bg_np:
            bg_add(X, 'wb', nm, L, k)


def emit_cache_cast(X):
    for L in range(DEPTH):
        for (nm, k) in (('lat', 2), ('kr', 1), ('fk', 2), ('fv', 2)):
            bg_add(X, 'cb', nm, L, k)


def cload(X, nm, L, dst, writes):
    P = X.P
    if (nm, L) in X.cb_ready:
        P.dma('sp', dst, X.cb[nm][L].rearrange("(k p) c -> p k c", p=128),
              [('cb', nm, L, i) for i in range(X.bg_np[('cb', nm, L)])], writes)
    else:
        P.dma('pool', dst, X.csrc32[nm][L].rearrange("(k p) c -> p k c", p=128), [], writes)


def wload(X, nm, L, dst, view_fn, writes):
    P = X.P
    if (nm, L) in X.wb_done:
        P.dma('sp', dst, view_fn(X.wb[nm][L]), [('wb', nm, L, i) for i in range(X.bg_np[('wb', nm, L)])], writes)
    else:
        P.dma('pool', dst, view_fn(X.wsrc32[nm][L]), [], writes)


def barrier(X, grp):
    P = X.P
    MEMSET(P, 'pool', X.dummy[:, 0:1], 0.0, [], [('BAR', grp), 'dummy'])


def emit_consts(X):
    P = X.P
    MEMSET(P, 'pool', X.identf[:, :], 0.0, [], ['identf'])
    P.op('pool', lambda e: e.affine_select(out=X.identf[:, :], in_=X.identf[:, :], pattern=[[-1, 128]],
                                           compare_op=ALU.not_equal, fill=1.0, base=0, channel_multiplier=1),
         ['identf'], ['identf'])
    CP(P, 'dve', X.identb[:, :], X.identf[:, :], ['identf'], ['identb'])
    MEMSET(P, 'pool', X.onesf[:, :], 1.0, [], ['onesf'])
    MEMSET(P, 'pool', X.onesb[:, :], 1.0, [], ['onesb'])
    MEMSET(P, 'pool', X.trif[:, :], 1.0, [], ['trif'])
    P.op('pool', lambda e: e.affine_select(out=X.trif[:, :], in_=X.trif[:, :], pattern=[[1, 128]],
                                           compare_op=ALU.is_ge, fill=0.0, base=0, channel_multiplier=-1),
         ['trif'], ['trif'])
    MEMSET(P, 'pool', X.epsb[:, :], EPS, [], ['epsb'])
    MEMSET(P, 'pool', X.oneb[:, :], 1.0, [], ['oneb'])
    P.dma('pool', X.maskb[:, :, :], X.masks[:, :, :], [], ['maskb'])
    MEMSET(P, 'pool', X.mvA[:, :, 64:128], 1.0, [], ['mvA1'])
    MEMSET(P, 'pool', X.mvA[:, :, 256:320], 1.0, [], ['mvA2'])


def transpose_rows_to_XT(X, pa, xtok, tokx, tb, R, use_banks):
    P = X.P
    for half in range(2):
        bi = use_banks[half]
        bank = X.pb[bi]
        for j in range(4):
            dt = half * 4 + j
            TR(P, bank[:, j * 128:j * 128 + R], xtok[0:R, dt * 128:(dt + 1) * 128], X.identf[0:R, 0:R],
               [tokx, 'identf'], [('ps', bi)])
        src = bank[:, :].rearrange("p (a b) -> p a b", b=128)[:, :, 0:R]
        dst = X.XT[:, half * 4:(half + 1) * 4, tb * 128:tb * 128 + R]
        CP(P, 'act' if half == 0 else 'dve', dst, src, [('ps', bi)], [('XT', tb)])


def phase0(X, pa):
    P = X.P
    barrier(X, 'S')
    cv = Carve(X.scr, X.nscr)
    cv.off = X.nscr - 4 * 2048 - 16
    xt = [cv.take([1024], F32) for _ in range(4)]
    src = X.xp if pa.kind == 'p' else X.xs
    R = pa.R
    for tb in range(pa.NTB):
        b = tb % 4
        tok = ('S', 'xtok', b)
        P.dma('sp' if tb % 2 == 0 else 'act', xt[b][0:R, :], src[tb * 128:tb * 128 + R, :], [], [tok])
        transpose_rows_to_XT(X, pa, xt[b], tok, tb, R, (2 * (tb % 2), 2 * (tb % 2) + 1))
    if pa.kind == 'p':
        for mb in range(2):
            b = mb % 2
            tok = ('S', 'xtok', b)
            P.dma('sp', xt[b][:, :], X.memp[mb * 128:(mb + 1) * 128, :], [], [tok])
            for half in range(2):
                bi = 4 + 2 * b + half
                bank = X.pb[bi]
                for j in range(4):
                    dt = half * 4 + j
                    TR(P, bank[:, j * 128:(j + 1) * 128], xt[b][:, dt * 128:(dt + 1) * 128], X.identf[:, :],
                       [tok, 'identf'], [('ps', bi)])
                srcv = bank[:, :].rearrange("p (a b) -> p a b", b=128)
                CP(P, 'act' if half == 0 else 'dve', X.memT[:, half * 4:(half + 1) * 4, mb * 128:(mb + 1) * 128],
                   srcv, [('ps', bi)], ['memT'])


WC = 384 + 388 + 512


def phase1(X, pa, L):
    P = X.P
    R = pa.R
    early = (pa.kind == 'p' and L == 0)
    if not early:
        barrier(X, 'S')
        barrier(X, 'KV')
        barrier(X, 'LT')
    cv = Carve(X.scr, X.nscr)
    WA = cv.take([NDT, WC], BF16)
    WQR = cv.take([3, 512], BF16)
    rope = cv.take([16, 128], F32)
    kvg = cv.take([256], F32)
    qg = cv.take([384], F32)
    bfb = cv.take([4], F32)
    junk = cv.take([384], BF16)
    NB = 2
    cqn_b = [cv.take([384], BF16) for _ in range(NB)]
    lat_f = [cv.take([256], F32) for _ in range(NB)]
    lat_b = [cv.take([256], BF16) for _ in range(NB)]
    kr_f = [cv.take([64], F32) for _ in range(NB)]
    kr_t = [cv.take([64], F32) for _ in range(NB)]
    kr_b = [cv.take([128], BF16) for _ in range(NB)]
    fkv_f = [cv.take([512], F32) for _ in range(NB)]
    fk_b = [cv.take([256], BF16) for _ in range(NB)]
    stat = [cv.take([8], F32) for _ in range(NB)]
    qro_a = [cv.take([256], F32) for _ in range(NB)]
    qro_t = [cv.take([256], F32) for _ in range(NB)]
    qro_b = [cv.take([256], BF16) for _ in range(NB)]
    latT_st = [cv.take([2, 512], BF16) for _ in range(2)]
    krT_st = [cv.take([512], BF16) for _ in range(2)]
    fkT_st = [cv.take([2, 512], BF16) for _ in range(2)]
    fv_st = [cv.take([4, 384], BF16) for _ in range(2)]
    zc = [cv.take([4, 4], F32) for _ in range(2)]
    ec = [cv.take([4, 4], F32) for _ in range(2)]
    lgf = [cv.take([4, 4], F32) for _ in range(2)]
    lg3 = [cv.take([4, 16], BF16) for _ in range(2)]
    r1 = [cv.take([4, 4], F32) for _ in range(2)]
    r2 = [cv.take([4, 4], F32) for _ in range(2)]

    tWA = ('S', 'WA'); tWQR = ('S', 'WQR')
    for (d0, d1, s0, s1) in ((0, 384, 0, 384), (384, 704, 384, 704), (704, 736, 672, 704), (736, 768, 640, 672),
                             (768, 772, 1984, 1988), (772, 1284, 1472, 1984)):
        wload(X, 'in', L, WA[:, :, d0:d1],
              (lambda a, b_: (lambda w: w.rearrange("(dt p) e -> p dt e", p=128)[:, :, a:b_]))(s0, s1), [tWA])
    for rt in range(3):
        WQ4 = WQR[:, rt, :].rearrange("p (s h c) -> p s h c", s=2, h=4)
        u3f = lambda a, b_, rt_: (lambda w: w.rearrange("(rt p) (h c) -> p rt h c", p=128, h=4)[:, rt_, :, a:b_])
        wload(X, 'uq', L, WQ4[:, 0, :, :], u3f(128, 192, rt), [tWQR])
        wload(X, 'uq', L, WQ4[:, 1, :, 0:32], u3f(160, 192, rt), [tWQR])
        wload(X, 'uq', L, WQ4[:, 1, :, 32:64], u3f(128, 160, rt), [tWQR])
    if early:
        barrier(X, 'S')
        barrier(X, 'KV')
        barrier(X, 'LT')
        if not os.environ.get('KDBG_NOWB'):
            X.l0_done = True
            for (nm, k) in (('mem', 2), ('ukv', 1), ('uq', 1), ('in', 8)):
                bg_add(X, 'wb', nm, 0, k)
    tvec = ('S', 'vec')
    if pa.kind == 'p':
        P.dma('sp', rope[:, :, :], X.ropeP[:, :, :], [], [tvec])
    else:
        P.dma('sp', rope[0:DEC, 0:1, :], X.ropeS[:, :, :], [], [tvec])
    P.dma('sp', kvg[:, :], X.kvnorm[L:L + 1, :].partition_broadcast(128), [], [tvec])
    P.dma('sp', qg[:, :], X.qnorm[L:L + 1, :].partition_broadcast(128), [], [tvec])
    P.dma('sp', bfb[:, :], X.b_f[L:L + 1, :].partition_broadcast(128), [], [tvec])

    if pa.kind == 'p':
        for i in range(2):
            MEMSET(P, 'pool', lg3[i][:, :, :], 0.0, [], [('S', 'lg3', i)])
            MEMSET(P, 'pool', fv_st[i][:, :, 64:128], 1.0, [], [('S', 'fv_st', i)])
            MEMSET(P, 'pool', fv_st[i][:, :, 256:320], 1.0, [], [('S', 'fv_st', i)])
        o_lat, o_kr, o_fk, o_fv, o_logf = X.o_lat, X.o_kr, X.o_fk, X.o_fv, X.o_logf
    else:
        o_lat, o_kr, o_fk, o_fv, o_logf = X.s_lat, X.s_kr, X.s_fk, X.s_fv, X.s_logf

    pbT = X.pb[6][:, :].bitcast(BF16)
    pbT2 = X.pb[5][:, :].bitcast(BF16)
    def body(tb, stage):
        b = tb % 2
        ci = tb // 4
        cb = ci % 2
        bi4 = tb % 4
        r0 = tb * 128
        tok = lambda n: ('S', n, b)
        bA = 0 + b; bB = 2 + b; bC = 4
        if stage == 1:
            for dt in range(NDT):
                lhsT = X.XT[:, dt, r0:r0 + R]
                st = (dt == 0); sp = (dt == NDT - 1)
                MM(P, X.pb[bA][0:R, 0:384], lhsT, WA[:, dt, 0:384], st, sp, [('XT', tb), tWA], [('ps', bA)])
                MM(P, X.pb[bB][0:R, 0:388], lhsT, WA[:, dt, 384:772], st, sp, [('XT', tb), tWA], [('ps', bB)])
                MM(P, X.pb[bC][0:R, 0:512], lhsT, WA[:, dt, 772:1284], st, sp, [('XT', tb), tWA], [('ps', bC)])
            st_ = stat[b]
            ACTF(P, junk[0:R, :], X.pb[bA][0:R, 0:384], AF.Square, [('ps', bA)], [('S', 'junk'), tok('stat')],
                 accum=st_[0:R, 0:1])
            ACTF(P, junk[0:R, 0:256], X.pb[bB][0:R, 0:256], AF.Square, [('ps', bB)], [('S', 'junk'), tok('stat')],
                 accum=st_[0:R, 1:2])
            CP(P, 'act', fkv_f[b][0:R, :], X.pb[bC][0:R, 0:512], [('ps', bC)], [tok('fkv_f')])
            TT(P, 'dve', kr_f[b][0:R, :], X.pb[bB][0:R, 256:320], rope[0:R, tb, 0:64], ALU.mult,
               [('ps', bB), tvec], [tok('kr_f')])
            TT(P, 'dve', kr_t[b][0:R, :], X.pb[bB][0:R, 320:384], rope[0:R, tb, 64:128], ALU.mult,
               [('ps', bB), tvec], [tok('kr_t')])
            TT(P, 'dve', kr_f[b][0:R, :], kr_f[b][0:R, :], kr_t[b][0:R, :], ALU.add,
               [tok('kr_f'), tok('kr_t')], [tok('kr_f')])
            TT(P, 'dve', zc[cb][0:R, bi4, :], X.pb[bB][0:R, 384:388], bfb[0:R, :], ALU.add,
               [('ps', bB), tvec], [('S', 'zc', cb)])
            ACTF(P, st_[0:R, 2:3], st_[0:R, 0:1], AF.Ln, [tok('stat'), 'epsb'], [tok('stat')],
                 bias=X.epsb[0:R, :], scale=1.0 / 384)
            ACTF(P, st_[0:R, 3:4], st_[0:R, 1:2], AF.Ln, [tok('stat'), 'epsb'], [tok('stat')],
                 bias=X.epsb[0:R, :], scale=1.0 / 256)
            ACTF(P, st_[0:R, 4:6], st_[0:R, 2:4], AF.Exp, [tok('stat')], [tok('stat')], scale=-0.5)
            P.dma('sp', o_kr[L, r0:r0 + R, :], kr_f[b][0:R, :], [tok('kr_f')], [])
            CP(P, 'pool', kr_b[b][0:R, 0:64], kr_f[b][0:R, :], [tok('kr_f')], [tok('kr_b')])
            CP(P, 'pool', kr_b[b][0:R, 64:128], kr_f[b][0:R, :], [tok('kr_f')], [tok('kr_b')])
            P.dma('sp', o_fk[L, r0:r0 + R, :], fkv_f[b][0:R, 0:256], [tok('fkv_f')], [])
            P.dma('sp', o_fv[L, r0:r0 + R, :], fkv_f[b][0:R, 256:512], [tok('fkv_f')], [])
            CP(P, 'pool', fk_b[b][0:R, :], fkv_f[b][0:R, 0:256], [tok('fkv_f')], [tok('fk_b')])
            if pa.kind == 'p':
                for (d0, d1, s0, s1) in ((0, 64, 256, 320), (128, 256, 320, 448), (320, 384, 448, 512)):
                    CP(P, 'pool', fv_st[cb][0:R, bi4, d0:d1], fkv_f[b][0:R, s0:s1], [tok('fkv_f')], [('S', 'fv_st', cb)])
            else:
                CP(P, 'pool', X.sfv[0:R, :], fkv_f[b][0:R, 256:512], [tok('fkv_f')], ['sfv'])
            STT(P, 'dve', cqn_b[b][0:R, :], X.pb[bA][0:R, 0:384], st_[0:R, 4:5], qg[0:R, :], ALU.mult, ALU.mult,
                [('ps', bA), tok('stat'), tvec], [tok('cqn_b')])
            STT(P, 'dve', lat_f[b][0:R, :], X.pb[bB][0:R, 0:256], st_[0:R, 5:6], kvg[0:R, :], ALU.mult, ALU.mult,
                [('ps', bB), tok('stat'), tvec], [tok('lat_f')])
            P.dma('sp', o_lat[L, r0:r0 + R, :], lat_f[b][0:R, :], [tok('lat_f')], [])
            CP(P, 'pool', lat_b[b][0:R, :], lat_f[b][0:R, :], [tok('lat_f')], [tok('lat_b')])
            if pa.kind == 's':
                CP(P, 'pool', X.slat[0:R, :], lat_f[b][0:R, :], [tok('lat_f')], ['slat'])
        if stage == 2:
            for j in range(3):
                TR(P, pbT[:, j * 128:j * 128 + R], cqn_b[b][0:R, j * 128:(j + 1) * 128], X.identb[0:R, 0:R],
                   [tok('cqn_b'), 'identb'], [('ps', 6)])
            for j in range(2):
                TR(P, pbT[:, 384 + j * 128:384 + j * 128 + R], lat_b[b][0:R, j * 128:(j + 1) * 128], X.identb[0:R, 0:R],
                   [tok('lat_b'), 'identb'], [('ps', 6)])
            TR(P, pbT[:, 640:640 + R], kr_b[b][0:R, :], X.identb[0:R, 0:R], [tok('kr_b'), 'identb'], [('ps', 6)])
            for j in range(2):
                TR(P, pbT[:, 768 + j * 128:768 + j * 128 + R], fk_b[b][0:R, j * 128:(j + 1) * 128], X.identb[0:R, 0:R],
                   [tok('fk_b'), 'identb'], [('ps', 6)])
            v3 = pbT[:, 0:384].rearrange("p (a b) -> p a b", b=128)[:, :, 0:R]
            CP(P, 'act', X.cqnT[:, :, r0:r0 + R], v3, [('ps', 6)], [('cqnT', tb)])
            vl = pbT[:, 384:640].rearrange("p (a b) -> p a b", b=128)[:, :, 0:R]
            vf = pbT[:, 768:1024].rearrange("p (a b) -> p a b", b=128)[:, :, 0:R]
            if pa.kind == 'p':
                CP(P, 'dve', latT_st[cb][:, :, bi4 * 128:bi4 * 128 + R], vl, [('ps', 6)], [('S', 'latT_st', cb)])
                CP(P, 'act', krT_st[cb][0:64, bi4 * 128:bi4 * 128 + R], pbT[0:64, 640:640 + R], [('ps', 6)],
                   [('S', 'krT_st', cb)])
                CP(P, 'dve', fkT_st[cb][:, :, bi4 * 128:bi4 * 128 + R], vf, [('ps', 6)], [('S', 'fkT_st', cb)])
            else:
                CP(P, 'dve', X.latT[:, :, PAST:PAST + R], vl, [('ps', 6)], [('LT', 'new')])
                CP(P, 'act', X.krT[:, PAST:PAST + R], pbT[:, 640:640 + R], [('ps', 6)], [('KV', 'krnew')])
                CP(P, 'dve', X.sfkT[:, :, 0:R], vf, [('ps', 6)], ['sfkT'])
            for rt in range(3):
                MM(P, X.pb[7][0:R, 0:512], X.cqnT[:, rt, r0:r0 + R], WQR[:, rt, :], rt == 0, rt == 2,
                   [('cqnT', tb), tWQR], [('ps', 7)])
            cc4 = rope[0:R, tb, 0:64].unsqueeze(1).broadcast_to([R, 4, 64])
            ss4 = rope[0:R, tb, 64:128].unsqueeze(1).broadcast_to([R, 4, 64])
            q4 = lambda ap: ap.rearrange("p (h c) -> p h c", h=4)
            TT(P, 'dve', q4(qro_a[b][0:R, :]), q4(X.pb[7][0:R, 0:256]), cc4, ALU.mult, [('ps', 7), tvec], [tok('qro_a')])
            TT(P, 'dve', q4(qro_t[b][0:R, :]), q4(X.pb[7][0:R, 256:512]), ss4, ALU.mult, [('ps', 7), tvec], [tok('qro_t')])
            TT(P, 'pool', qro_b[b][0:R, :], qro_a[b][0:R, :], qro_t[b][0:R, :], ALU.add,
               [tok('qro_a'), tok('qro_t')], [tok('qro_b')])
        if stage == 3:
            for j in range(2):
                TR(P, pbT2[:, j * 128:j * 128 + R], qro_b[b][0:R, j * 128:(j + 1) * 128], X.identb[0:R, 0:R],
                   [tok('qro_b'), 'identb'], [('ps', 5)])
            vq = pbT2[:, 0:256].rearrange("p (a b) -> p a b", b=128)[:, :, 0:R]
            CP(P, 'act', X.qrT[:, :, r0:r0 + R], vq, [('ps', 5)], [('qrT', tb)])

            last_in_chunk = (bi4 == 3) or (tb == pa.NTB - 1)
            if last_in_chunk:
                nb = bi4 + 1
                z2 = zc[cb][0:R, 0:nb, :]
                ACTF(P, ec[cb][0:R, 0:nb, :], z2, AF.Exp, [('S', 'zc', cb)], [('S', 'ec', cb)], scale=-1.0)
                ACTF(P, ec[cb][0:R, 0:nb, :], ec[cb][0:R, 0:nb, :], AF.Ln, [('S', 'ec', cb), 'oneb'], [('S', 'ec', cb)],
                     bias=X.oneb[0:R, :], scale=1.0)
                TS(P, 'dve', lgf[cb][0:R, 0:nb, :], ec[cb][0:R, 0:nb, :], -1.0, None, ALU.mult, None,
                   [('S', 'ec', cb)], [('S', 'lgf', cb)])
                dst = o_logf[L, ci * 512:ci * 512 + (nb - 1) * 128 + R, :]
                if pa.kind == 'p':
                    P.dma('sp', dst.rearrange("(k p) h -> p k h", p=128), lgf[cb][:, :, :], [('S', 'lgf', cb)], [])
                    l3 = lg3[cb][:, :, 0:12].rearrange("p k (s h) -> p k s h", s=3)
                    CP(P, 'dve', l3[:, :, 0, :], lgf[cb][:, :, :], [('S', 'lgf', cb)], [('S', 'lg3', cb)])
                    TT(P, 'dve', r1[cb][:, :, :], lgf[cb][:, :, :], l3[:, :, 0, :], ALU.subtract,
                       [('S', 'lgf', cb), ('S', 'lg3', cb)], [('S', 'r1', cb)])
                    CP(P, 'dve', l3[:, :, 1, :], r1[cb][:, :, :], [('S', 'r1', cb)], [('S', 'lg3', cb)])
                    TT(P, 'dve', r2[cb][:, :, :], r1[cb][:, :, :], l3[:, :, 1, :], ALU.subtract,
                       [('S', 'r1', cb), ('S', 'lg3', cb)], [('S', 'r2', cb)])
                    CP(P, 'dve', l3[:, :, 2, :], r2[cb][:, :, :], [('S', 'r2', cb)], [('S', 'lg3', cb)])
                    exs = X.exs[L][ci]
                    ea = exs.ap()
                    tex = ('exs', L, ci)
                    reg = lambda a, b_, c_: ea[a:b_, :].rearrange("a (b c) -> (a b) c", c=c_)
                    P.dma('sp', reg(0, 128, 512).rearrange("(rt p) t -> p rt t", p=128), latT_st[cb][:, :, :],
                          [('S', 'latT_st', cb)], [tex])
                    P.dma('sp', reg(128, 256, 512).rearrange("(rt p) t -> p rt t", p=128), fkT_st[cb][:, :, :],
                          [('S', 'fkT_st', cb)], [tex])
                    P.dma('sp', reg(256, 448, 128).rearrange("(k p x) c -> p k (x c)", p=128, x=3), fv_st[cb][:, :, :],
                          [('S', 'fv_st', cb)], [tex])
                    P.dma('sp', reg(448, 480, 512), krT_st[cb][0:64, :], [('S', 'krT_st', cb)], [tex])
                    P.dma('sp', reg(480, 488, 16).rearrange("(k p) c -> p k c", p=128), lg3[cb][:, :, :],
                          [('S', 'lg3', cb)], [tex])
                    exd = X.exd[L][ci]
                    if os.environ.get('KDBG_NOCC'):
                        P.dma('sp', exd.ap()[0:EX_ROWS, :], ea[:, :], [tex], [('exd', L, ci)])
                        P.dma('sp', exd.ap()[EX_ROWS:2 * EX_ROWS, :], ea[:, :], [tex], [('exd', L, ci)])
                    else:
                      P.op('pool', (lambda exs_, exd_: (lambda e: e.collective_compute(
                        "AllGather", ALU.bypass, replica_groups=GROUPS,
                        ins=[exs_.ap().opt()], outs=[exd_.ap().opt()])))(exs, exd),
                        [tex], [('exd', L, ci)], cc=True)
                else:
                    P.dma('sp', dst, lgf[cb][0:R, 0, :], [('S', 'lgf', cb)], [])
                    CP(P, 'pool', X.slogf[0:R, :], lgf[cb][0:R, 0, :], [('S', 'lgf', cb)], ['slogf'])

    if pa.kind == 's':
        sample_kprep(X, L)
    for i in range(pa.NTB + 2):
        if i < pa.NTB:
            body(i, 1)
            if early:
                bg_pump(X, 1)
        if 0 <= i - 1 < pa.NTB:
            body(i - 1, 2)
        if 0 <= i - 2 < pa.NTB:
            body(i - 2, 3)


def vis_prompt(j, kb, kind):
    if kb < 8 * j:
        return (0, None)
    m = kb - 8 * j
    if m > 7:
        return None
    c0 = (m // 2) * 128
    base = 0 if kind == 'mla' else 2
    return (c0, base + (m % 2))


def phase2(X, pa, L):
    P = X.P
    barrier(X, 'S')
    cv = Carve(X.scr, X.nscr)
    Y = Ctx()
    Y.WU = cv.take([2, 1024], BF16)
    Y.WQN = cv.take([3, 512], BF16)
    Y.wcol = [cv.take([NDT, 128], BF16) for _ in range(4)]
    Y.q_b = [cv.take([512], BF16) for _ in range(2)]
    Y.g_b = [cv.take([512], BF16) for _ in range(2)]
    Y.th = [cv.take([512], F32) for _ in range(2)]
    Y.PT = [cv.take([512], BF16) for _ in range(3)]
    Y.Rinv = [cv.take([512], F32) for _ in range(2)]
    Y.tmpO = [cv.take([512], F32) for _ in range(2)]
    Y.lgall = cv.take([32, 16], BF16)
    Y.lgA = cv.take([32, 4], F32)
    Y.lgB = cv.take([32, 4], F32)
    Y.logf = cv.take([32, 4], F32)
    Y.cumin = cv.take([32, 4], F32)
    Y.tot = cv.take([32, 4], F32)
    Y.scA = cv.take([32, 4], F32)
    Y.scB = cv.take([32, 4], F32)
    Y.cum = cv.take([32, 4], F32)
    Y.bias = [cv.take([32], F32) for _ in range(2)]
    Y.CR = [cv.take([512], BF16) for _ in range(2)]
    Y.d4 = cv.take([4], F32)
    Y.r4 = cv.take([4], F32)
    Y.hi4 = cv.take([4], BF16)
    Y.lo4 = cv.take([4], BF16)
    Y.WM = cv.take([NDT, 512], BF16)
    Y.mkv_f = [cv.take([512], F32) for _ in range(2)]
    Y.mk_b = cv.take([256], BF16)
    Y.stage = Y.WM[:, :, :].rearrange("p a b -> p (a b)").rearrange("p (a b) -> p a b", b=256)
    Y.qz = [[cv.take([512], BF16) for _ in range(2)] for _ in range(2)]
    Y.qrz = [[cv.take([512], BF16) for _ in range(2)] for _ in range(2)]
    Y.stkr = cv.take([16, 128], BF16)
    Y.WUT = cv.take([8, 128], BF16)
    Y.qn_all = cv.take([64], BF16)
    Y.qabs = cv.take([2, 64], BF16)
    Y.g_all = cv.take([64], BF16)
    Y.th_all = cv.take([64], F32)
    Y.qr_all = cv.take([64], BF16)
    Y.OLn = cv.take([2, 64], BF16)
    Y.Rs = cv.take([64], F32)
    print('phase2 scratch used', cv.off * 2, 'of', X.nscr * 2) if os.environ.get('KDBG_STATS') else None
    Y.wcol_n = 0
    Y.qbuf_n = 0
    Y.od_n = 0
    Y.unit_n = 0

    wload(X, 'mem', L, Y.WM[:, :, :], lambda w: w.rearrange("(dt p) e -> p dt e", p=128), [('S', 'WM')])
    mem_kside(X, pa, L, Y)
    for a in range(2):
        for b_ in range(2):
            MEMSET(P, 'pool', Y.qz[a][b_][:, :], 0.0, [], [('S', 'q_b', a)])
            MEMSET(P, 'pool', Y.qrz[a][b_][:, :], 0.0, [], [('S', 'qrz', a, b_)])
    barrier(X, 'KV')
    barrier(X, 'LT')
    pre_m = {0: (load_wcol(X, L, Y, 2244), load_wcol(X, L, Y, 2500)),
             1: (load_wcol(X, L, Y, 2244 + 128), load_wcol(X, L, Y, 2500 + 128))}
    wload(X, 'ukv', L, Y.WU[:, :, :], lambda w: w.rearrange("(rt p) e -> p rt e", p=128), [('S', 'WU')])
    for rt in range(3):
        wload(X, 'uq', L, Y.WQN[:, rt, :].rearrange("p (h c) -> p h c", h=4),
              (lambda rt_: (lambda w: w.rearrange("(rt p) (h c) -> p rt h c", p=128, h=4)[:, rt_, :, 0:128]))(rt),
              [('S', 'WQN')])
    load_kside_mla(X, pa, L, Y)
    load_logf(X, pa, L, Y, 'dma')
    for p in range(2):
        pair_attention(X, pa, L, Y, 'mem', p, pre_m.get(p))
    load_logf(X, pa, L, Y, 'cmp')
    fox_bias_prep(X, pa, L, Y)
    pre_w = {}
    if pa.kind == 'p' and L == 1 and not os.environ.get('KDBG_NOWB'):
        emit_cache_cast(X)
    if pa.kind == 'p' and L == 0 and not os.environ.get('KDBG_NOWB'):
        for h in range(3):
            pre_w[h] = load_wcol(X, L, Y, 704 + h * 128)
        emit_weight_cache(X)
    def fk_after_last_decompress():
        barrier(X, 'LT')
        load_kside_fox(X, pa, L, Y, 'fk')
    if pa.kind == 's' and not os.environ.get('KDBG_NOABS'):
        mla_sample_absorbed(X, pa, L, Y)
        fk_after_last_decompress()
    else:
        for h in range(4):
            mla_head(X, pa, L, Y, h, fk_after_last_decompress if h == 3 else None, pre_w.get(h))
    pre_f = {p: (load_wcol(X, L, Y, 1216 + p * 128), load_wcol(X, L, Y, 1988 + p * 128)) for p in range(2)}
    barrier(X, 'KV')
    load_kside_fox(X, pa, L, Y, 'fv')
    for p in range(2):
        pair_attention(X, pa, L, Y, 'fox', p, pre_f[p])
    bg_pump(X, len(X.bg))
    if pa.kind == 'p' and L == 0:
        X.l0_done = True
    for it in X.wb_later:
        X.wb_done.add(it)
    X.wb_later = []


def load_wcol(X, L, Y, c0):
    P = X.P
    i = Y.wcol_n % 4
    Y.wcol_n += 1
    wload(X, 'in', L, Y.wcol[i][:, :, :],
          lambda w: w.rearrange("(dt p) e -> p dt e", p=128)[:, :, c0:c0 + 128], [('S', 'wcol', i)])
    return Y.wcol[i], ('S', 'wcol', i)


def proj_feature_major(X, pa, Y, w, tw, g0, n, bank):
    P = X.P
    tbs = sorted(set(range(g0 // 128, (g0 + n + 127) // 128)))
    rd = [('XT', t) for t in tbs] + [tw]
    for dt in range(NDT):
        MM(P, X.pb[bank][:, 0:n], w[:, dt, :], X.XT[:, dt, g0:g0 + n], dt == 0, dt == NDT - 1, rd, [('ps', bank)])


def gate_from_bank(X, Y, bank, n, qb):
    P = X.P
    ACTF(P, Y.th[qb][:, 0:n], X.pb[bank][:, 0:n], AF.Tanh, [('ps', bank)], [('S', 'th', qb)], scale=0.5)
    STT(P, 'dve', Y.g_b[qb][:, 0:n], Y.th[qb][:, 0:n], 1.0, X.pb[bank][:, 0:n], ALU.add, ALU.mult,
        [('S', 'th', qb), ('ps', bank)], [('S', 'g_b', qb)])


def mem_kside(X, pa, L, Y):
    P = X.P
    pbT = X.pb[5][:, :].bitcast(BF16)
    for mb in range(2):
        f = Y.mkv_f[mb]
        tf = ('S', 'mkv_f', mb)
        if pa.kind == 'p':
            bank = 6 + mb
            for dt in range(NDT):
                MM(P, X.pb[bank][:, 0:512], X.memT[:, dt, mb * 128:(mb + 1) * 128], Y.WM[:, dt, :],
                   dt == 0, dt == NDT - 1, ['memT', ('S', 'WM')], [('ps', bank)])
            CP(P, 'act', f[:, :], X.pb[bank][:, 0:512], [('ps', bank)], [tf])
            P.dma('sp', X.o_mk[L, mb * 128:(mb + 1) * 128, :], f[:, 0:256], [tf], [])
            P.dma('sp', X.o_mv[L, mb * 128:(mb + 1) * 128, :], f[:, 256:512], [tf], [])
        else:
            P.dma('sp', f[:, 0:256], X.c_mk[L, mb * 128:(mb + 1) * 128, :], [], [tf])
            P.dma('sp', f[:, 256:512], X.c_mv[L, mb * 128:(mb + 1) * 128, :], [], [tf])
        CP(P, 'pool', Y.mk_b[:, :], f[:, 0:256], [tf], [('S', 'mk_b')])
        CP(P, 'pool', X.mvA[:, mb, 0:64], f[:, 256:320], [tf], ['mvA'])
        CP(P, 'pool', X.mvA[:, mb, 128:256], f[:, 320:448], [tf], ['mvA'])
        CP(P, 'pool', X.mvA[:, mb, 320:384], f[:, 448:512], [tf], ['mvA'])
        for p in range(2):
            TR(P, pbT[:, p * 128:(p + 1) * 128], Y.mk_b[:, p * 128:(p + 1) * 128], X.identb[:, :],
               [('S', 'mk_b'), 'identb'], [('ps', 5)])
        CP(P, 'dve', X.mkT[:, :, mb * 128:(mb + 1) * 128], pbT[:, 0:256].rearrange("p (a b) -> p a b", b=128),
           [('ps', 5)], ['mkT'])


def load_kside_mla(X, pa, L, Y):
    P = X.P
    if pa.kind == 'p':
        for ci in range(4):
            ea = X.exd[L][ci].ap()
            tex = ('exd', L, ci)
            for rr in range(2):
                o = rr * EX_ROWS
                c0 = rr * 2048 + ci * 512
                src = ea[o:o + 128, :].rearrange("a (b c) -> (a b) c", c=512).rearrange("(rt p) t -> p rt t", p=128)
                P.dma('sp', X.latT[:, :, c0:c0 + 512], src, [tex], [('LT', ci)])
                srk = ea[o + 448:o + 480, :].rearrange("a (b c) -> (a b) c", c=512)
                for hf in range(2):
                    P.dma('sp', X.krT[hf * 64:(hf + 1) * 64, c0:c0 + 512], srk, [tex], [('KV', 'kr', ci)])


def sample_kprep(X, L):
    P = X.P
    pbT = [X.pb[6][:, :].bitcast(BF16), X.pb[7][:, :].bitcast(BF16)]
    st3 = X.stageL
    cload(X, 'lat', L, st3[:, :, :], [('KV', 'stgL')])
    for k4 in range(8):
        bsel = k4 % 2
        for i in range(2):
            kb = 2 * k4 + i
            for rt in range(2):
                TR(P, pbT[bsel][:, (i * 2 + rt) * 128:(i * 2 + rt + 1) * 128], st3[:, kb, rt * 128:(rt + 1) * 128],
                   X.identb[:, :], [('KV', 'stgL'), 'identb'], [('ps', 6 + bsel)])
        for rt in range(2):
            src = pbT[bsel][:, 0:512].rearrange("p (i r t) -> p i r t", i=2, r=2)[:, :, rt, :]
            CP(P, 'act' if rt == 0 else 'dve', X.latT4[:, rt, 2 * k4:2 * k4 + 2, :], src, [('ps', 6 + bsel)],
               [('LT', k4)])
    sk = X.stageK
    cload(X, 'kr', L, sk[:, :, 0:64], [('KV', 'stgK')])
    cload(X, 'kr', L, sk[:, :, 64:128], [('KV', 'stgK')])
    for k8 in range(4):
        bsel = k8 % 2
        for i in range(4):
            kb = 4 * k8 + i
            TR(P, pbT[bsel][:, i * 128:(i + 1) * 128], sk[:, kb, :], X.identb[:, :],
               [('KV', 'stgK'), 'identb'], [('ps', 6 + bsel)])
        CP(P, 'act' if k8 % 2 == 0 else 'dve', X.krT[:, k8 * 512:(k8 + 1) * 512], pbT[bsel][:, 0:512],
           [('ps', 6 + bsel)], [('KV', 'kr', k8)])


def kblocks(pa):
    if pa.kind == 'p':
        return [(kb, 128, (kb % 2) * 16 + kb // 2) for kb in range(32)]
    return [(kb, 128, kb) for kb in range(16)] + [(16, DEC, 16)]


def mla_head(X, pa, L, Y, h, after_dec=None, pre_w=None):
    P = X.P
    kbs = kblocks(pa)
    nkeys = sum(k[1] for k in kbs)
    lt_all = [('LT', i) for i in range(8)] + [('LT', 'new')]
    wg, twg = pre_w if pre_w is not None else load_wcol(X, L, Y, 704 + h * 128)
    nch = (nkeys + 511) // 512
    for c in range(nch):
        n = min(512, nkeys - c * 512)
        bank = 6 + (c % 2)
        for rt in range(2):
            MM(P, X.pb[bank][:, 0:n], Y.WU[:, rt, h * 256:h * 256 + 128], X.latT[:, rt, c * 512:c * 512 + n],
               rt == 0, rt == 1, [('S', 'WU')] + lt_all, [('ps', bank)])
        CP(P, 'act' if c % 2 == 0 else 'dve', X.KTh[:, c * 512:c * 512 + n], X.pb[bank][:, 0:n], [('ps', bank)],
           [('KV', 'KT', c)])
    sblks = sorted([(sl, KR) for (kb, KR, sl) in kbs])
    ngr = (len(sblks) + 3) // 4
    for g in range(ngr):
        bank = 6 + (g % 2)
        blks = sblks[4 * g:4 * g + 4]
        for i, (kb, KR) in enumerate(blks):
            for rt in range(2):
                MM(P, X.pb[bank][0:KR, i * 128:(i + 1) * 128], X.latT[:, rt, kb * 128:kb * 128 + KR],
                   Y.WU[:, rt, h * 256 + 128:h * 256 + 256], rt == 0, rt == 1, [('S', 'WU')] + lt_all, [('ps', bank)])
        KRg = blks[0][1]
        nb = len(blks)
        if all(b_[1] == 128 for b_ in blks):
            src = X.pb[bank][:, 0:nb * 128].rearrange("p (a b) -> p a b", b=128)
            CP(P, 'dve' if g % 2 == 0 else 'act', X.Vh[:, 4 * g:4 * g + nb, :], src, [('ps', bank)], [('KV', 'V', g)])
        else:
            for i, (kb, KR) in enumerate(blks):
                CP(P, 'dve', X.Vh[0:KR, kb, :], X.pb[bank][0:KR, i * 128:(i + 1) * 128], [('ps', bank)],
                   [('KV', 'V', g)])
    if after_dec is not None:
        after_dec()
    kv_reads = [('KV', 'KT', c) for c in range(nch)] + [('KV', 'V', g) for g in range(ngr)] + \
               [('KV', 'kr', i) for i in range(8)] + [('KV', 'krnew')]

    def qside(j):
        g0, n = pa.groups[j]
        qb = Y.qbuf_n % 2
        Y.qbuf_n += 1
        tbs = sorted(set(range(g0 // 128, (g0 + n + 127) // 128)))
        bank = 6
        for rt in range(3):
            MM(P, X.pb[bank][:, 0:n], Y.WQN[:, rt, h * 128:(h + 1) * 128], X.cqnT[:, rt, g0:g0 + n], rt == 0, rt == 2,
               [('S', 'WQN')] + [('cqnT', t) for t in tbs], [('ps', bank)])
        CP(P, 'act', Y.q_b[qb][:, 0:n], X.pb[bank][:, 0:n], [('ps', bank)], [('S', 'q_b', qb)])
        hp_ = h % 2
        CP(P, 'pool', Y.qrz[hp_][qb][hp_ * 64:(hp_ + 1) * 64, 0:n], X.qrT[hp_ * 64:(hp_ + 1) * 64, h // 2, g0:g0 + n],
           [('qrT', t) for t in tbs], [('S', 'qrz', hp_, qb)])
        proj_feature_major(X, pa, Y, wg, twg, g0, n, 7)
        gate_from_bank(X, Y, 7, n, qb)
        return qb

    hp = h % 2
    qbs = {0: qside(0)}
    for j in range(len(pa.groups)):
        if j + 1 < len(pa.groups):
            qbs[j + 1] = qside(j + 1)
        qb = qbs[j]
        g0, n = pa.groups[j]
        tbs = sorted(set(range(g0 // 128, (g0 + n + 127) // 128)))
        bg_pump(X, 1)
        units = []
        for (kb, KR, sl) in kbs:
            v = vis_prompt(j, kb, 'mla') if pa.kind == 'p' else (0, None)
            if v is None:
                continue
            units.append((kb, KR, v[0], v[1], sl))
        od = Y.od_n % 2
        Y.od_n += 1
        bO = 2 + od
        bD = 4 + od
        nU = len(units)

        def emit_qk(u):
            kb, KR, c0, mi, sl = units[u]
            sbk = Y.unit_n_base + u
            bS = sbk % 2
            S = X.pb[bS]
            MM(P, S[0:KR, c0:n], X.KTh[:, sl * 128:sl * 128 + KR], Y.q_b[qb][:, c0:n], True, False,
               kv_reads + [('S', 'q_b', qb)], [('ps', bS)])
            MM(P, S[0:KR, c0:n], X.krT[:, sl * 128:sl * 128 + KR],
               Y.qrz[hp][qb][:, c0:n], False, mi is None,
               kv_reads + [('S', 'qrz', hp, qb)], [('ps', bS)])
            if mi is not None:
                MM(P, S[0:KR, c0:c0 + 128], X.identb[0:KR, 0:KR], X.maskb[0:KR, mi, 0:128], False, True,
                   ['identb', 'maskb'], [('ps', bS)])
            pt = sbk % 3
            ACTF(P, Y.PT[pt][0:KR, c0:n], S[0:KR, c0:n], AF.Exp, [('ps', bS)], [('S', 'PT', pt)], scale=MLA_SCALE)

        def emit_pv(u):
            kb, KR, c0, mi, sl = units[u]
            sbk = Y.unit_n_base + u
            pt = sbk % 3
            MM(P, X.pb[bO][:, c0:n], X.Vh[0:KR, sl, :], Y.PT[pt][0:KR, c0:n], u == 0, u == nU - 1,
               kv_reads + [('S', 'PT', pt)], [('ps', bO)])
            MM(P, X.pb[bD][:, c0:n], X.onesb[0:KR, :], Y.PT[pt][0:KR, c0:n], u == 0, u == nU - 1,
               ['onesb', ('S', 'PT', pt)], [('ps', bD)])

        Y.unit_n_base = Y.unit_n
        for u in range(nU + 1):
            if u < nU:
                emit_qk(u)
            if u >= 1:
                emit_pv(u - 1)
        Y.unit_n += nU
        rb = od
        CP_recip(P, Y.Rinv[rb][:, 0:n], X.pb[bD][:, 0:n], [('ps', bD)], [('S', 'Rinv', rb)])
        STT(P, 'dve', Y.tmpO[rb][:, 0:n], X.pb[bO][:, 0:n], 0.5, Y.Rinv[rb][:, 0:n], ALU.mult, ALU.mult,
            [('ps', bO), ('S', 'Rinv', rb)], [('S', 'tmpO', rb)])
        TT(P, 'pool', X.MIXT[:, h, g0:g0 + n], Y.tmpO[rb][:, 0:n], Y.g_b[qb][:, 0:n], ALU.mult,
           [('S', 'tmpO', rb), ('S', 'g_b', qb)], [('MIXT', h, j)])


def mla_sample_absorbed(X, pa, L, Y):
    P = X.P
    n = DEC
    kbs = kblocks(pa)
    pbT6 = X.pb[6][:, :].bitcast(BF16)
    lt_all = [('LT', i) for i in range(8)] + [('LT', 'new')]
    kr_all = [('KV', 'kr', i) for i in range(8)] + [('KV', 'krnew')]
    for h in range(4):
        for rt in range(2):
            i = h * 2 + rt
            TR(P, pbT6[:, i * 128:(i + 1) * 128], Y.WU[:, rt, h * 256:h * 256 + 128], X.identb[:, :],
               [('S', 'WU'), 'identb'], [('ps', 6)])
    CP(P, 'act', Y.WUT[:, :, :], pbT6[:, 0:1024].rearrange("p (a b) -> p a b", b=128), [('ps', 6)], [('S', 'WUT')])
    for h in range(4):
        for rt in range(3):
            MM(P, X.pb[7][:, h * n:(h + 1) * n], Y.WQN[:, rt, h * 128:(h + 1) * 128], X.cqnT[:, rt, 0:n], rt == 0, rt == 2,
               [('S', 'WQN'), ('cqnT', 0)], [('ps', 7)])
    CP(P, 'act', Y.qn_all[:, :], X.pb[7][:, 0:4 * n], [('ps', 7)], [('S', 'qn_all')])
    for h in range(4):
        for rt in range(2):
            c = (rt * 4 + h) * n
            MM(P, X.pb[6][:, c:c + n], Y.WUT[:, h * 2 + rt, :], Y.qn_all[:, h * n:(h + 1) * n], True, True,
               [('S', 'WUT'), ('S', 'qn_all')], [('ps', 6)])
    CP(P, 'act', Y.qabs[:, :, :], X.pb[6][:, 0:8 * n].rearrange("p (a b) -> p a b", b=4 * n), [('ps', 6)], [('S', 'qabs')])
    for h in range(4):
        wg, twg = load_wcol(X, L, Y, 704 + h * 128)
        for dt in range(NDT):
            MM(P, X.pb[7][:, h * n:(h + 1) * n], wg[:, dt, :], X.XT[:, dt, 0:n], dt == 0, dt == NDT - 1,
               [twg, ('XT', 0)], [('ps', 7)])
    ACTF(P, Y.th_all[:, :], X.pb[7][:, 0:4 * n], AF.Tanh, [('ps', 7)], [('S', 'th_all')], scale=0.5)
    STT(P, 'dve', Y.g_all[:, :], Y.th_all[:, :], 1.0, X.pb[7][:, 0:4 * n], ALU.add, ALU.mult,
        [('S', 'th_all'), ('ps', 7)], [('S', 'g_all')])
    MEMSET(P, 'pool', Y.qr_all[:, :], 0.0, [], [('S', 'qr_all')])
    for h in range(4):
        r_ = slice((h % 2) * 64, (h % 2) * 64 + 64)
        CP(P, 'pool', Y.qr_all[r_, h * n:(h + 1) * n], X.qrT[r_, h // 2, 0:n], [('qrT', 0), ('S', 'qr_all')], [('S', 'qr_all')])
    sbanks = (0, 1, 4)
    nU = len(kbs)
    W = 4 * n

    def lat_tok(kb, KR, rt):
        if kb < 16:
            return X.stageL[0:KR, kb, rt * 128:(rt + 1) * 128], ('KV', 'stgL')
        return X.slat[0:KR, rt * 128:(rt + 1) * 128], 'slat'

    def qk(u):
        kb, KR, sl = kbs[u]
        bS = sbanks[u % 3]
        S = X.pb[bS]
        k0 = sl * 128
        MM(P, S[0:KR, 0:W], X.latT[:, 0, k0:k0 + KR], Y.qabs[:, 0, :], True, False, lt_all + [('S', 'qabs')], [('ps', bS)])
        MM(P, S[0:KR, 0:W], X.latT[:, 1, k0:k0 + KR], Y.qabs[:, 1, :], False, False, lt_all + [('S', 'qabs')], [('ps', bS)])
        MM(P, S[0:KR, 0:W], X.krT[:, k0:k0 + KR], Y.qr_all[:, :], False, True, kr_all + [('S', 'qr_all')], [('ps', bS)])
        pt = u % 3
        ACTF(P, Y.PT[pt][0:KR, 0:W], S[0:KR, 0:W], AF.Exp, [('ps', bS)], [('S', 'PT', pt)], scale=MLA_SCALE)

    def pv(u):
        kb, KR, sl = kbs[u]
        pt = u % 3
        for rt in range(2):
            lt, tk = lat_tok(kb, KR, rt)
            MM(P, X.pb[2 + rt][:, 0:W], lt, Y.PT[pt][0:KR, 0:W], u == 0, u == nU - 1, [tk, ('S', 'PT', pt)], [('ps', 2 + rt)])
        MM(P, X.pb[5][:, 0:W], X.onesb[0:KR, :], Y.PT[pt][0:KR, 0:W], u == 0, u == nU - 1,
           ['onesb', ('S', 'PT', pt)], [('ps', 5)])

    for u in range(nU + 2):
        if u < nU:
            qk(u)
        if u >= 2:
            pv(u - 2)
    CP_recip(P, Y.Rs[:, :], X.pb[5][:, 0:W], [('ps', 5)], [('S', 'Rs')])
    for rt in range(2):
        TT(P, 'dve', Y.OLn[:, rt, :], X.pb[2 + rt][:, 0:W], Y.Rs[:, :], ALU.mult, [('ps', 2 + rt), ('S', 'Rs')], [('S', 'OLn')])
    for h in range(4):
        for rt in range(2):
            MM(P, X.pb[6][:, h * n:(h + 1) * n], Y.WU[:, rt, h * 256 + 128:h * 256 + 256], Y.OLn[:, rt, h * n:(h + 1) * n],
               rt == 0, rt == 1, [('S', 'WU'), ('S', 'OLn')], [('ps', 6)])
    STT(P, 'dve', X.MIXT[:, 0:4, 0:n], X.pb[6][:, 0:W].rearrange("p (h q) -> p h q", h=4), 0.5,
        Y.g_all[:, :].rearrange("p (h q) -> p h q", h=4), ALU.mult, ALU.mult,
        [('ps', 6), ('S', 'g_all')], [('MIXT', h, 0) for h in range(4)])


def CP_recip(P, out, in_, reads, writes):
    return P.op('dve', lambda e: e.reciprocal(out, in_), reads, writes)


def load_logf(X, pa, L, Y, part):
    P = X.P
    if part == 'dma':
        MEMSET(P, 'pool', Y.logf[:, :, :], 0.0, [], [('S', 'logf')])
    if pa.kind == 'p' and part == 'dma':
        for ci in range(4):
            ea = X.exd[L][ci].ap()
            tex = ('exd', L, ci)
            for rr in range(2):
                o = rr * EX_ROWS
                sl = ea[o + 480:o + 488, :].rearrange("a (b c) -> (a b) c", c=16).rearrange("(k p) c -> p k c", p=128)
                P.dma('sp', Y.lgall[:, 8 * ci + rr:8 * ci + 8:2, :], sl, [tex], [('S', 'lgall')])
    elif pa.kind == 'p':
        l3 = Y.lgall[:, :, 0:12].rearrange("p k (s h) -> p k s h", s=3)
        TT(P, 'dve', Y.lgA[:, :, :], l3[:, :, 0, :], l3[:, :, 1, :], ALU.add, [('S', 'lgall')], [('S', 'lgA')])
        TT(P, 'dve', Y.logf[:, :, :], Y.lgA[:, :, :], l3[:, :, 2, :], ALU.add, [('S', 'lgA'), ('S', 'lgall'), ('S', 'logf')],
           [('S', 'logf')])
    elif part == 'dma':
        P.dma('sp', Y.logf[:, 0:16, :], X.c_logf[L].rearrange("(k p) h -> p k h", p=128), [('S', 'logf')], [('S', 'logf')])
    else:
        CP(P, 'dve', Y.logf[0:DEC, 16, :], X.slogf[:, :], ['slogf', ('S', 'logf')], [('S', 'logf')])


def load_kside_fox(X, pa, L, Y, part):
    P = X.P
    fkT4 = X.latT4
    slots = ((0, 64, 0, 64), (128, 256, 64, 192), (320, 384, 192, 256))
    if part == 'fv' and pa.kind != 'p':
        MEMSET(P, 'pool', X.fvA[:, :, 64:128], 1.0, [], [('KV', 'ones')])
        MEMSET(P, 'pool', X.fvA[:, :, 256:320], 1.0, [], [('KV', 'ones')])
    if pa.kind == 'p':
        for ci in range(4):
            ea = X.exd[L][ci].ap()
            tex = ('exd', L, ci)
            for rr in range(2):
                o = rr * EX_ROWS
                if part == 'fk':
                    c0 = rr * 2048 + ci * 512
                    src = ea[o + 128:o + 256, :].rearrange("a (b c) -> (a b) c", c=512).rearrange("(rt p) t -> p rt t", p=128)
                    P.dma('sp', X.latT[:, :, c0:c0 + 512], src, [tex], [('LT', 'fk', ci)])
                else:
                    sv = ea[o + 256:o + 448, :].rearrange("a (b c) -> (a b) c", c=128).rearrange("(k p x) c -> p k (x c)", p=128, x=3)
                    k0 = rr * 16 + ci * 4
                    P.dma('sp', X.fvA[:, k0:k0 + 4, :], sv, [tex], [('KV', 'fv', ci)])
    elif part == 'fk':
        pbT = [X.pb[6][:, :].bitcast(BF16), X.pb[7][:, :].bitcast(BF16)]
        st3 = Y.stage
        cload(X, 'fk', L, st3[:, :, :], [('S', 'WM')])
        for k4 in range(8):
            bsel = k4 % 2
            for i in range(2):
                kb = 2 * k4 + i
                for rt in range(2):
                    TR(P, pbT[bsel][:, (i * 2 + rt) * 128:(i * 2 + rt + 1) * 128], st3[:, kb, rt * 128:(rt + 1) * 128],
                       X.identb[:, :], [('S', 'WM'), 'identb'], [('ps', 6 + bsel)])
            for rt in range(2):
                src = pbT[bsel][:, 0:512].rearrange("p (i r t) -> p i r t", i=2, r=2)[:, :, rt, :]
                CP(P, 'act' if rt == 0 else 'dve', fkT4[:, rt, 2 * k4:2 * k4 + 2, :], src, [('ps', 6 + bsel)],
                   [('LT', 'fk', k4)])
        CP(P, 'dve', X.latT[:, :, PAST:PAST + DEC], X.sfkT[:, :, :], ['sfkT'], [('LT', 'fk', 'new')])
    else:
        fv_c = ('fv', L) in X.cb_ready
        if fv_c:
            cv = X.cb['fv'][L].rearrange("(k p) c -> p k c", p=128)
        else:
            cv = X.c_fv[L].rearrange("(k p) c -> p k c", p=128)
        for (d0, d1, s0, s1) in slots:
            P.dma('sp' if fv_c else 'pool', X.fvA[:, 0:16, d0:d1], cv[:, :, s0:s1],
                  [('cb', 'fv', L, i) for i in range(X.bg_np[('cb', 'fv', L)])] if fv_c else [], [('KV', 'fv', 0)])
            CP(P, 'dve', X.fvA[0:DEC, 16, d0:d1], X.sfv[:, s0:s1], ['sfv'], [('KV', 'fv', 1)])


def fox_bias_prep(X, pa, L, Y):
    P = X.P
    lf = Y.logf[:, :, :].rearrange("p k h -> p (k h)")
    MM(P, X.pb[6][:, 0:128], X.trif[:, :], lf, True, True, ['trif', ('S', 'logf')], [('ps', 6)])
    MM(P, X.pb[7][:, 0:128], X.onesf[:, :], lf, True, True, ['onesf', ('S', 'logf')], [('ps', 7)])
    f2 = lambda t: t[:, :, :].rearrange("p k h -> p (k h)")
    CP(P, 'act', f2(Y.cumin), X.pb[6][:, 0:128], [('ps', 6)], [('S', 'cumin')])
    CP(P, 'dve', f2(Y.tot), X.pb[7][:, 0:128], [('ps', 7)], [('S', 'tot')])
    cur, tcur = Y.tot, ('S', 'tot')
    pp = [(Y.scA, ('S', 'scA')), (Y.scB, ('S', 'scB'))]
    k = 0
    d = 1
    while d < 32:
        nxt, tnxt = pp[k % 2]
        k += 1
        CP(P, 'dve', nxt[:, 0:d, :], cur[:, 0:d, :], [tcur], [tnxt])
        TT(P, 'dve', nxt[:, d:32, :], cur[:, d:32, :], cur[:, 0:32 - d, :], ALU.add, [tcur], [tnxt])
        cur, tcur = nxt, tnxt
        d *= 2
    Y.incl, Y.tincl = cur, tcur
    TT(P, 'dve', Y.lgB[:, :, :], cur[:, :, :], Y.tot[:, :, :], ALU.subtract, [tcur, ('S', 'tot')], [('S', 'lgB')])
    TT(P, 'dve', Y.cum[:, :, :], Y.cumin[:, :, :], Y.lgB[:, :, :], ALU.add, [('S', 'cumin'), ('S', 'lgB')], [('S', 'cum')])


def pair_attention(X, pa, L, Y, kind, p, pre_w=None):
    P = X.P
    if kind == 'fox':
        qcol = 1216 + p * 128; gcol = 1988 + p * 128; tile = 4 + p
        kbs = kblocks(pa)
        scale = FOX_SCALE
    else:
        qcol = 2244 + p * 128; gcol = 2500 + p * 128; tile = 6 + p
        kbs = [(0, 128, 0), (1, 128, 1)]
        scale = MEM_SCALE
    if pre_w is not None:
        (wq, twq), (wg, twg) = pre_w
    else:
        wq, twq = load_wcol(X, L, Y, qcol)
        wg, twg = load_wcol(X, L, Y, gcol)
    if kind == 'fox':
        kreads = [('LT', 'fk', i) for i in range(8)] + [('LT', 'fk', 'new')]
        vreads = [('KV', 'fv', i) for i in range(4)] + [('KV', 'ones')]
    else:
        kreads = ['mkT']
        vreads = ['mvA', 'mvA1', 'mvA2']
    vslot = (0, 64, 192, 256)

    def qside(j):
        g0, n = pa.groups[j]
        qb = Y.qbuf_n % 2
        Y.qbuf_n += 1
        proj_feature_major(X, pa, Y, wq, twq, g0, n, 6)
        CP(P, 'act', Y.qz[qb][0][0:64, 0:n], X.pb[6][0:64, 0:n], [('ps', 6)], [('S', 'q_b', qb)])
        CP(P, 'act', Y.qz[qb][1][64:128, 0:n], X.pb[6][64:128, 0:n], [('ps', 6)], [('S', 'q_b', qb)])
        proj_feature_major(X, pa, Y, wg, twg, g0, n, 7)
        gate_from_bank(X, Y, 7, n, qb)
        return qb

    use_corr = (kind == 'fox' and pa.kind == 'p') and not os.environ.get('KDBG_NOCORR')
    sbanks = (0, 1, 4)
    items = [(j, hp) for j in range(len(pa.groups)) for hp in range(2)]

    def prep(idx):
        if kind != 'fox':
            return
        j, hp = items[idx]
        h = 2 * p + hp
        bb = idx % 2
        kbG = 8 * j + 7 if pa.kind == 'p' else 16
        TS(P, 'dve', Y.bias[bb][:, :], Y.cum[:, :, h], Y.incl[:, kbG, h:h + 1], -1.0, ALU.subtract,
           ALU.mult, [('S', 'cum'), Y.tincl], [('S', 'bias', bb)])
        if use_corr:
            TS(P, 'dve', Y.d4[:, :], Y.incl[:, 8 * j + 1:8 * j + 8:2, h], Y.incl[:, kbG, h:h + 1], 1.0 / scale,
               ALU.subtract, ALU.mult, [Y.tincl], [('S', 'd4')])
            CP(P, 'dve', Y.hi4[:, :], Y.d4[:, :], [('S', 'd4')], [('S', 'hi4')])
            TT(P, 'dve', Y.r4[:, :], Y.d4[:, :], Y.hi4[:, :], ALU.subtract, [('S', 'd4'), ('S', 'hi4')], [('S', 'r4')])
            TS(P, 'dve', Y.d4[:, :], Y.hi4[:, :], X.identf[:, 0:1], None, ALU.mult, None,
               [('S', 'hi4'), ('S', 'r4'), 'identf'], [('S', 'd4')])
            STT(P, 'dve', Y.r4[:, :], Y.r4[:, :], X.identf[:, 32:33], Y.d4[:, :], ALU.mult, ALU.add,
                [('S', 'd4'), 'identf'], [('S', 'r4')])
            CP(P, 'dve', Y.CR[bb][:, :].rearrange("p (i c) -> p i c", i=4),
               Y.r4[:, :].unsqueeze(2).broadcast_to([128, 4, 128]), [('S', 'r4')], [('S', 'CR', bb)])

    def run_item(idx, qb):
        j, hp = items[idx]
        g0, n = pa.groups[j]
        h = 2 * p + hp
        bb = idx % 2
        orows = slice(0, 64) if hp == 0 else slice(64, 128)
        drows = slice(64, 128) if hp == 0 else slice(0, 64)
        units = []
        for (kb, KR, sl) in kbs:
            if kind == 'fox' and pa.kind == 'p':
                v = vis_prompt(j, kb, 'fox')
            elif kind == 'fox':
                v = (0, 4 if kb == 16 else None)
            else:
                v = (0, None)
            if v is None:
                continue
            units.append((kb, KR, v[0], v[1], sl))
        nU = len(units)
        od = Y.od_n % 2
        Y.od_n += 1
        bO = 2 + od
        base = Y.unit_n

        def emit_qk(u):
            kb, KR, c0, mi, sl = units[u]
            sbk = base + u
            bS = sbanks[sbk % 3]
            S = X.pb[bS]
            if kind == 'fox':
                kt = X.latT[:, p, sl * 128:sl * 128 + KR]
            else:
                kt = X.mkT[:, p, sl * 128:sl * 128 + KR]
            MM(P, S[0:KR, c0:n], kt, Y.qz[qb][hp][:, c0:n], True, (mi is None) and (not use_corr),
               kreads + [('S', 'q_b', qb)], [('ps', bS)])
            if use_corr:
                MM(P, S[0:KR, c0:n], X.onesb[:, 0:KR], Y.CR[bb][:, c0:n], False, mi is None,
                   ['onesb', ('S', 'CR', bb)], [('ps', bS)])
            if mi is not None:
                w_ = min(128, n - c0)
                MM(P, S[0:KR, c0:c0 + w_], X.identb[0:KR, 0:KR], X.maskb[0:KR, mi, 0:w_], False, True,
                   ['identb', 'maskb'], [('ps', bS)])
            pt = sbk % 3
            if kind == 'fox':
                ACTF(P, Y.PT[pt][0:KR, c0:n], S[0:KR, c0:n], AF.Exp, [('ps', bS), ('S', 'bias', bb)],
                     [('S', 'PT', pt)], bias=Y.bias[bb][0:KR, kb:kb + 1], scale=scale)
            else:
                ACTF(P, Y.PT[pt][0:KR, c0:n], S[0:KR, c0:n], AF.Exp, [('ps', bS)], [('S', 'PT', pt)], scale=scale)

        def emit_pv(u):
            kb, KR, c0, mi, sl = units[u]
            sbk = base + u
            pt = sbk % 3
            vs = vslot[h]
            if kind == 'fox':
                va = X.fvA[0:KR, sl, vs:vs + 128]
            else:
                va = X.mvA[0:KR, sl, vs:vs + 128]
            MM(P, X.pb[bO][:, c0:n], va, Y.PT[pt][0:KR, c0:n], u == 0, u == nU - 1,
               vreads + [('S', 'PT', pt)], [('ps', bO)])

        for u in range(nU + 2):
            if u < nU:
                emit_qk(u)
            if u >= 2:
                emit_pv(u - 2)
        Y.unit_n += nU
        rb = od
        CP_recip(P, Y.Rinv[rb][orows, 0:n], X.pb[bO][drows, 0:n], [('ps', bO)], [('S', 'Rinv', rb)])
        STT(P, 'dve', Y.tmpO[rb][orows, 0:n], X.pb[bO][orows, 0:n], 0.5, Y.Rinv[rb][orows, 0:n], ALU.mult, ALU.mult,
            [('ps', bO), ('S', 'Rinv', rb)], [('S', 'tmpO', rb)])
        TT(P, 'pool', X.MIXT[orows, tile, g0:g0 + n], Y.tmpO[rb][orows, 0:n], Y.g_b[qb][orows, 0:n], ALU.mult,
           [('S', 'tmpO', rb), ('S', 'g_b', qb)], [('MIXT', tile, j, hp)])

    qbs = {0: qside(0)}
    prep(0)
    for idx, (j, hp) in enumerate(items):
        if hp == 0 and j + 1 < len(pa.groups):
            qbs[j + 1] = qside(j + 1)
        if idx + 1 < len(items):
            prep(idx + 1)
        if kind == 'fox':
            bg_pump(X, 1)
        run_item(idx, qbs[j])


def phase3(X, pa, L):
    P = X.P
    R = pa.R
    barrier(X, 'S')
    cv = Carve(X.scr, X.nscr)
    WO = cv.take([NDT, D], BF16)
    lng = cv.take([D], F32)
    lnb = cv.take([D], F32)
    NB = 4
    xtok = [cv.take([D], F32) for _ in range(2)]
    res = [cv.take([D], F32) for _ in range(NB)]
    yy = [cv.take([D], F32) for _ in range(NB)]
    junk = cv.take([D], BF16)
    stat = [cv.take([8], F32) for _ in range(NB)]
    wload(X, 'out', L, WO[:, :, :], lambda w: w.rearrange("(et p) d -> p et d", p=128), [('S', 'WO')])
    P.dma('sp', lng[:, :], X.ln_g[L:L + 1, :].partition_broadcast(128), [], [('S', 'ln')])
    P.dma('sp', lnb[:, :], X.ln_b[L:L + 1, :].partition_broadcast(128), [], [('S', 'ln')])
    if pa.kind == 'p':
        rsrc = X.xp if L == 0 else X.y1
        ydst = X.y1 if L == 0 else X.yp
    else:
        rsrc = X.xs if L == 0 else X.ys1
        ydst = X.ys1 if L == 0 else X.ys
    mix_reads = [('MIXT', h, j) for h in range(4) for j in range(4)] + \
                [('MIXT', t, j, hp) for t in range(4, 8) for j in range(4) for hp in range(2)]

    def s1(tb):
        b2 = tb % 2
        b = tb % NB
        r0 = tb * 128
        tok = lambda n: ('S', n, b)
        bA = 0 + 2 * b2; bB = 1 + 2 * b2
        P.dma('act', xtok[b2][0:R, :], rsrc[r0:r0 + R, :], [('y1dram', tb)] if L == 1 else [], [('S', 'xtok', b2)])
        for et in range(NDT):
            lhsT = X.MIXT[:, et, r0:r0 + R]
            MM(P, X.pb[bA][0:R, :], lhsT, WO[:, et, 0:512], et == 0, et == NDT - 1, mix_reads + [('S', 'WO')], [('ps', bA)])
            MM(P, X.pb[bB][0:R, :], lhsT, WO[:, et, 512:1024], et == 0, et == NDT - 1, mix_reads + [('S', 'WO')], [('ps', bB)])
        STT(P, 'dve', res[b][0:R, 0:512], xtok[b2][0:R, 0:512], ALPHA, X.pb[bA][0:R, :], ALU.mult, ALU.add,
            [('S', 'xtok', b2), ('ps', bA)], [tok('res')])
        STT(P, 'dve', res[b][0:R, 512:1024], xtok[b2][0:R, 512:1024], ALPHA, X.pb[bB][0:R, :], ALU.mult, ALU.add,
            [('S', 'xtok', b2), ('ps', bB)], [tok('res')])

    def s1c(tb):
        b = tb % NB
        tok = lambda n: ('S', n, b)
        st_ = stat[b]
        ACTF(P, junk[0:R, :], res[b][0:R, :], AF.Copy, [tok('res')], [('S', 'junk3'), tok('stat')], accum=st_[0:R, 0:1])
        ACTF(P, junk[0:R, :], res[b][0:R, :], AF.Square, [tok('res')], [('S', 'junk3'), tok('stat')], accum=st_[0:R, 1:2])

    def s1b(tb):
        b = tb % NB
        r0 = tb * 128
        tok = lambda n: ('S', n, b)
        st_ = stat[b]
        TS(P, 'dve', st_[0:R, 2:3], st_[0:R, 0:1], 1.0 / D, None, ALU.mult, None, [tok('stat')], [tok('stat')])
        TT(P, 'dve', st_[0:R, 3:4], st_[0:R, 2:3], st_[0:R, 2:3], ALU.mult, [tok('stat')], [tok('stat')])
        STT(P, 'dve', st_[0:R, 4:5], st_[0:R, 1:2], 1.0 / D, st_[0:R, 3:4], ALU.mult, ALU.subtract,
            [tok('stat')], [tok('stat')])
        ACTF(P, st_[0:R, 5:6], st_[0:R, 4:5], AF.Ln, [tok('stat'), 'epsb'], [tok('stat')], bias=X.epsb[0:R, :], scale=1.0)
        ACTF(P, st_[0:R, 6:7], st_[0:R, 5:6], AF.Exp, [tok('stat')], [tok('stat')], scale=-0.5)
        TS(P, 'dve', res[b][0:R, :], res[b][0:R, :], st_[0:R, 2:3], st_[0:R, 6:7], ALU.subtract, ALU.mult,
           [tok('res'), tok('stat')], [tok('res')])
        TT(P, 'pool', yy[b][0:R, :], res[b][0:R, :], lng[0:R, :], ALU.mult, [tok('res'), ('S', 'ln')], [tok('yy')])
        TT(P, 'pool', yy[b][0:R, :], yy[b][0:R, :], lnb[0:R, :], ALU.add, [tok('yy'), ('S', 'ln')], [tok('yy')])
        P.dma('sp', ydst[r0:r0 + R, :], yy[b][0:R, :], [tok('yy')], [('y1dram', tb)] if L == 0 else [])

    def s2(tb):
        b2 = tb % 2
        b = tb % NB
        if L == 0:
            transpose_rows_to_XT(X, pa, yy[b], ('S', 'yy', b), tb, R, (4 + 2 * b2, 5 + 2 * b2))

    for i in range(pa.NTB + 3):
        if i < pa.NTB:
            s1(i)
        if 0 <= i - 1 < pa.NTB:
            s1b(i - 1)
        if i < pa.NTB:
            s1c(i)
        if 0 <= i - 3 < pa.NTB:
            s2(i - 3)


_NC_CACHE = {}


def _rope_tables(pos):
    half = 32
    inv = (10000.0 ** (-np.arange(half, dtype=np.float32) / half)).astype(np.float32)
    ang = pos.astype(np.float32)[:, None] * inv[None, :]
    cos = np.cos(ang).astype(np.float32)
    sin = np.sin(ang).astype(np.float32)
    return np.concatenate([cos, cos, -sin, sin], axis=1).astype(np.float32)


def _mask_tiles(r):
    k = np.arange(128)[:, None]
    q = np.arange(128)[None, :]
    tri = np.where(k <= q, 0.0, NEG).astype(np.float32)
    chunk = np.where((k >= 64) & (q < 64), NEG, 0.0).astype(np.float32)
    zeros = np.zeros((128, 128), np.float32)
    neg = np.full((128, 128), NEG, np.float32)
    if r == 0:
        t = [chunk, neg, tri, neg, tri]
    else:
        t = [zeros, chunk, zeros, tri, tri]
    return np.ascontiguousarray(np.stack(t, axis=1))


def kernel(x_prompt, x_sample, cache_mla_latent, cache_mla_krope, cache_fox_k, cache_fox_v, cache_fox_logf,
           cache_mem_k, cache_mem_v, mem_prompt, w_in, b_fox_f, mla_q_norm, mla_kv_norm, w_uq, w_ukv,
           w_mem_kv, w_out, ln_g, ln_b):
    f = lambda a: np.ascontiguousarray(np.asarray(a, dtype=np.float32))
    x_prompt = f(x_prompt); x_sample = f(x_sample)
    if 'nc' not in _NC_CACHE:
        _NC_CACHE['nc'] = build_program()
    nc = _NC_CACHE['nc']
    shared = dict(w_in=f(w_in), b_fox_f=f(b_fox_f), mla_q_norm=f(mla_q_norm), mla_kv_norm=f(mla_kv_norm),
                  w_uq=f(w_uq), w_ukv=f(w_ukv), w_mem_kv=f(w_mem_kv), w_out=f(w_out), ln_g=f(ln_g), ln_b=f(ln_b))
    ropeS = _rope_tables(PAST + np.arange(DEC)).reshape(DEC, 1, 128)
    in_maps = []
    for c in range(NCORES):
        b, r = c // 2, c % 2
        xp = x_prompt[b].reshape(32, 128, D)[r::2].reshape(TOWN, D)
        pos = ((2 * np.arange(16)[:, None] + r) * 128 + np.arange(128)[None, :])
        ropeP = _rope_tables(pos.reshape(-1)).reshape(16, 128, 128).transpose(1, 0, 2)
        m = dict(shared)
        m.update(
            xp=np.ascontiguousarray(xp), xs=f(x_sample[c]),
            c_lat=f(np.asarray(cache_mla_latent)[:, c]), c_kr=f(np.asarray(cache_mla_krope)[:, c]),
            c_fk=f(np.asarray(cache_fox_k)[:, c]).reshape(DEPTH, PAST, 256),
            c_fv=f(np.asarray(cache_fox_v)[:, c]).reshape(DEPTH, PAST, 256),
            c_logf=f(np.asarray(cache_fox_logf)[:, c]),
            c_mk=f(np.asarray(cache_mem_k)[:, c]).reshape(DEPTH, 256, 256),
            c_mv=f(np.asarray(cache_mem_v)[:, c]).reshape(DEPTH, 256, 256),
            memp=f(np.asarray(mem_prompt)[b]),
            ropeP=np.ascontiguousarray(ropeP), ropeS=np.ascontiguousarray(ropeS), masks=_mask_tiles(r))
        in_maps.append(m)
    res = run_bass_kernel_spmd(nc, in_maps, core_ids=list(range(NCORES)))
    R = res.results
    B = 4
    y_prompt = np.empty((B, SEQ, D), np.float32)
    y_sample = np.empty((NCORES, DEC, D), np.float32)
    p_lat = np.empty((DEPTH, B, SEQ, 256), np.float32); p_kr = np.empty((DEPTH, B, SEQ, 64), np.float32)
    p_fk = np.empty((DEPTH, B, SEQ, 256), np.float32); p_fv = np.empty((DEPTH, B, SEQ, 256), np.float32)
    p_logf = np.empty((DEPTH, B, SEQ, 4), np.float32)
    p_mk = np.empty((DEPTH, B, 256, 256), np.float32); p_mv = np.empty((DEPTH, B, 256, 256), np.float32)
    s_lat = np.empty((DEPTH, NCORES, DEC, 256), np.float32); s_kr = np.empty((DEPTH, NCORES, DEC, 64), np.float32)
    s_fk = np.empty((DEPTH, NCORES, DEC, 256), np.float32); s_fv = np.empty((DEPTH, NCORES, DEC, 256), np.float32)
    s_logf = np.empty((DEPTH, NCORES, DEC, 4), np.float32)
    for c in range(NCORES):
        b, r = c // 2, c % 2
        o = R[c]
        y_prompt[b].reshape(32, 128, D)[r::2] = np.asarray(o["yp"]).reshape(16, 128, D)
        y_sample[c] = np.asarray(o["ys"])
        for (dst, key, w) in ((p_lat, "o_lat", 256), (p_kr, "o_kr", 64), (p_fk, "o_fk", 256), (p_fv, "o_fv", 256),
                              (p_logf, "o_logf", 4)):
            a = np.asarray(o[key]).reshape(DEPTH, 16, 128, w)
            for l in range(DEPTH):
                dst[l, b].reshape(32, 128, w)[r::2] = a[l]
        if r == 0:
            p_mk[:, b] = np.asarray(o["o_mk"]); p_mv[:, b] = np.asarray(o["o_mv"])
        s_lat[:, c] = np.asarray(o["s_lat"]); s_kr[:, c] = np.asarray(o["s_kr"])
        s_fk[:, c] = np.asarray(o["s_fk"]); s_fv[:, c] = np.asarray(o["s_fv"]); s_logf[:, c] = np.asarray(o["s_logf"])
    return (y_prompt, y_sample, p_lat, p_kr, p_fk.reshape(DEPTH, B, SEQ, 4, 64), p_fv.reshape(DEPTH, B, SEQ, 4, 64),
            p_logf, p_mk.reshape(DEPTH, B, 256, 4, 64), p_mv.reshape(DEPTH, B, 256, 4, 64),
            s_lat, s_kr, s_fk.reshape(DEPTH, NCORES, DEC, 4, 64), s_fv.reshape(DEPTH, NCORES, DEC, 4, 64), s_logf)
```

```python
import os
import numpy as np
from contextlib import ExitStack
import concourse.bass as bass
import concourse.mybir as mybir
from concourse.bass_utils import run_bass_kernel_spmd

F32 = mybir.dt.float32
BF16 = mybir.dt.bfloat16
AF = mybir.ActivationFunctionType
ALU = mybir.AluOpType

NCORES = 8
D = 1024
NDT = 8
DEPTH = 2
SEQ = 4096
TOWN = 2048
PAST = 2048
DEC = 16
D_IN = 2756
MLA_SCALE = 192.0 ** -0.5
FOX_SCALE = 0.125
MEM_SCALE = 0.125
ALPHA = (2.0 * DEPTH) ** 0.25
EPS = 1e-6
NEG = -30000.0
GROUPS = [[0, 1], [2, 3], [4, 5], [6, 7]]
EX_ROWS = 488

ENGS = ['pe', 'act', 'dve', 'pool', 'sp']
ALIAS_GROUPS = ('S', 'KV', 'LT')


class Op:
    __slots__ = ('eng', 'fn', 'deps', 'signal', 'seq', 'is_dma', 'qn', 'idx', 'is_cc', 'ccsem')


class Prog:
    def __init__(self, nc, es, nslot):
        self.nc = nc
        self.es = es
        self.nslot = nslot
        self.ops = {e: [] for e in ENGS}
        self.last_w = {}
        self.readers = {}
        self.dma_w = {}
        self.epoch_deps = {}
        self.dma_count = {e: 0 for e in ENGS}
        self.nops = 0

    def op(self, eng, fn, reads=(), writes=(), dma=False, cc=False):
        o = Op()
        o.eng = eng; o.fn = fn; o.is_dma = dma; o.signal = False; o.deps = set(); o.seq = None
        o.is_cc = cc; o.ccsem = None
        o.idx = self.nops; self.nops += 1
        reads = list(reads); writes = list(writes)
        extra = set()
        for t in reads + writes:
            if isinstance(t, tuple) and t[0] in ALIAS_GROUPS:
                extra.add(('BAR', t[0]))
        for t in extra:
            if t not in writes:
                reads.append(t)
        for t in reads:
            w = self.last_w.get(t)
            if w is not None:
                o.deps.add(w)
            if isinstance(t, tuple) and t[0] == 'ps':
                for r in self.readers.get(t, ()):
                    if r.eng != eng:
                        o.deps.add(r)
        for t in reads:
            for d in self.dma_w.get(t, ()):
                o.deps.add(d)
        for t in writes:
            w = self.last_w.get(t)
            R = self.readers.get(t, [])
            if dma:
                if w is not None and w.is_dma and not R:
                    for d in self.epoch_deps.get(t, ()):
                        o.deps.add(d)
                else:
                    base = list(R)
                    if w is not None:
                        base.append(w)
                    base += self.dma_w.get(t, [])
                    for d in base:
                        o.deps.add(d)
                    self.epoch_deps[t] = base
                    self.dma_w[t] = []
            else:
                same = lambda d: (d.eng == eng and not d.is_dma and not d.is_cc) and os.environ.get('KSAME', '1') == '1'
                if w is not None and not same(w):
                    o.deps.add(w)
                for r in R:
                    if not same(r):
                        o.deps.add(r)
                for d in self.dma_w.get(t, ()):
                    o.deps.add(d)
        for t in reads:
            self.readers.setdefault(t, []).append(o)
        for t in writes:
            self.last_w[t] = o
            self.readers[t] = []
            if dma:
                self.dma_w.setdefault(t, []).append(o)
            else:
                self.dma_w[t] = []
        if dma:
            o.qn = self.dma_count[eng]
            self.dma_count[eng] += 1
        o.deps.discard(o)
        self.ops[eng].append(o)
        return o

    def dma(self, q, out, in_, reads=(), writes=()):
        return self.op(q, lambda e: e.dma_start(out=out, in_=in_), reads, writes, dma=True)

    def finalize(self):
        nc = self.nc
        for e in ENGS:
            for o in self.ops[e]:
                nd = set()
                for d in o.deps:
                    if d.is_dma or d.is_cc:
                        nd.add(d)
                    elif d.eng == 'pe' and o.eng == 'pe' and not o.is_dma:
                        continue
                    else:
                        d.signal = True
                        nd.add(d)
                o.deps = nd
        self.esem = {e: self.es.enter_context(nc.semaphore('sem_' + e)) for e in ENGS}
        for e in ENGS:
            c = 0
            for o in self.ops[e]:
                if o.is_cc:
                    o.ccsem = self.es.enter_context(nc.semaphore('cc_%d' % o.idx))
                if (not o.is_dma) and (not o.is_cc) and o.signal:
                    c += 1
                    o.seq = c
        self.ring = {e: [self.es.enter_context(nc.semaphore('ring_%s_%d' % (e, i))) for i in range(self.nslot[e])]
                     for e in ENGS if self.dma_count[e] > 0}

    def _semval(self, d):
        if d.is_cc:
            return d.ccsem, 1
        if d.is_dma:
            ns = self.nslot[d.eng]
            return self.ring[d.eng][d.qn % ns], 16 * (d.qn // ns + 1)
        return self.esem[d.eng], d.seq

    def emit_engine(self, e, eng):
        waited = {}

        def wait(sem, val):
            k = id(sem)
            if waited.get(k, 0) < val:
                eng.wait_ge(sem, val)
                waited[k] = val

        for o in self.ops[e]:
            need = {}
            for d in o.deps:
                sem, val = self._semval(d)
                k = id(sem)
                if k not in need or need[k][1] < val:
                    need[k] = (sem, val)
            for sem, val in need.values():
                wait(sem, val)
            if o.is_dma:
                ns = self.nslot[e]
                slot = o.qn % ns
                if o.qn >= ns:
                    wait(self.ring[e][slot], 16 * (o.qn // ns))
                ins = o.fn(eng)
                ins.then_inc(self.ring[e][slot], 16)
            elif o.is_cc:
                ins = o.fn(eng)
                ins.then_inc(o.ccsem)
            else:
                ins = o.fn(eng)
                if o.signal:
                    ins.then_inc(self.esem[e], 1)
        if e == 'sp':
            for q, n in self.dma_count.items():
                if n == 0:
                    continue
                ns = self.nslot[q]
                for slot in range(min(n, ns)):
                    cnt = (n - 1 - slot) // ns + 1
                    wait(self.ring[q][slot], 16 * cnt)

    def run_block(self):
        nc = self.nc
        self.finalize()
        with nc.Block() as block:
            @block.tensor
            def _(eng):
                self.emit_engine('pe', eng)

            @block.scalar
            def _(eng):
                self.emit_engine('act', eng)

            @block.vector
            def _(eng):
                self.emit_engine('dve', eng)

            @block.gpsimd
            def _(eng):
                self.emit_engine('pool', eng)

            @block.sync
            def _(eng):
                self.emit_engine('sp', eng)


def MM(P, out, lhsT, rhs, start, stop, reads, writes):
    return P.op('pe', lambda e: e.matmul(out, lhsT, rhs, start=start, stop=stop), reads, writes)


def TR(P, out, in_, ident, reads, writes):
    return P.op('pe', lambda e: e.transpose(out, in_, ident), reads, writes)


def ACTF(P, out, in_, func, reads, writes, bias=None, scale=1.0, accum=None):
    def f(e):
        kw = {}
        if bias is not None:
            kw['bias'] = bias
        if accum is not None:
            kw['accum_out'] = accum
        return e.activation(out, in_, func, scale=scale, **kw)
    return P.op('act', f, reads, writes)


def CP(P, eng, out, in_, reads, writes):
    if eng == 'act':
        return P.op('act', lambda e: e.copy(out, in_), reads, writes)
    return P.op(eng, lambda e: e.tensor_copy(out, in_), reads, writes)


def TT(P, eng, out, a, b, op, reads, writes):
    return P.op(eng, lambda e: e.tensor_tensor(out, a, b, op), reads, writes)


def TS(P, eng, out, a, s1, s2, op0, op1, reads, writes):
    if op1 is None:
        return P.op(eng, lambda e: e.tensor_scalar(out, a, s1, None, op0=op0), reads, writes)
    return P.op(eng, lambda e: e.tensor_scalar(out, a, s1, s2, op0=op0, op1=op1), reads, writes)


def STT(P, eng, out, a, s, b, op0, op1, reads, writes):
    return P.op(eng, lambda e: e.scalar_tensor_tensor(out, a, s, b, op0=op0, op1=op1), reads, writes)


def MEMSET(P, eng, ap, val, reads, writes):
    return P.op(eng, lambda e: e.memset(ap, val), reads, writes)


class Carve:
    def __init__(self, scr, nelem):
        self.scr = scr
        self.nelem = nelem
        self.off = 0

    def take(self, shape, dtype):
        n = 1
        for s in shape:
            n *= s
        nel = n * (2 if dtype == F32 else 1)
        nel += nel % 2
        assert self.off + nel <= self.nelem, ("scratch overflow", self.off + nel, self.nelem)
        ap = self.scr[:, self.off:self.off + nel]
        self.off += nel
        if dtype == F32:
            ap = ap.bitcast(F32)
        ap = ap[:, 0:n]
        if len(shape) == 2:
            return ap.rearrange("p (a b) -> p a b", b=shape[1])
        if len(shape) == 3:
            return ap.rearrange("p (a b c) -> p a b c", b=shape[1], c=shape[2])
        return ap


class Ctx:
    pass


def build_program():
    nc = bass.Bass("TRN2", target_bir_lowering=False)

    def dI(n, s, dt=F32):
        return nc.dram_tensor(n, s, dt, kind="ExternalInput").ap()

    def dO(n, s):
        return nc.dram_tensor(n, s, F32, kind="ExternalOutput").ap()

    X = Ctx()
    X.xp = dI("xp", [TOWN, D]); X.xs = dI("xs", [DEC, D])
    X.c_lat = dI("c_lat", [DEPTH, PAST, 256]); X.c_kr = dI("c_kr", [DEPTH, PAST, 64])
    X.c_fk = dI("c_fk", [DEPTH, PAST, 256]); X.c_fv = dI("c_fv", [DEPTH, PAST, 256])
    X.c_logf = dI("c_logf", [DEPTH, PAST, 4])
    X.c_mk = dI("c_mk", [DEPTH, 256, 256]); X.c_mv = dI("c_mv", [DEPTH, 256, 256])
    X.memp = dI("memp", [256, D])
    X.w_in = dI("w_in", [DEPTH, D, D_IN]); X.b_f = dI("b_fox_f", [DEPTH, 4])
    X.qnorm = dI("mla_q_norm", [DEPTH, 384]); X.kvnorm = dI("mla_kv_norm", [DEPTH, 256])
    X.w_uq = dI("w_uq", [DEPTH, 384, 768]); X.w_ukv = dI("w_ukv", [DEPTH, 256, 1024])
    X.w_mem = dI("w_mem_kv", [DEPTH, D, 512]); X.w_out = dI("w_out", [DEPTH, D, D])
    X.ln_g = dI("ln_g", [DEPTH, D]); X.ln_b = dI("ln_b", [DEPTH, D])
    X.ropeP = dI("ropeP", [128, 16, 128]); X.ropeS = dI("ropeS", [16, 1, 128])
    X.masks = dI("masks", [128, 5, 128])

    X.yp = dO("yp", [TOWN, D]); X.ys = dO("ys", [DEC, D])
    X.o_lat = dO("o_lat", [DEPTH, TOWN, 256]); X.o_kr = dO("o_kr", [DEPTH, TOWN, 64])
    X.o_fk = dO("o_fk", [DEPTH, TOWN, 256]); X.o_fv = dO("o_fv", [DEPTH, TOWN, 256])
    X.o_logf = dO("o_logf", [DEPTH, TOWN, 4])
    X.o_mk = dO("o_mk", [DEPTH, 256, 256]); X.o_mv = dO("o_mv", [DEPTH, 256, 256])
    X.s_lat = dO("s_lat", [DEPTH, DEC, 256]); X.s_kr = dO("s_kr", [DEPTH, DEC, 64])
    X.s_fk = dO("s_fk", [DEPTH, DEC, 256]); X.s_fv = dO("s_fv", [DEPTH, DEC, 256])
    X.s_logf = dO("s_logf", [DEPTH, DEC, 4])

    X.wb = {
        'in': nc.dram_tensor("wb_in", [DEPTH, D, D_IN], BF16).ap(),
        'uq': nc.dram_tensor("wb_uq", [DEPTH, 384, 768], BF16).ap(),
        'ukv': nc.dram_tensor("wb_ukv", [DEPTH, 256, 1024], BF16).ap(),
        'mem': nc.dram_tensor("wb_mem", [DEPTH, D, 512], BF16).ap(),
        'out': nc.dram_tensor("wb_out", [DEPTH, D, D], BF16).ap(),
    }
    X.cb = {
        'lat': nc.dram_tensor("cb_lat", [DEPTH, PAST, 256], BF16).ap(),
        'kr': nc.dram_tensor("cb_kr", [DEPTH, PAST, 64], BF16).ap(),
        'fk': nc.dram_tensor("cb_fk", [DEPTH, PAST, 256], BF16).ap(),
        'fv': nc.dram_tensor("cb_fv", [DEPTH, PAST, 256], BF16).ap(),
    }
    X.csrc32 = {'lat': X.c_lat, 'kr': X.c_kr, 'fk': X.c_fk, 'fv': X.c_fv}
    X.cb_ready = set()
    X.bg = []
    X.bg_np = {}
    X.l0_done = False
    X.wsrc32 = {'in': X.w_in, 'uq': X.w_uq, 'ukv': X.w_ukv, 'mem': X.w_mem, 'out': X.w_out}
    X.wb_done = set()
    X.wb_later = []
    X.y1 = nc.dram_tensor("y1_scr", [TOWN, D], F32).ap()
    X.ys1 = nc.dram_tensor("ys1_scr", [DEC, D], F32).ap()
    X.exs = [[nc.dram_tensor("exs_%d_%d" % (l, c), [EX_ROWS, 1024], BF16) for c in range(4)] for l in range(DEPTH)]
    X.exd = [[nc.dram_tensor("exd_%d_%d" % (l, c), [2 * EX_ROWS, 1024], BF16) for c in range(4)] for l in range(DEPTH)]

    with ExitStack() as es:
        P = Prog(nc, es, {'pe': 1, 'act': 8, 'dve': 1, 'pool': 8, 'sp': 16})
        X.P = P

        def sb(n, s, d):
            return es.enter_context(nc.sbuf_tensor(n, s, d))

        X.identb = sb("identb", [128, 128], BF16); X.identf = sb("identf", [128, 128], F32)
        X.onesb = sb("onesb", [128, 128], BF16); X.onesf = sb("onesf", [128, 128], F32)
        X.trif = sb("trif", [128, 128], F32); X.maskb = sb("maskb", [128, 5, 128], BF16)
        X.epsb = sb("epsb", [128, 1], F32); X.oneb = sb("oneb", [128, 1], F32)
        X.dummy = sb("bar_dummy_t", [128, 2], F32)
        X.XT = sb("XT", [128, NDT, TOWN], BF16)
        X.MIXT = sb("MIXT", [128, NDT, TOWN], BF16)
        X.latT = sb("latT", [128, 2, SEQ], BF16)
        X.KV = sb("KV24", [128, 12288], BF16)
        X.cqnT = sb("cqnT", [128, 3, TOWN], BF16)
        X.qrT = sb("qrT", [128, 2, TOWN], BF16)
        X.memT = sb("memT", [128, NDT, 256], BF16)
        X.mkT = sb("mkT", [128, 2, 256], BF16)
        X.mvA = sb("mvA", [128, 2, 384], BF16)
        X.sfkT = sb("sfkT", [128, 2, DEC], BF16)
        X.sfv = sb("sfv", [DEC, 256], BF16)
        X.slogf = sb("slogf", [DEC, 4], F32)
        X.slat = sb("slat", [DEC, 256], BF16)
        nscr = (nc.sbuf_bytes_remaining - 2048) // 2
        nscr -= nscr % 2
        X.nscr = nscr
        X.scr = sb("scr_big_t", [128, nscr], BF16)
        X.pb = [es.enter_context(nc.psum_tensor("pb%d" % i, [128, 512], F32)) for i in range(8)]
        X.KTh = X.KV[:, 0:4096]
        X.Vh = X.KV[:, 4096:8192].rearrange("p (k d) -> p k d", d=128)
        X.krT = X.KV[:, 8192:12288]
        X.krT3 = X.krT.rearrange("p (k t) -> p k t", t=128)
        X.fvA = X.KV[:, :].rearrange("p (k c) -> p k c", c=384)
        X.latT4 = X.latT[:, :, :].rearrange("p r (k t) -> p r k t", t=128)

        emit_consts(X)
        pr = Ctx(); pr.kind = 'p'; pr.R = 128; pr.NTB = 16; pr.NT = TOWN
        pr.groups = [(g * 512, 512) for g in range(4)]
        pr.NKB = 32
        sa = Ctx(); sa.kind = 's'; sa.R = DEC; sa.NTB = 1; sa.NT = DEC
        sa.groups = [(0, DEC)]
        sa.NKB = 17
        stop = int(os.environ.get('KDBG_STOP', '99'))
        only_sample = bool(os.environ.get('KDBG_SAMPLE'))
        st = 0
        for pa in ((sa,) if only_sample else (pr, sa)):
            for L in range(DEPTH):
                for ph in ((phase0,) if L == 0 else ()) + (phase1, phase2, phase3):
                    if st < stop:
                        if ph is phase0:
                            ph(X, pa)
                        else:
                            ph(X, pa, L)
                    st += 1
        if os.environ.get('KDBG_STATS'):
            for e in ENGS:
                print('ENG', e, 'ops', len(P.ops[e]), 'dmas', P.dma_count[e])
        P.run_block()
        if os.environ.get('KDBG_STATS'):
            for e in ENGS:
                print('ENG', e, 'signals', max([o.seq or 0 for o in P.ops[e]] + [0]))
    return nc


def bg_add(X, kind, nm, L, npieces):
    src = (X.wsrc32 if kind == 'wb' else X.csrc32)[nm][L]
    dst = (X.wb if kind == 'wb' else X.cb)[nm][L]
    rows = src.shape[0]
    step = rows // npieces
    for i in range(npieces):
        X.bg.append((kind, nm, L, i, npieces, dst[i * step:(i + 1) * step, :], src[i * step:(i + 1) * step, :]))
    X.bg_np[(kind, nm, L)] = npieces


def bg_pump(X, n=1):
    P = X.P
    for _ in range(n):
        if not X.bg:
            return
        kind, nm, L, i, npieces, dst, src = X.bg.pop(0)
        P.dma('pool', dst, src, [], [(kind, nm, L, i)])
        if i == npieces - 1:
            if kind == 'cb':
                X.cb_ready.add((nm, L))
            elif L == 1 or nm == 'out' or X.l0_done:
                X.wb_done.add((nm, L))
            else:
                X.wb_later.append((nm, L))


def emit_weight_cache(X):
    for (nm, L, k) in (('out', 0, 4), ('in', 1, 8), ('uq', 1, 1), ('ukv', 1, 1), ('mem', 1, 2), ('out', 1, 4),
                       ('in', 0, 8), ('uq', 0, 1), ('ukv', 0, 1), ('mem', 0, 2)):
        if ('wb', nm, L) not in X.bg_np:
            bg_add(X, 'wb', nm, L, k)


def emit_cache_cast(X):
    for L in range(DEPTH):
        for (nm, k) in (('lat', 2), ('kr', 1), ('fk', 2), ('fv', 2)):
            bg_add(X, 'cb', nm, L, k)


def cload(X, nm, L, dst, writes):
    P = X.P
    if (nm, L) in X.cb_ready:
        P.dma('sp', dst, X.cb[nm][L].rearrange("(k p) c -> p k c", p=128),
              [('cb', nm, L, i) for i in range(X.bg_np[('cb', nm, L)])], writes)
    else:
        P.dma('pool', dst, X.csrc32[nm][L].rearrange("(k p) c -> p k c", p=128), [], writes)


def wload(X, nm, L, dst, view_fn, writes):
    P = X.P
    if (nm, L) in X.wb_done:
        P.dma('sp', dst, view_fn(X.wb[nm][L]), [('wb', nm, L, i) for i in range(X.bg_np[('wb', nm, L)])], writes)
    else:
        P.dma('pool', dst, view_fn(X.wsrc32[nm][L]), [], writes)


def barrier(X, grp):
    P = X.P
    MEMSET(P, 'pool', X.dummy[:, 0:1], 0.0, [], [('BAR', grp), 'dummy'])


def emit_consts(X):
    P = X.P
    MEMSET(P, 'pool', X.identf[:, :], 0.0, [], ['identf'])
    P.op('pool', lambda e: e.affine_select(out=X.identf[:, :], in_=X.identf[:, :], pattern=[[-1, 128]],
                                           compare_op=ALU.not_equal, fill=1.0, base=0, channel_multiplier=1),
         ['identf'], ['identf'])
    CP(P, 'dve', X.identb[:, :], X.identf[:, :], ['identf'], ['identb'])
    MEMSET(P, 'pool', X.onesf[:, :], 1.0, [], ['onesf'])
    MEMSET(P, 'pool', X.onesb[:, :], 1.0, [], ['onesb'])
    MEMSET(P, 'pool', X.trif[:, :], 1.0, [], ['trif'])
    P.op('pool', lambda e: e.affine_select(out=X.trif[:, :], in_=X.trif[:, :], pattern=[[1, 128]],
                                           compare_op=ALU.is_ge, fill=0.0, base=0, channel_multiplier=-1),
         ['trif'], ['trif'])
    MEMSET(P, 'pool', X.epsb[:, :], EPS, [], ['epsb'])
    MEMSET(P, 'pool', X.oneb[:, :], 1.0, [], ['oneb'])
    P.dma('pool', X.maskb[:, :, :], X.masks[:, :, :], [], ['maskb'])
    MEMSET(P, 'pool', X.mvA[:, :, 64:128], 1.0, [], ['mvA1'])
    MEMSET(P, 'pool', X.mvA[:, :, 256:320], 1.0, [], ['mvA2'])


def transpose_rows_to_XT(X, pa, xtok, tokx, tb, R, use_banks):
    P = X.P
    for half in range(2):
        bi = use_banks[half]
        bank = X.pb[bi]
        for j in range(4):
            dt = half * 4 + j
            TR(P, bank[:, j * 128:j * 128 + R], xtok[0:R, dt * 128:(dt + 1) * 128], X.identf[0:R, 0:R],
               [tokx, 'identf'], [('ps', bi)])
        src = bank[:, :].rearrange("p (a b) -> p a b", b=128)[:, :, 0:R]
        dst = X.XT[:, half * 4:(half + 1) * 4, tb * 128:tb * 128 + R]
        CP(P, 'act' if half == 0 else 'dve', dst, src, [('ps', bi)], [('XT', tb)])


def phase0(X, pa):
    P = X.P
    barrier(X, 'S')
    cv = Carve(X.scr, X.nscr)
    cv.off = X.nscr - 4 * 2048 - 16
    xt = [cv.take([1024], F32) for _ in range(4)]
    src = X.xp if pa.kind == 'p' else X.xs
    R = pa.R
    for tb in range(pa.NTB):
        b = tb % 4
        tok = ('S', 'xtok', b)
        P.dma('sp' if tb % 2 == 0 else 'act', xt[b][0:R, :], src[tb * 128:tb * 128 + R, :], [], [tok])
        transpose_rows_to_XT(X, pa, xt[b], tok, tb, R, (2 * (tb % 2), 2 * (tb % 2) + 1))
    if pa.kind == 'p':
        for mb in range(2):
            b = mb % 2
            tok = ('S', 'xtok', b)
            P.dma('sp', xt[b][:, :], X.memp[mb * 128:(mb + 1) * 128, :], [], [tok])
            for half in range(2):
                bi = 4 + 2 * b + half
                bank = X.pb[bi]
                for j in range(4):
                    dt = half * 4 + j
                    TR(P, bank[:, j * 128:(j + 1) * 128], xt[b][:, dt * 128:(dt + 1) * 128], X.identf[:, :],
                       [tok, 'identf'], [('ps', bi)])
                srcv = bank[:, :].rearrange("p (a b) -> p a b", b=128)
                CP(P, 'act' if half == 0 else 'dve', X.memT[:, half * 4:(half + 1) * 4, mb * 128:(mb + 1) * 128],
                   srcv, [('ps', bi)], ['memT'])


WC = 384 + 388 + 512


def phase1(X, pa, L):
    P = X.P
    R = pa.R
    early = (pa.kind == 'p' and L == 0)
    if not early:
        barrier(X, 'S')
        barrier(X, 'KV')
        barrier(X, 'LT')
    cv = Carve(X.scr, X.nscr)
    WA = cv.take([NDT, WC], BF16)
    WQR = cv.take([3, 512], BF16)
    rope = cv.take([16, 128], F32)
    kvg = cv.take([256], F32)
    qg = cv.take([384], F32)
    bfb = cv.take([4], F32)
    junk = cv.take([384], BF16)
    NB = 2
    cqn_b = [cv.take([384], BF16) for _ in range(NB)]
    lat_f = [cv.take([256], F32) for _ in range(NB)]
    lat_b = [cv.take([256], BF16) for _ in range(NB)]
    kr_f = [cv.take([64], F32) for _ in range(NB)]
    kr_t = [cv.take([64], F32) for _ in range(NB)]
    kr_b = [cv.take([128], BF16) for _ in range(NB)]
    fkv_f = [cv.take([512], F32) for _ in range(NB)]
    fk_b = [cv.take([256], BF16) for _ in range(NB)]
    stat = [cv.take([8], F32) for _ in range(NB)]
    qro_a = [cv.take([256], F32) for _ in range(NB)]
    qro_t = [cv.take([256], F32) for _ in range(NB)]
    qro_b = [cv.take([256], BF16) for _ in range(NB)]
    latT_st = [cv.take([2, 512], BF16) for _ in range(2)]
    krT_st = [cv.take([512], BF16) for _ in range(2)]
    fkT_st = [cv.take([2, 512], BF16) for _ in range(2)]
    fv_st = [cv.take([4, 384], BF16) for _ in range(2)]
    zc = [cv.take([4, 4], F32) for _ in range(2)]
    ec = [cv.take([4, 4], F32) for _ in range(2)]
    lgf = [cv.take([4, 4], F32) for _ in range(2)]
    lg3 = [cv.take([4, 16], BF16) for _ in range(2)]
    r1 = [cv.take([4, 4], F32) for _ in range(2)]
    r2 = [cv.take([4, 4], F32) for _ in range(2)]

    tWA = ('S', 'WA'); tWQR = ('S', 'WQR')
    for (d0, d1, s0, s1) in ((0, 384, 0, 384), (384, 704, 384, 704), (704, 736, 672, 704), (736, 768, 640, 672),
                             (768, 772, 1984, 1988), (772, 1284, 1472, 1984)):
        wload(X, 'in', L, WA[:, :, d0:d1],
              (lambda a, b_: (lambda w: w.rearrange("(dt p) e -> p dt e", p=128)[:, :, a:b_]))(s0, s1), [tWA])
    for rt in range(3):
        WQ4 = WQR[:, rt, :].rearrange("p (s h c) -> p s h c", s=2, h=4)
        u3f = lambda a, b_, rt_: (lambda w: w.rearrange("(rt p) (h c) -> p rt h c", p=128, h=4)[:, rt_, :, a:b_])
        wload(X, 'uq', L, WQ4[:, 0, :, :], u3f(128, 192, rt), [tWQR])
        wload(X, 'uq', L, WQ4[:, 1, :, 0:32], u3f(160, 192, rt), [tWQR])
        wload(X, 'uq', L, WQ4[:, 1, :, 32:64], u3f(128, 160, rt), [tWQR])
    if early:
        barrier(X, 'S')
        barrier(X, 'KV')
        barrier(X, 'LT')
        if not os.environ.get('KDBG_NOWB'):
            X.l0_done = True
            for (nm, k) in (('mem', 2), ('ukv', 1), ('uq', 1), ('in', 8)):
                bg_add(X, 'wb', nm, 0, k)
    tvec = ('S', 'vec')
    if pa.kind == 'p':
        P.dma('sp', rope[:, :, :], X.ropeP[:, :, :], [], [tvec])
    else:
        P.dma('sp', rope[0:DEC, 0:1, :], X.ropeS[:, :, :], [], [tvec])
    P.dma('sp', kvg[:, :], X.kvnorm[L:L + 1, :].partition_broadcast(128), [], [tvec])
    P.dma('sp', qg[:, :], X.qnorm[L:L + 1, :].partition_broadcast(128), [], [tvec])
    P.dma('sp', bfb[:, :], X.b_f[L:L + 1, :].partition_broadcast(128), [], [tvec])

    if pa.kind == 'p':
        for i in range(2):
            MEMSET(P, 'pool', lg3[i][:, :, :], 0.0, [], [('S', 'lg3', i)])
            MEMSET(P, 'pool', fv_st[i][:, :, 64:128], 1.0, [], [('S', 'fv_st', i)])
            MEMSET(P, 'pool', fv_st[i][:, :, 256:320], 1.0, [], [('S', 'fv_st', i)])
        o_lat, o_kr, o_fk, o_fv, o_logf = X.o_lat, X.o_kr, X.o_fk, X.o_fv, X.o_logf
    else:
        o_lat, o_kr, o_fk, o_fv, o_logf = X.s_lat, X.s_kr, X.s_fk, X.s_fv, X.s_logf

    pbT = X.pb[6][:, :].bitcast(BF16)
    pbT2 = X.pb[5][:, :].bitcast(BF16)
    def body(tb, stage):
        b = tb % 2
        ci = tb // 4
        cb = ci % 2
        bi4 = tb % 4
        r0 = tb * 128
        tok = lambda n: ('S', n, b)
        bA = 0 + b; bB = 2 + b; bC = 4
        if stage == 1:
            for dt in range(NDT):
                lhsT = X.XT[:, dt, r0:r0 + R]
                st = (dt == 0); sp = (dt == NDT - 1)
                MM(P, X.pb[bA][0:R, 0:384], lhsT, WA[:, dt, 0:384], st, sp, [('XT', tb), tWA], [('ps', bA)])
                MM(P, X.pb[bB][0:R, 0:388], lhsT, WA[:, dt, 384:772], st, sp, [('XT', tb), tWA], [('ps', bB)])
                MM(P, X.pb[bC][0:R, 0:512], lhsT, WA[:, dt, 772:1284], st, sp, [('XT', tb), tWA], [('ps', bC)])
            st_ = stat[b]
            ACTF(P, junk[0:R, :], X.pb[bA][0:R, 0:384], AF.Square, [('ps', bA)], [('S', 'junk'), tok('stat')],
                 accum=st_[0:R, 0:1])
            ACTF(P, junk[0:R, 0:256], X.pb[bB][0:R, 0:256], AF.Square, [('ps', bB)], [('S', 'junk'), tok('stat')],
                 accum=st_[0:R, 1:2])
            CP(P, 'act', fkv_f[b][0:R, :], X.pb[bC][0:R, 0:512], [('ps', bC)], [tok('fkv_f')])
            TT(P, 'dve', kr_f[b][0:R, :], X.pb[bB][0:R, 256:320], rope[0:R, tb, 0:64], ALU.mult,
               [('ps', bB), tvec], [tok('kr_f')])
            TT(P, 'dve', kr_t[b][0:R, :], X.pb[bB][0:R, 320:384], rope[0:R, tb, 64:128], ALU.mult,
               [('ps', bB), tvec], [tok('kr_t')])
            TT(P, 'dve', kr_f[b][0:R, :], kr_f[b][0:R, :], kr_t[b][0:R, :], ALU.add,
               [tok('kr_f'), tok('kr_t')], [tok('kr_f')])
            TT(P, 'dve', zc[cb][0:R, bi4, :], X.pb[bB][0:R, 384:388], bfb[0:R, :], ALU.add,
               [('ps', bB), tvec], [('S', 'zc', cb)])
            ACTF(P, st_[0:R, 2:3], st_[0:R, 0:1], AF.Ln, [tok('stat'), 'epsb'], [tok('stat')],
                 bias=X.epsb[0:R, :], scale=1.0 / 384)
            ACTF(P, st_[0:R, 3:4], st_[0:R, 1:2], AF.Ln, [tok('stat'), 'epsb'], [tok('stat')],
                 bias=X.epsb[0:R, :], scale=1.0 / 256)
            ACTF(P, st_[0:R, 4:6], st_[0:R, 2:4], AF.Exp, [tok('stat')], [tok('stat')], scale=-0.5)
            P.dma('sp', o_kr[L, r0:r0 + R, :], kr_f[b][0:R, :], [tok('kr_f')], [])
            CP(P, 'pool', kr_b[b][0:R, 0:64], kr_f[b][0:R, :], [tok('kr_f')], [tok('kr_b')])
            CP(P, 'pool', kr_b[b][0:R, 64:128], kr_f[b][0:R, :], [tok('kr_f')], [tok('kr_b')])
            P.dma('sp', o_fk[L, r0:r0 + R, :], fkv_f[b][0:R, 0:256], [tok('fkv_f')], [])
            P.dma('sp', o_fv[L, r0:r0 + R, :], fkv_f[b][0:R, 256:512], [tok('fkv_f')], [])
            CP(P, 'pool', fk_b[b][0:R, :], fkv_f[b][0:R, 0:256], [tok('fkv_f')], [tok('fk_b')])
            if pa.kind == 'p':
                for (d0, d1, s0, s1) in ((0, 64, 256, 320), (128, 256, 320, 448), (320, 384, 448, 512)):
                    CP(P, 'pool', fv_st[cb][0:R, bi4, d0:d1], fkv_f[b][0:R, s0:s1], [tok('fkv_f')], [('S', 'fv_st', cb)])
            else:
                CP(P, 'pool', X.sfv[0:R, :], fkv_f[b][0:R, 256:512], [tok('fkv_f')], ['sfv'])
            STT(P, 'dve', cqn_b[b][0:R, :], X.pb[bA][0:R, 0:384], st_[0:R, 4:5], qg[0:R, :], ALU.mult, ALU.mult,
                [('ps', bA), tok('stat'), tvec], [tok('cqn_b')])
            STT(P, 'dve', lat_f[b][0:R, :], X.pb[bB][0:R, 0:256], st_[0:R, 5:6], kvg[0:R, :], ALU.mult, ALU.mult,
                [('ps', bB), tok('stat'), tvec], [tok('lat_f')])
            P.dma('sp', o_lat[L, r0:r0 + R, :], lat_f[b][0:R, :], [tok('lat_f')], [])
            CP(P, 'pool', lat_b[b][0:R, :], lat_f[b][0:R, :], [tok('lat_f')], [tok('lat_b')])
            if pa.kind == 's':
                CP(P, 'pool', X.slat[0:R, :], lat_f[b][0:R, :], [tok('lat_f')], ['slat'])
        if stage == 2:
            for j in range(3):
                TR(P, pbT[:, j * 128:j * 128 + R], cqn_b[b][0:R, j * 128:(j + 1) * 128], X.identb[0:R, 0:R],
                   [tok('cqn_b'), 'identb'], [('ps', 6)])
            for j in range(2):
                TR(P, pbT[:, 384 + j * 128:384 + j * 128 + R], lat_b[b][0:R, j * 128:(j + 1) * 128], X.identb[0:R, 0:R],
                   [tok('lat_b'), 'identb'], [('ps', 6)])
            TR(P, pbT[:, 640:640 + R], kr_b[b][0:R, :], X.identb[0:R, 0:R], [tok('kr_b'), 'identb'], [('ps', 6)])
            for j in range(2):
                TR(P, pbT[:, 768 + j * 128:768 + j * 128 + R], fk_b[b][0:R, j * 128:(j + 1) * 128], X.identb[0:R, 0:R],
                   [tok('fk_b'), 'identb'], [('ps', 6)])
            v3 = pbT[:, 0:384].rearrange("p (a b) -> p a b", b=128)[:, :, 0:R]
            CP(P, 'act', X.cqnT[:, :, r0:r0 + R], v3, [('ps', 6)], [('cqnT', tb)])
            vl = pbT[:, 384:640].rearrange("p (a b) -> p a b", b=128)[:, :, 0:R]
            vf = pbT[:, 768:1024].rearrange("p (a b) -> p a b", b=128)[:, :, 0:R]
            if pa.kind == 'p':
                CP(P, 'dve', latT_st[cb][:, :, bi4 * 128:bi4 * 128 + R], vl, [('ps', 6)], [('S', 'latT_st', cb)])
                CP(P, 'act', krT_st[cb][0:64, bi4 * 128:bi4 * 128 + R], pbT[0:64, 640:640 + R], [('ps', 6)],
                   [('S', 'krT_st', cb)])
                CP(P, 'dve', fkT_st[cb][:, :, bi4 * 128:bi4 * 128 + R], vf, [('ps', 6)], [('S', 'fkT_st', cb)])
            else:
                CP(P, 'dve', X.latT[:, :, PAST:PAST + R], vl, [('ps', 6)], [('LT', 'new')])
                CP(P, 'act', X.krT[:, PAST:PAST + R], pbT[:, 640:640 + R], [('ps', 6)], [('KV', 'krnew')])
                CP(P, 'dve', X.sfkT[:, :, 0:R], vf, [('ps', 6)], ['sfkT'])
            for rt in range(3):
                MM(P, X.pb[7][0:R, 0:512], X.cqnT[:, rt, r0:r0 + R], WQR[:, rt, :], rt == 0, rt == 2,
                   [('cqnT', tb), tWQR], [('ps', 7)])
            cc4 = rope[0:R, tb, 0:64].unsqueeze(1).broadcast_to([R, 4, 64])
            ss4 = rope[0:R, tb, 64:128].unsqueeze(1).broadcast_to([R, 4, 64])
            q4 = lambda ap: ap.rearrange("p (h c) -> p h c", h=4)
            TT(P, 'dve', q4(qro_a[b][0:R, :]), q4(X.pb[7][0:R, 0:256]), cc4, ALU.mult, [('ps', 7), tvec], [tok('qro_a')])
            TT(P, 'dve', q4(qro_t[b][0:R, :]), q4(X.pb[7][0:R, 256:512]), ss4, ALU.mult, [('ps', 7), tvec], [tok('qro_t')])
            TT(P, 'pool', qro_b[b][0:R, :], qro_a[b][0:R, :], qro_t[b][0:R, :], ALU.add,
               [tok('qro_a'), tok('qro_t')], [tok('qro_b')])
        if stage == 3:
            for j in range(2):
                TR(P, pbT2[:, j * 128:j * 128 + R], qro_b[b][0:R, j * 128:(j + 1) * 128], X.identb[0:R, 0:R],
                   [tok('qro_b'), 'identb'], [('ps', 5)])
            vq = pbT2[:, 0:256].rearrange("p (a b) -> p a b", b=128)[:, :, 0:R]
            CP(P, 'act', X.qrT[:, :, r0:r0 + R], vq, [('ps', 5)], [('qrT', tb)])

            last_in_chunk = (bi4 == 3) or (tb == pa.NTB - 1)
            if last_in_chunk:
                nb = bi4 + 1
                z2 = zc[cb][0:R, 0:nb, :]
                ACTF(P, ec[cb][0:R, 0:nb, :], z2, AF.Exp, [('S', 'zc', cb)], [('S', 'ec', cb)], scale=-1.0)
                ACTF(P, ec[cb][0:R, 0:nb, :], ec[cb][0:R, 0:nb, :], AF.Ln, [('S', 'ec', cb), 'oneb'], [('S', 'ec', cb)],
                     bias=X.oneb[0:R, :], scale=1.0)
                TS(P, 'dve', lgf[cb][0:R, 0:nb, :], ec[cb][0:R, 0:nb, :], -1.0, None, ALU.mult, None,
                   [('S', 'ec', cb)], [('S', 'lgf', cb)])
                dst = o_logf[L, ci * 512:ci * 512 + (nb - 1) * 128 + R, :]
                if pa.kind == 'p':
                    P.dma('sp', dst.rearrange("(k p) h -> p k h", p=128), lgf[cb][:, :, :], [('S', 'lgf', cb)], [])
                    l3 = lg3[cb][:, :, 0:12].rearrange("p k (s h) -> p k s h", s=3)
                    CP(P, 'dve', l3[:, :, 0, :], lgf[cb][:, :, :], [('S', 'lgf', cb)], [('S', 'lg3', cb)])
                    TT(P, 'dve', r1[cb][:, :, :], lgf[cb][:, :, :], l3[:, :, 0, :], ALU.subtract,
                       [('S', 'lgf', cb), ('S', 'lg3', cb)], [('S', 'r1', cb)])
                    CP(P, 'dve', l3[:, :, 1, :], r1[cb][:, :, :], [('S', 'r1', cb)], [('S', 'lg3', cb)])
                    TT(P, 'dve', r2[cb][:, :, :], r1[cb][:, :, :], l3[:, :, 1, :], ALU.subtract,
                       [('S', 'r1', cb), ('S', 'lg3', cb)], [('S', 'r2', cb)])
                    CP(P, 'dve', l3[:, :, 2, :], r2[cb][:, :, :], [('S', 'r2', cb)], [('S', 'lg3', cb)])
                    exs = X.exs[L][ci]
                    ea = exs.ap()
                    tex = ('exs', L, ci)
                    reg = lambda a, b_, c_: ea[a:b_, :].rearrange("a (b c) -> (a b) c", c=c_)
                    P.dma('sp', reg(0, 128, 512).rearrange("(rt p) t -> p rt t", p=128), latT_st[cb][:, :, :],
                          [('S', 'latT_st', cb)], [tex])
                    P.dma('sp', reg(128, 256, 512).rearrange("(rt p) t -> p rt t", p=128), fkT_st[cb][:, :, :],
                          [('S', 'fkT_st', cb)], [tex])
                    P.dma('sp', reg(256, 448, 128).rearrange("(k p x) c -> p k (x c)", p=128, x=3), fv_st[cb][:, :, :],
                          [('S', 'fv_st', cb)], [tex])
                    P.dma('sp', reg(448, 480, 512), krT_st[cb][0:64, :], [('S', 'krT_st', cb)], [tex])
                    P.dma('sp', reg(480, 488, 16).rearrange("(k p) c -> p k c", p=128), lg3[cb][:, :, :],
                          [('S', 'lg3', cb)], [tex])
                    exd = X.exd[L][ci]
                    if os.environ.get('KDBG_NOCC'):
                        P.dma('sp', exd.ap()[0:EX_ROWS, :], ea[:, :], [tex], [('exd', L, ci)])
                        P.dma('sp', exd.ap()[EX_ROWS:2 * EX_ROWS, :], ea[:, :], [tex], [('exd', L, ci)])
                    else:
                      P.op('pool', (lambda exs_, exd_: (lambda e: e.collective_compute(
                        "AllGather", ALU.bypass, replica_groups=GROUPS,
                        ins=[exs_.ap().opt()], outs=[exd_.ap().opt()])))(exs, exd),
                        [tex], [('exd', L, ci)], cc=True)
                else:
                    P.dma('sp', dst, lgf[cb][0:R, 0, :], [('S', 'lgf', cb)], [])
                    CP(P, 'pool', X.slogf[0:R, :], lgf[cb][0:R, 0, :], [('S', 'lgf', cb)], ['slogf'])

    for i in range(pa.NTB + 2):
        if i < pa.NTB:
            body(i, 1)
            if early:
                bg_pump(X, 1)
        if 0 <= i - 1 < pa.NTB:
            body(i - 1, 2)
        if 0 <= i - 2 < pa.NTB:
            body(i - 2, 3)


def vis_prompt(j, kb, kind):
    if kb < 8 * j:
        return (0, None)
    m = kb - 8 * j
    if m > 7:
        return None
    c0 = (m // 2) * 128
    base = 0 if kind == 'mla' else 2
    return (c0, base + (m % 2))


def phase2(X, pa, L):
    P = X.P
    barrier(X, 'S')
    cv = Carve(X.scr, X.nscr)
    Y = Ctx()
    Y.WU = cv.take([2, 1024], BF16)
    Y.WQN = cv.take([3, 512], BF16)
    Y.wcol = [cv.take([NDT, 128], BF16) for _ in range(4)]
    Y.q_b = [cv.take([512], BF16) for _ in range(2)]
    Y.g_b = [cv.take([512], BF16) for _ in range(2)]
    Y.th = [cv.take([512], F32) for _ in range(2)]
    Y.PT = [cv.take([512], BF16) for _ in range(3)]
    Y.Rinv = [cv.take([512], F32) for _ in range(2)]
    Y.tmpO = [cv.take([512], F32) for _ in range(2)]
    Y.lgall = cv.take([32, 16], BF16)
    Y.lgA = cv.take([32, 4], F32)
    Y.lgB = cv.take([32, 4], F32)
    Y.logf = cv.take([32, 4], F32)
    Y.cumin = cv.take([32, 4], F32)
    Y.tot = cv.take([32, 4], F32)
    Y.scA = cv.take([32, 4], F32)
    Y.scB = cv.take([32, 4], F32)
    Y.cum = cv.take([32, 4], F32)
    Y.bias = [cv.take([32], F32) for _ in range(2)]
    Y.CR = [cv.take([512], BF16) for _ in range(2)]
    Y.d4 = cv.take([4], F32)
    Y.r4 = cv.take([4], F32)
    Y.hi4 = cv.take([4], BF16)
    Y.lo4 = cv.take([4], BF16)
    Y.WM = cv.take([NDT, 512], BF16)
    Y.mkv_f = [cv.take([512], F32) for _ in range(2)]
    Y.mk_b = cv.take([256], BF16)
    Y.stage = Y.WM[:, :, :].rearrange("p a b -> p (a b)").rearrange("p (a b) -> p a b", b=256)
    Y.qz = [[cv.take([512], BF16) for _ in range(2)] for _ in range(2)]
    Y.qrz = [[cv.take([512], BF16) for _ in range(2)] for _ in range(2)]
    Y.stkr = cv.take([16, 128], BF16)
    Y.WUT = cv.take([8, 128], BF16)
    Y.qn_all = cv.take([64], BF16)
    Y.qabs = cv.take([2, 64], BF16)
    Y.g_all = cv.take([64], BF16)
    Y.th_all = cv.take([64], F32)
    Y.qr_all = cv.take([64], BF16)
    Y.OLn = cv.take([2, 64], BF16)
    Y.Rs = cv.take([64], F32)
    print('phase2 scratch used', cv.off * 2, 'of', X.nscr * 2) if os.environ.get('KDBG_STATS') else None
    Y.wcol_n = 0
    Y.qbuf_n = 0
    Y.od_n = 0
    Y.unit_n = 0

    wload(X, 'mem', L, Y.WM[:, :, :], lambda w: w.rearrange("(dt p) e -> p dt e", p=128), [('S', 'WM')])
    mem_kside(X, pa, L, Y)
    for a in range(2):
        for b_ in range(2):
            MEMSET(P, 'pool', Y.qz[a][b_][:, :], 0.0, [], [('S', 'q_b', a)])
            MEMSET(P, 'pool', Y.qrz[a][b_][:, :], 0.0, [], [('S', 'qrz', a, b_)])
    barrier(X, 'KV')
    barrier(X, 'LT')
    pre_m = {0: (load_wcol(X, L, Y, 2244), load_wcol(X, L, Y, 2500)),
             1: (load_wcol(X, L, Y, 2244 + 128), load_wcol(X, L, Y, 2500 + 128))}
    wload(X, 'ukv', L, Y.WU[:, :, :], lambda w: w.rearrange("(rt p) e -> p rt e", p=128), [('S', 'WU')])
    for rt in range(3):
        wload(X, 'uq', L, Y.WQN[:, rt, :].rearrange("p (h c) -> p h c", h=4),
              (lambda rt_: (lambda w: w.rearrange("(rt p) (h c) -> p rt h c", p=128, h=4)[:, rt_, :, 0:128]))(rt),
              [('S', 'WQN')])
    load_kside_mla(X, pa, L, Y)
    load_logf(X, pa, L, Y, 'dma')
    for p in range(2):
        pair_attention(X, pa, L, Y, 'mem', p, pre_m.get(p))
    load_logf(X, pa, L, Y, 'cmp')
    fox_bias_prep(X, pa, L, Y)
    pre_w = {}
    if pa.kind == 'p' and L == 1 and not os.environ.get('KDBG_NOWB'):
        emit_cache_cast(X)
    if pa.kind == 'p' and L == 0 and not os.environ.get('KDBG_NOWB'):
        for h in range(3):
            pre_w[h] = load_wcol(X, L, Y, 704 + h * 128)
        emit_weight_cache(X)
    def fk_after_last_decompress():
        barrier(X, 'LT')
        load_kside_fox(X, pa, L, Y, 'fk')
    if pa.kind == 's' and not os.environ.get('KDBG_NOABS'):
        mla_sample_absorbed(X, pa, L, Y)
        fk_after_last_decompress()
    else:
        for h in range(4):
            mla_head(X, pa, L, Y, h, fk_after_last_decompress if h == 3 else None, pre_w.get(h))
    pre_f = {p: (load_wcol(X, L, Y, 1216 + p * 128), load_wcol(X, L, Y, 1988 + p * 128)) for p in range(2)}
    barrier(X, 'KV')
    load_kside_fox(X, pa, L, Y, 'fv')
    for p in range(2):
        pair_attention(X, pa, L, Y, 'fox', p, pre_f[p])
    bg_pump(X, len(X.bg))
    if pa.kind == 'p' and L == 0:
        X.l0_done = True
    for it in X.wb_later:
        X.wb_done.add(it)
    X.wb_later = []


def load_wcol(X, L, Y, c0):
    P = X.P
    i = Y.wcol_n % 4
    Y.wcol_n += 1
    wload(X, 'in', L, Y.wcol[i][:, :, :],
          lambda w: w.rearrange("(dt p) e -> p dt e", p=128)[:, :, c0:c0 + 128], [('S', 'wcol', i)])
    return Y.wcol[i], ('S', 'wcol', i)


def proj_feature_major(X, pa, Y, w, tw, g0, n, bank):
    P = X.P
    tbs = sorted(set(range(g0 // 128, (g0 + n + 127) // 128)))
    rd = [('XT', t) for t in tbs] + [tw]
    for dt in range(NDT):
        MM(P, X.pb[bank][:, 0:n], w[:, dt, :], X.XT[:, dt, g0:g0 + n], dt == 0, dt == NDT - 1, rd, [('ps', bank)])


def gate_from_bank(X, Y, bank, n, qb):
    P = X.P
    ACTF(P, Y.th[qb][:, 0:n], X.pb[bank][:, 0:n], AF.Tanh, [('ps', bank)], [('S', 'th', qb)], scale=0.5)
    STT(P, 'dve', Y.g_b[qb][:, 0:n], Y.th[qb][:, 0:n], 1.0, X.pb[bank][:, 0:n], ALU.add, ALU.mult,
        [('S', 'th', qb), ('ps', bank)], [('S', 'g_b', qb)])


def mem_kside(X, pa, L, Y):
    P = X.P
    pbT = X.pb[5][:, :].bitcast(BF16)
    for mb in range(2):
        f = Y.mkv_f[mb]
        tf = ('S', 'mkv_f', mb)
        if pa.kind == 'p':
            bank = 6 + mb
            for dt in range(NDT):
                MM(P, X.pb[bank][:, 0:512], X.memT[:, dt, mb * 128:(mb + 1) * 128], Y.WM[:, dt, :],
                   dt == 0, dt == NDT - 1, ['memT', ('S', 'WM')], [('ps', bank)])
            CP(P, 'act', f[:, :], X.pb[bank][:, 0:512], [('ps', bank)], [tf])
            P.dma('sp', X.o_mk[L, mb * 128:(mb + 1) * 128, :], f[:, 0:256], [tf], [])
            P.dma('sp', X.o_mv[L, mb * 128:(mb + 1) * 128, :], f[:, 256:512], [tf], [])
        else:
            P.dma('sp', f[:, 0:256], X.c_mk[L, mb * 128:(mb + 1) * 128, :], [], [tf])
            P.dma('sp', f[:, 256:512], X.c_mv[L, mb * 128:(mb + 1) * 128, :], [], [tf])
        CP(P, 'pool', Y.mk_b[:, :], f[:, 0:256], [tf], [('S', 'mk_b')])
        CP(P, 'pool', X.mvA[:, mb, 0:64], f[:, 256:320], [tf], ['mvA'])
        CP(P, 'pool', X.mvA[:, mb, 128:256], f[:, 320:448], [tf], ['mvA'])
        CP(P, 'pool', X.mvA[:, mb, 320:384], f[:, 448:512], [tf], ['mvA'])
        for p in range(2):
            TR(P, pbT[:, p * 128:(p + 1) * 128], Y.mk_b[:, p * 128:(p + 1) * 128], X.identb[:, :],
               [('S', 'mk_b'), 'identb'], [('ps', 5)])
        CP(P, 'dve', X.mkT[:, :, mb * 128:(mb + 1) * 128], pbT[:, 0:256].rearrange("p (a b) -> p a b", b=128),
           [('ps', 5)], ['mkT'])


def load_kside_mla(X, pa, L, Y):
    P = X.P
    if pa.kind == 'p':
        for ci in range(4):
            ea = X.exd[L][ci].ap()
            tex = ('exd', L, ci)
            for rr in range(2):
                o = rr * EX_ROWS
                c0 = rr * 2048 + ci * 512
                src = ea[o:o + 128, :].rearrange("a (b c) -> (a b) c", c=512).rearrange("(rt p) t -> p rt t", p=128)
                P.dma('sp', X.latT[:, :, c0:c0 + 512], src, [tex], [('LT', ci)])
                srk = ea[o + 448:o + 480, :].rearrange("a (b c) -> (a b) c", c=512)
                for hf in range(2):
                    P.dma('sp', X.krT[hf * 64:(hf + 1) * 64, c0:c0 + 512], srk, [tex], [('KV', 'kr', ci)])
    else:
        pbT = [X.pb[6][:, :].bitcast(BF16), X.pb[7][:, :].bitcast(BF16)]
        st3 = Y.stage
        cload(X, 'lat', L, st3[:, :, :], [('S', 'WM')])
        for k4 in range(8):
            bsel = k4 % 2
            for i in range(2):
                kb = 2 * k4 + i
                for rt in range(2):
                    TR(P, pbT[bsel][:, (i * 2 + rt) * 128:(i * 2 + rt + 1) * 128], st3[:, kb, rt * 128:(rt + 1) * 128],
                       X.identb[:, :], [('S', 'WM'), 'identb'], [('ps', 6 + bsel)])
            for rt in range(2):
                src = pbT[bsel][:, 0:512].rearrange("p (i r t) -> p i r t", i=2, r=2)[:, :, rt, :]
                CP(P, 'act' if rt == 0 else 'dve', X.latT4[:, rt, 2 * k4:2 * k4 + 2, :], src, [('ps', 6 + bsel)],
                   [('LT', k4)])
        sk = Y.stkr
        cload(X, 'kr', L, sk[:, :, 0:64], [('S', 'stkr')])
        cload(X, 'kr', L, sk[:, :, 64:128], [('S', 'stkr')])
        for k8 in range(4):
            bsel = k8 % 2
            for i in range(4):
                kb = 4 * k8 + i
                TR(P, pbT[bsel][:, i * 128:(i + 1) * 128], sk[:, kb, :], X.identb[:, :],
                   [('S', 'stkr'), 'identb'], [('ps', 6 + bsel)])
            CP(P, 'act' if k8 % 2 == 0 else 'dve', X.krT[:, k8 * 512:(k8 + 1) * 512], pbT[bsel][:, 0:512],
               [('ps', 6 + bsel)], [('KV', 'kr', k8)])


def kblocks(pa):
    if pa.kind == 'p':
        return [(kb, 128, (kb % 2) * 16 + kb // 2) for kb in range(32)]
    return [(kb, 128, kb) for kb in range(16)] + [(16, DEC, 16)]


def mla_head(X, pa, L, Y, h, after_dec=None, pre_w=None):
    P = X.P
    kbs = kblocks(pa)
    nkeys = sum(k[1] for k in kbs)
    lt_all = [('LT', i) for i in range(8)] + [('LT', 'new')]
    wg, twg = pre_w if pre_w is not None else load_wcol(X, L, Y, 704 + h * 128)
    nch = (nkeys + 511) // 512
    for c in range(nch):
        n = min(512, nkeys - c * 512)
        bank = 6 + (c % 2)
        for rt in range(2):
            MM(P, X.pb[bank][:, 0:n], Y.WU[:, rt, h * 256:h * 256 + 128], X.latT[:, rt, c * 512:c * 512 + n],
               rt == 0, rt == 1, [('S', 'WU')] + lt_all, [('ps', bank)])
        CP(P, 'act' if c % 2 == 0 else 'dve', X.KTh[:, c * 512:c * 512 + n], X.pb[bank][:, 0:n], [('ps', bank)],
           [('KV', 'KT', c)])
    sblks = sorted([(sl, KR) for (kb, KR, sl) in kbs])
    ngr = (len(sblks) + 3) // 4
    for g in range(ngr):
        bank = 6 + (g % 2)
        blks = sblks[4 * g:4 * g + 4]
        for i, (kb, KR) in enumerate(blks):
            for rt in range(2):
                MM(P, X.pb[bank][0:KR, i * 128:(i + 1) * 128], X.latT[:, rt, kb * 128:kb * 128 + KR],
                   Y.WU[:, rt, h * 256 + 128:h * 256 + 256], rt == 0, rt == 1, [('S', 'WU')] + lt_all, [('ps', bank)])
        KRg = blks[0][1]
        nb = len(blks)
        if all(b_[1] == 128 for b_ in blks):
            src = X.pb[bank][:, 0:nb * 128].rearrange("p (a b) -> p a b", b=128)
            CP(P, 'dve' if g % 2 == 0 else 'act', X.Vh[:, 4 * g:4 * g + nb, :], src, [('ps', bank)], [('KV', 'V', g)])
        else:
            for i, (kb, KR) in enumerate(blks):
                CP(P, 'dve', X.Vh[0:KR, kb, :], X.pb[bank][0:KR, i * 128:(i + 1) * 128], [('ps', bank)],
                   [('KV', 'V', g)])
    if after_dec is not None:
        after_dec()
    kv_reads = [('KV', 'KT', c) for c in range(nch)] + [('KV', 'V', g) for g in range(ngr)] + \
               [('KV', 'kr', i) for i in range(8)] + [('KV', 'krnew')]

    def qside(j):
        g0, n = pa.groups[j]
        qb = Y.qbuf_n % 2
        Y.qbuf_n += 1
        tbs = sorted(set(range(g0 // 128, (g0 + n + 127) // 128)))
        bank = 6
        for rt in range(3):
            MM(P, X.pb[bank][:, 0:n], Y.WQN[:, rt, h * 128:(h + 1) * 128], X.cqnT[:, rt, g0:g0 + n], rt == 0, rt == 2,
               [('S', 'WQN')] + [('cqnT', t) for t in tbs], [('ps', bank)])
        CP(P, 'act', Y.q_b[qb][:, 0:n], X.pb[bank][:, 0:n], [('ps', bank)], [('S', 'q_b', qb)])
        hp_ = h % 2
        CP(P, 'pool', Y.qrz[hp_][qb][hp_ * 64:(hp_ + 1) * 64, 0:n], X.qrT[hp_ * 64:(hp_ + 1) * 64, h // 2, g0:g0 + n],
           [('qrT', t) for t in tbs], [('S', 'qrz', hp_, qb)])
        proj_feature_major(X, pa, Y, wg, twg, g0, n, 7)
        gate_from_bank(X, Y, 7, n, qb)
        return qb

    hp = h % 2
    qbs = {0: qside(0)}
    for j in range(len(pa.groups)):
        if j + 1 < len(pa.groups):
            qbs[j + 1] = qside(j + 1)
        qb = qbs[j]
        g0, n = pa.groups[j]
        tbs = sorted(set(range(g0 // 128, (g0 + n + 127) // 128)))
        bg_pump(X, 1)
        units = []
        for (kb, KR, sl) in kbs:
            v = vis_prompt(j, kb, 'mla') if pa.kind == 'p' else (0, None)
            if v is None:
                continue
            units.append((kb, KR, v[0], v[1], sl))
        od = Y.od_n % 2
        Y.od_n += 1
        bO = 2 + od
        bD = 4 + od
        nU = len(units)

        def emit_qk(u):
            kb, KR, c0, mi, sl = units[u]
            sbk = Y.unit_n_base + u
            bS = sbk % 2
            S = X.pb[bS]
            MM(P, S[0:KR, c0:n], X.KTh[:, sl * 128:sl * 128 + KR], Y.q_b[qb][:, c0:n], True, False,
               kv_reads + [('S', 'q_b', qb)], [('ps', bS)])
            MM(P, S[0:KR, c0:n], X.krT[:, sl * 128:sl * 128 + KR],
               Y.qrz[hp][qb][:, c0:n], False, mi is None,
               kv_reads + [('S', 'qrz', hp, qb)], [('ps', bS)])
            if mi is not None:
                MM(P, S[0:KR, c0:c0 + 128], X.identb[0:KR, 0:KR], X.maskb[0:KR, mi, 0:128], False, True,
                   ['identb', 'maskb'], [('ps', bS)])
            pt = sbk % 3
            ACTF(P, Y.PT[pt][0:KR, c0:n], S[0:KR, c0:n], AF.Exp, [('ps', bS)], [('S', 'PT', pt)], scale=MLA_SCALE)

        def emit_pv(u):
            kb, KR, c0, mi, sl = units[u]
            sbk = Y.unit_n_base + u
            pt = sbk % 3
            MM(P, X.pb[bO][:, c0:n], X.Vh[0:KR, sl, :], Y.PT[pt][0:KR, c0:n], u == 0, u == nU - 1,
               kv_reads + [('S', 'PT', pt)], [('ps', bO)])
            MM(P, X.pb[bD][:, c0:n], X.onesb[0:KR, :], Y.PT[pt][0:KR, c0:n], u == 0, u == nU - 1,
               ['onesb', ('S', 'PT', pt)], [('ps', bD)])

        Y.unit_n_base = Y.unit_n
        for u in range(nU + 1):
            if u < nU:
                emit_qk(u)
            if u >= 1:
                emit_pv(u - 1)
        Y.unit_n += nU
        rb = od
        CP_recip(P, Y.Rinv[rb][:, 0:n], X.pb[bD][:, 0:n], [('ps', bD)], [('S', 'Rinv', rb)])
        STT(P, 'dve', Y.tmpO[rb][:, 0:n], X.pb[bO][:, 0:n], 0.5, Y.Rinv[rb][:, 0:n], ALU.mult, ALU.mult,
            [('ps', bO), ('S', 'Rinv', rb)], [('S', 'tmpO', rb)])
        TT(P, 'pool', X.MIXT[:, h, g0:g0 + n], Y.tmpO[rb][:, 0:n], Y.g_b[qb][:, 0:n], ALU.mult,
           [('S', 'tmpO', rb), ('S', 'g_b', qb)], [('MIXT', h, j)])


def mla_sample_absorbed(X, pa, L, Y):
    P = X.P
    n = DEC
    kbs = kblocks(pa)
    pbT6 = X.pb[6][:, :].bitcast(BF16)
    lt_all = [('LT', i) for i in range(8)] + [('LT', 'new')]
    kr_all = [('KV', 'kr', i) for i in range(8)] + [('KV', 'krnew')]
    for h in range(4):
        for rt in range(2):
            i = h * 2 + rt
            TR(P, pbT6[:, i * 128:(i + 1) * 128], Y.WU[:, rt, h * 256:h * 256 + 128], X.identb[:, :],
               [('S', 'WU'), 'identb'], [('ps', 6)])
    CP(P, 'act', Y.WUT[:, :, :], pbT6[:, 0:1024].rearrange("p (a b) -> p a b", b=128), [('ps', 6)], [('S', 'WUT')])
    for h in range(4):
        for rt in range(3):
            MM(P, X.pb[7][:, h * n:(h + 1) * n], Y.WQN[:, rt, h * 128:(h + 1) * 128], X.cqnT[:, rt, 0:n], rt == 0, rt == 2,
               [('S', 'WQN'), ('cqnT', 0)], [('ps', 7)])
    CP(P, 'act', Y.qn_all[:, :], X.pb[7][:, 0:4 * n], [('ps', 7)], [('S', 'qn_all')])
    for h in range(4):
        for rt in range(2):
            c = (rt * 4 + h) * n
            MM(P, X.pb[6][:, c:c + n], Y.WUT[:, h * 2 + rt, :], Y.qn_all[:, h * n:(h + 1) * n], True, True,
               [('S', 'WUT'), ('S', 'qn_all')], [('ps', 6)])
    CP(P, 'act', Y.qabs[:, :, :], X.pb[6][:, 0:8 * n].rearrange("p (a b) -> p a b", b=4 * n), [('ps', 6)], [('S', 'qabs')])
    for h in range(4):
        wg, twg = load_wcol(X, L, Y, 704 + h * 128)
        for dt in range(NDT):
            MM(P, X.pb[7][:, h * n:(h + 1) * n], wg[:, dt, :], X.XT[:, dt, 0:n], dt == 0, dt == NDT - 1,
               [twg, ('XT', 0)], [('ps', 7)])
    ACTF(P, Y.th_all[:, :], X.pb[7][:, 0:4 * n], AF.Tanh, [('ps', 7)], [('S', 'th_all')], scale=0.5)
    STT(P, 'dve', Y.g_all[:, :], Y.th_all[:, :], 1.0, X.pb[7][:, 0:4 * n], ALU.add, ALU.mult,
        [('S', 'th_all'), ('ps', 7)], [('S', 'g_all')])
    MEMSET(P, 'pool', Y.qr_all[:, :], 0.0, [], [('S', 'qr_all')])
    for h in range(4):
        r_ = slice((h % 2) * 64, (h % 2) * 64 + 64)
        CP(P, 'pool', Y.qr_all[r_, h * n:(h + 1) * n], X.qrT[r_, h // 2, 0:n], [('qrT', 0), ('S', 'qr_all')], [('S', 'qr_all')])
    sbanks = (0, 1, 4)
    nU = len(kbs)
    W = 4 * n

    def lat_tok(kb, KR, rt):
        if kb < 16:
            return Y.stage[0:KR, kb, rt * 128:(rt + 1) * 128], ('S', 'WM')
        return X.slat[0:KR, rt * 128:(rt + 1) * 128], 'slat'

    def qk(u):
        kb, KR, sl = kbs[u]
        bS = sbanks[u % 3]
        S = X.pb[bS]
        k0 = sl * 128
        MM(P, S[0:KR, 0:W], X.latT[:, 0, k0:k0 + KR], Y.qabs[:, 0, :], True, False, lt_all + [('S', 'qabs')], [('ps', bS)])
        MM(P, S[0:KR, 0:W], X.latT[:, 1, k0:k0 + KR], Y.qabs[:, 1, :], False, False, lt_all + [('S', 'qabs')], [('ps', bS)])
        MM(P, S[0:KR, 0:W], X.krT[:, k0:k0 + KR], Y.qr_all[:, :], False, True, kr_all + [('S', 'qr_all')], [('ps', bS)])
        pt = u % 3
        ACTF(P, Y.PT[pt][0:KR, 0:W], S[0:KR, 0:W], AF.Exp, [('ps', bS)], [('S', 'PT', pt)], scale=MLA_SCALE)

    def pv(u):
        kb, KR, sl = kbs[u]
        pt = u % 3
        for rt in range(2):
            lt, tk = lat_tok(kb, KR, rt)
            MM(P, X.pb[2 + rt][:, 0:W], lt, Y.PT[pt][0:KR, 0:W], u == 0, u == nU - 1, [tk, ('S', 'PT', pt)], [('ps', 2 + rt)])
        MM(P, X.pb[5][:, 0:W], X.onesb[0:KR, :], Y.PT[pt][0:KR, 0:W], u == 0, u == nU - 1,
           ['onesb', ('S', 'PT', pt)], [('ps', 5)])

    for u in range(nU + 2):
        if u < nU:
            qk(u)
        if u >= 2:
            pv(u - 2)
    CP_recip(P, Y.Rs[:, :], X.pb[5][:, 0:W], [('ps', 5)], [('S', 'Rs')])
    for rt in range(2):
        TT(P, 'dve', Y.OLn[:, rt, :], X.pb[2 + rt][:, 0:W], Y.Rs[:, :], ALU.mult, [('ps', 2 + rt), ('S', 'Rs')], [('S', 'OLn')])
    for h in range(4):
        for rt in range(2):
            MM(P, X.pb[6][:, h * n:(h + 1) * n], Y.WU[:, rt, h * 256 + 128:h * 256 + 256], Y.OLn[:, rt, h * n:(h + 1) * n],
               rt == 0, rt == 1, [('S', 'WU'), ('S', 'OLn')], [('ps', 6)])
    STT(P, 'dve', X.MIXT[:, 0:4, 0:n], X.pb[6][:, 0:W].rearrange("p (h q) -> p h q", h=4), 0.5,
        Y.g_all[:, :].rearrange("p (h q) -> p h q", h=4), ALU.mult, ALU.mult,
        [('ps', 6), ('S', 'g_all')], [('MIXT', h, 0) for h in range(4)])


def CP_recip(P, out, in_, reads, writes):
    return P.op('dve', lambda e: e.reciprocal(out, in_), reads, writes)


def load_logf(X, pa, L, Y, part):
    P = X.P
    if part == 'dma':
        MEMSET(P, 'pool', Y.logf[:, :, :], 0.0, [], [('S', 'logf')])
    if pa.kind == 'p' and part == 'dma':
        for ci in range(4):
            ea = X.exd[L][ci].ap()
            tex = ('exd', L, ci)
            for rr in range(2):
                o = rr * EX_ROWS
                sl = ea[o + 480:o + 488, :].rearrange("a (b c) -> (a b) c", c=16).rearrange("(k p) c -> p k c", p=128)
                P.dma('sp', Y.lgall[:, 8 * ci + rr:8 * ci + 8:2, :], sl, [tex], [('S', 'lgall')])
    elif pa.kind == 'p':
        l3 = Y.lgall[:, :, 0:12].rearrange("p k (s h) -> p k s h", s=3)
        TT(P, 'dve', Y.lgA[:, :, :], l3[:, :, 0, :], l3[:, :, 1, :], ALU.add, [('S', 'lgall')], [('S', 'lgA')])
        TT(P, 'dve', Y.logf[:, :, :], Y.lgA[:, :, :], l3[:, :, 2, :], ALU.add, [('S', 'lgA'), ('S', 'lgall'), ('S', 'logf')],
           [('S', 'logf')])
    elif part == 'dma':
        P.dma('sp', Y.logf[:, 0:16, :], X.c_logf[L].rearrange("(k p) h -> p k h", p=128), [('S', 'logf')], [('S', 'logf')])
    else:
        CP(P, 'dve', Y.logf[0:DEC, 16, :], X.slogf[:, :], ['slogf', ('S', 'logf')], [('S', 'logf')])


def load_kside_fox(X, pa, L, Y, part):
    P = X.P
    fkT4 = X.latT4
    slots = ((0, 64, 0, 64), (128, 256, 64, 192), (320, 384, 192, 256))
    if part == 'fv' and pa.kind != 'p':
        MEMSET(P, 'pool', X.fvA[:, :, 64:128], 1.0, [], [('KV', 'ones')])
        MEMSET(P, 'pool', X.fvA[:, :, 256:320], 1.0, [], [('KV', 'ones')])
    if pa.kind == 'p':
        for ci in range(4):
            ea = X.exd[L][ci].ap()
            tex = ('exd', L, ci)
            for rr in range(2):
                o = rr * EX_ROWS
                if part == 'fk':
                    c0 = rr * 2048 + ci * 512
                    src = ea[o + 128:o + 256, :].rearrange("a (b c) -> (a b) c", c=512).rearrange("(rt p) t -> p rt t", p=128)
                    P.dma('sp', X.latT[:, :, c0:c0 + 512], src, [tex], [('LT', 'fk', ci)])
                else:
                    sv = ea[o + 256:o + 448, :].rearrange("a (b c) -> (a b) c", c=128).rearrange("(k p x) c -> p k (x c)", p=128, x=3)
                    k0 = rr * 16 + ci * 4
                    P.dma('sp', X.fvA[:, k0:k0 + 4, :], sv, [tex], [('KV', 'fv', ci)])
    elif part == 'fk':
        pbT = [X.pb[6][:, :].bitcast(BF16), X.pb[7][:, :].bitcast(BF16)]
        st3 = Y.stage
        cload(X, 'fk', L, st3[:, :, :], [('S', 'WM')])
        for k4 in range(8):
            bsel = k4 % 2
            for i in range(2):
                kb = 2 * k4 + i
                for rt in range(2):
                    TR(P, pbT[bsel][:, (i * 2 + rt) * 128:(i * 2 + rt + 1) * 128], st3[:, kb, rt * 128:(rt + 1) * 128],
                       X.identb[:, :], [('S', 'WM'), 'identb'], [('ps', 6 + bsel)])
            for rt in range(2):
                src = pbT[bsel][:, 0:512].rearrange("p (i r t) -> p i r t", i=2, r=2)[:, :, rt, :]
                CP(P, 'act' if rt == 0 else 'dve', fkT4[:, rt, 2 * k4:2 * k4 + 2, :], src, [('ps', 6 + bsel)],
                   [('LT', 'fk', k4)])
        CP(P, 'dve', X.latT[:, :, PAST:PAST + DEC], X.sfkT[:, :, :], ['sfkT'], [('LT', 'fk', 'new')])
    else:
        fv_c = ('fv', L) in X.cb_ready
        if fv_c:
            cv = X.cb['fv'][L].rearrange("(k p) c -> p k c", p=128)
        else:
            cv = X.c_fv[L].rearrange("(k p) c -> p k c", p=128)
        for (d0, d1, s0, s1) in slots:
            P.dma('sp' if fv_c else 'pool', X.fvA[:, 0:16, d0:d1], cv[:, :, s0:s1],
                  [('cb', 'fv', L, i) for i in range(X.bg_np[('cb', 'fv', L)])] if fv_c else [], [('KV', 'fv', 0)])
            CP(P, 'dve', X.fvA[0:DEC, 16, d0:d1], X.sfv[:, s0:s1], ['sfv'], [('KV', 'fv', 1)])


def fox_bias_prep(X, pa, L, Y):
    P = X.P
    lf = Y.logf[:, :, :].rearrange("p k h -> p (k h)")
    MM(P, X.pb[6][:, 0:128], X.trif[:, :], lf, True, True, ['trif', ('S', 'logf')], [('ps', 6)])
    MM(P, X.pb[7][:, 0:128], X.onesf[:, :], lf, True, True, ['onesf', ('S', 'logf')], [('ps', 7)])
    f2 = lambda t: t[:, :, :].rearrange("p k h -> p (k h)")
    CP(P, 'act', f2(Y.cumin), X.pb[6][:, 0:128], [('ps', 6)], [('S', 'cumin')])
    CP(P, 'dve', f2(Y.tot), X.pb[7][:, 0:128], [('ps', 7)], [('S', 'tot')])
    cur, tcur = Y.tot, ('S', 'tot')
    pp = [(Y.scA, ('S', 'scA')), (Y.scB, ('S', 'scB'))]
    k = 0
    d = 1
    while d < 32:
        nxt, tnxt = pp[k % 2]
        k += 1
        CP(P, 'dve', nxt[:, 0:d, :], cur[:, 0:d, :], [tcur], [tnxt])
        TT(P, 'dve', nxt[:, d:32, :], cur[:, d:32, :], cur[:, 0:32 - d, :], ALU.add, [tcur], [tnxt])
        cur, tcur = nxt, tnxt
        d *= 2
    Y.incl, Y.tincl = cur, tcur
    TT(P, 'dve', Y.lgB[:, :, :], cur[:, :, :], Y.tot[:, :, :], ALU.subtract, [tcur, ('S', 'tot')], [('S', 'lgB')])
    TT(P, 'dve', Y.cum[:, :, :], Y.cumin[:, :, :], Y.lgB[:, :, :], ALU.add, [('S', 'cumin'), ('S', 'lgB')], [('S', 'cum')])


def pair_attention(X, pa, L, Y, kind, p, pre_w=None):
    P = X.P
    if kind == 'fox':
        qcol = 1216 + p * 128; gcol = 1988 + p * 128; tile = 4 + p
        kbs = kblocks(pa)
        scale = FOX_SCALE
    else:
        qcol = 2244 + p * 128; gcol = 2500 + p * 128; tile = 6 + p
        kbs = [(0, 128, 0), (1, 128, 1)]
        scale = MEM_SCALE
    if pre_w is not None:
        (wq, twq), (wg, twg) = pre_w
    else:
        wq, twq = load_wcol(X, L, Y, qcol)
        wg, twg = load_wcol(X, L, Y, gcol)
    if kind == 'fox':
        kreads = [('LT', 'fk', i) for i in range(8)] + [('LT', 'fk', 'new')]
        vreads = [('KV', 'fv', i) for i in range(4)] + [('KV', 'ones')]
    else:
        kreads = ['mkT']
        vreads = ['mvA', 'mvA1', 'mvA2']
    vslot = (0, 64, 192, 256)

    def qside(j):
        g0, n = pa.groups[j]
        qb = Y.qbuf_n % 2
        Y.qbuf_n += 1
        proj_feature_major(X, pa, Y, wq, twq, g0, n, 6)
        CP(P, 'act', Y.qz[qb][0][0:64, 0:n], X.pb[6][0:64, 0:n], [('ps', 6)], [('S', 'q_b', qb)])
        CP(P, 'act', Y.qz[qb][1][64:128, 0:n], X.pb[6][64:128, 0:n], [('ps', 6)], [('S', 'q_b', qb)])
        proj_feature_major(X, pa, Y, wg, twg, g0, n, 7)
        gate_from_bank(X, Y, 7, n, qb)
        return qb

    use_corr = (kind == 'fox' and pa.kind == 'p') and not os.environ.get('KDBG_NOCORR')
    sbanks = (0, 1, 4)
    items = [(j, hp) for j in range(len(pa.groups)) for hp in range(2)]

    def prep(idx):
        if kind != 'fox':
            return
        j, hp = items[idx]
        h = 2 * p + hp
        bb = idx % 2
        kbG = 8 * j + 7 if pa.kind == 'p' else 16
        TS(P, 'dve', Y.bias[bb][:, :], Y.cum[:, :, h], Y.incl[:, kbG, h:h + 1], -1.0, ALU.subtract,
           ALU.mult, [('S', 'cum'), Y.tincl], [('S', 'bias', bb)])
        if use_corr:
            TS(P, 'dve', Y.d4[:, :], Y.incl[:, 8 * j + 1:8 * j + 8:2, h], Y.incl[:, kbG, h:h + 1], 1.0 / scale,
               ALU.subtract, ALU.mult, [Y.tincl], [('S', 'd4')])
            CP(P, 'dve', Y.hi4[:, :], Y.d4[:, :], [('S', 'd4')], [('S', 'hi4')])
            TT(P, 'dve', Y.r4[:, :], Y.d4[:, :], Y.hi4[:, :], ALU.subtract, [('S', 'd4'), ('S', 'hi4')], [('S', 'r4')])
            TS(P, 'dve', Y.d4[:, :], Y.hi4[:, :], X.identf[:, 0:1], None, ALU.mult, None,
               [('S', 'hi4'), ('S', 'r4'), 'identf'], [('S', 'd4')])
            STT(P, 'dve', Y.r4[:, :], Y.r4[:, :], X.identf[:, 32:33], Y.d4[:, :], ALU.mult, ALU.add,
                [('S', 'd4'), 'identf'], [('S', 'r4')])
            CP(P, 'dve', Y.CR[bb][:, :].rearrange("p (i c) -> p i c", i=4),
               Y.r4[:, :].unsqueeze(2).broadcast_to([128, 4, 128]), [('S', 'r4')], [('S', 'CR', bb)])

    def run_item(idx, qb):
        j, hp = items[idx]
        g0, n = pa.groups[j]
        h = 2 * p + hp
        bb = idx % 2
        orows = slice(0, 64) if hp == 0 else slice(64, 128)
        drows = slice(64, 128) if hp == 0 else slice(0, 64)
        units = []
        for (kb, KR, sl) in kbs:
            if kind == 'fox' and pa.kind == 'p':
                v = vis_prompt(j, kb, 'fox')
            elif kind == 'fox':
                v = (0, 4 if kb == 16 else None)
            else:
                v = (0, None)
            if v is None:
                continue
            units.append((kb, KR, v[0], v[1], sl))
        nU = len(units)
        od = Y.od_n % 2
        Y.od_n += 1
        bO = 2 + od
        base = Y.unit_n

        def emit_qk(u):
            kb, KR, c0, mi, sl = units[u]
            sbk = base + u
            bS = sbanks[sbk % 3]
            S = X.pb[bS]
            if kind == 'fox':
                kt = X.latT[:, p, sl * 128:sl * 128 + KR]
            else:
                kt = X.mkT[:, p, sl * 128:sl * 128 + KR]
            MM(P, S[0:KR, c0:n], kt, Y.qz[qb][hp][:, c0:n], True, (mi is None) and (not use_corr),
               kreads + [('S', 'q_b', qb)], [('ps', bS)])
            if use_corr:
                MM(P, S[0:KR, c0:n], X.onesb[:, 0:KR], Y.CR[bb][:, c0:n], False, mi is None,
                   ['onesb', ('S', 'CR', bb)], [('ps', bS)])
            if mi is not None:
                w_ = min(128, n - c0)
                MM(P, S[0:KR, c0:c0 + w_], X.identb[0:KR, 0:KR], X.maskb[0:KR, mi, 0:w_], False, True,
                   ['identb', 'maskb'], [('ps', bS)])
            pt = sbk % 3
            if kind == 'fox':
                ACTF(P, Y.PT[pt][0:KR, c0:n], S[0:KR, c0:n], AF.Exp, [('ps', bS), ('S', 'bias', bb)],
                     [('S', 'PT', pt)], bias=Y.bias[bb][0:KR, kb:kb + 1], scale=scale)
            else:
                ACTF(P, Y.PT[pt][0:KR, c0:n], S[0:KR, c0:n], AF.Exp, [('ps', bS)], [('S', 'PT', pt)], scale=scale)

        def emit_pv(u):
            kb, KR, c0, mi, sl = units[u]
            sbk = base + u
            pt = sbk % 3
            vs = vslot[h]
            if kind == 'fox':
                va = X.fvA[0:KR, sl, vs:vs + 128]
            else:
                va = X.mvA[0:KR, sl, vs:vs + 128]
            MM(P, X.pb[bO][:, c0:n], va, Y.PT[pt][0:KR, c0:n], u == 0, u == nU - 1,
               vreads + [('S', 'PT', pt)], [('ps', bO)])

        for u in range(nU + 2):
            if u < nU:
                emit_qk(u)
            if u >= 2:
                emit_pv(u - 2)
        Y.unit_n += nU
        rb = od
        CP_recip(P, Y.Rinv[rb][orows, 0:n], X.pb[bO][drows, 0:n], [('ps', bO)], [('S', 'Rinv', rb)])
        STT(P, 'dve', Y.tmpO[rb][orows, 0:n], X.pb[bO][orows, 0:n], 0.5, Y.Rinv[rb][orows, 0:n], ALU.mult, ALU.mult,
            [('ps', bO), ('S', 'Rinv', rb)], [('S', 'tmpO', rb)])
        TT(P, 'pool', X.MIXT[orows, tile, g0:g0 + n], Y.tmpO[rb][orows, 0:n], Y.g_b[qb][orows, 0:n], ALU.mult,
           [('S', 'tmpO', rb), ('S', 'g_b', qb)], [('MIXT', tile, j, hp)])

    qbs = {0: qside(0)}
    prep(0)
    for idx, (j, hp) in enumerate(items):
        if hp == 0 and j + 1 < len(pa.groups):
            qbs[j + 1] = qside(j + 1)
        if idx + 1 < len(items):
            prep(idx + 1)
        if kind == 'fox':
            bg_pump(X, 1)
        run_item(idx, qbs[j])


def phase3(X, pa, L):
    P = X.P
    R = pa.R
    barrier(X, 'S')
    cv = Carve(X.scr, X.nscr)
    WO = cv.take([NDT, D], BF16)
    lng = cv.take([D], F32)
    lnb = cv.take([D], F32)
    NB = 4
    xtok = [cv.take([D], F32) for _ in range(2)]
    res = [cv.take([D], F32) for _ in range(NB)]
    yy = [cv.take([D], F32) for _ in range(NB)]
    junk = cv.take([D], BF16)
    stat = [cv.take([8], F32) for _ in range(NB)]
    wload(X, 'out', L, WO[:, :, :], lambda w: w.rearrange("(et p) d -> p et d", p=128), [('S', 'WO')])
    P.dma('sp', lng[:, :], X.ln_g[L:L + 1, :].partition_broadcast(128), [], [('S', 'ln')])
    P.dma('sp', lnb[:, :], X.ln_b[L:L + 1, :].partition_broadcast(128), [], [('S', 'ln')])
    if pa.kind == 'p':
        rsrc = X.xp if L == 0 else X.y1
        ydst = X.y1 if L == 0 else X.yp
    else:
        rsrc = X.xs if L == 0 else X.ys1
        ydst = X.ys1 if L == 0 else X.ys
    mix_reads = [('MIXT', h, j) for h in range(4) for j in range(4)] + \
                [('MIXT', t, j, hp) for t in range(4, 8) for j in range(4) for hp in range(2)]

    def s1(tb):
        b2 = tb % 2
        b = tb % NB
        r0 = tb * 128
        tok = lambda n: ('S', n, b)
        bA = 0 + 2 * b2; bB = 1 + 2 * b2
        P.dma('act', xtok[b2][0:R, :], rsrc[r0:r0 + R, :], [('y1dram', tb)] if L == 1 else [], [('S', 'xtok', b2)])
        for et in range(NDT):
            lhsT = X.MIXT[:, et, r0:r0 + R]
            MM(P, X.pb[bA][0:R, :], lhsT, WO[:, et, 0:512], et == 0, et == NDT - 1, mix_reads + [('S', 'WO')], [('ps', bA)])
            MM(P, X.pb[bB][0:R, :], lhsT, WO[:, et, 512:1024], et == 0, et == NDT - 1, mix_reads + [('S', 'WO')], [('ps', bB)])
        STT(P, 'dve', res[b][0:R, 0:512], xtok[b2][0:R, 0:512], ALPHA, X.pb[bA][0:R, :], ALU.mult, ALU.add,
            [('S', 'xtok', b2), ('ps', bA)], [tok('res')])
        STT(P, 'dve', res[b][0:R, 512:1024], xtok[b2][0:R, 512:1024], ALPHA, X.pb[bB][0:R, :], ALU.mult, ALU.add,
            [('S', 'xtok', b2), ('ps', bB)], [tok('res')])

    def s1c(tb):
        b = tb % NB
        tok = lambda n: ('S', n, b)
        st_ = stat[b]
        ACTF(P, junk[0:R, :], res[b][0:R, :], AF.Copy, [tok('res')], [('S', 'junk3'), tok('stat')], accum=st_[0:R, 0:1])
        ACTF(P, junk[0:R, :], res[b][0:R, :], AF.Square, [tok('res')], [('S', 'junk3'), tok('stat')], accum=st_[0:R, 1:2])

    def s1b(tb):
        b = tb % NB
        r0 = tb * 128
        tok = lambda n: ('S', n, b)
        st_ = stat[b]
        TS(P, 'dve', st_[0:R, 2:3], st_[0:R, 0:1], 1.0 / D, None, ALU.mult, None, [tok('stat')], [tok('stat')])
        TT(P, 'dve', st_[0:R, 3:4], st_[0:R, 2:3], st_[0:R, 2:3], ALU.mult, [tok('stat')], [tok('stat')])
        STT(P, 'dve', st_[0:R, 4:5], st_[0:R, 1:2], 1.0 / D, st_[0:R, 3:4], ALU.mult, ALU.subtract,
            [tok('stat')], [tok('stat')])
        ACTF(P, st_[0:R, 5:6], st_[0:R, 4:5], AF.Ln, [tok('stat'), 'epsb'], [tok('stat')], bias=X.epsb[0:R, :], scale=1.0)
        ACTF(P, st_[0:R, 6:7], st_[0:R, 5:6], AF.Exp, [tok('stat')], [tok('stat')], scale=-0.5)
        TS(P, 'dve', res[b][0:R, :], res[b][0:R, :], st_[0:R, 2:3], st_[0:R, 6:7], ALU.subtract, ALU.mult,
           [tok('res'), tok('stat')], [tok('res')])
        TT(P, 'pool', yy[b][0:R, :], res[b][0:R, :], lng[0:R, :], ALU.mult, [tok('res'), ('S', 'ln')], [tok('yy')])
        TT(P, 'pool', yy[b][0:R, :], yy[b][0:R, :], lnb[0:R, :], ALU.add, [tok('yy'), ('S', 'ln')], [tok('yy')])
        P.dma('sp', ydst[r0:r0 + R, :], yy[b][0:R, :], [tok('yy')], [('y1dram', tb)] if L == 0 else [])

    def s2(tb):
        b2 = tb % 2
        b = tb % NB
        if L == 0:
            transpose_rows_to_XT(X, pa, yy[b], ('S', 'yy', b), tb, R, (4 + 2 * b2, 5 + 2 * b2))

    for i in range(pa.NTB + 3):
        if i < pa.NTB:
            s1(i)
        if 0 <= i - 1 < pa.NTB:
            s1b(i - 1)
        if i < pa.NTB:
            s1c(i)
        if 0 <= i - 3 < pa.NTB:
            s2(i - 3)


_NC_CACHE = {}


def _rope_tables(pos):
    half = 32
    inv = (10000.0 ** (-np.arange(half, dtype=np.float32) / half)).astype(np.float32)
    ang = pos.astype(np.float32)[:, None] * inv[None, :]
    cos = np.cos(ang).astype(np.float32)
    sin = np.sin(ang).astype(np.float32)
    return np.concatenate([cos, cos, -sin, sin], axis=1).astype(np.float32)


def _mask_tiles(r):
    k = np.arange(128)[:, None]
    q = np.arange(128)[None, :]
    tri = np.where(k <= q, 0.0, NEG).astype(np.float32)
    chunk = np.where((k >= 64) & (q < 64), NEG, 0.0).astype(np.float32)
    zeros = np.zeros((128, 128), np.float32)
    neg = np.full((128, 128), NEG, np.float32)
    if r == 0:
        t = [chunk, neg, tri, neg, tri]
    else:
        t = [zeros, chunk, zeros, tri, tri]
    return np.ascontiguousarray(np.stack(t, axis=1))


def kernel(x_prompt, x_sample, cache_mla_latent, cache_mla_krope, cache_fox_k, cache_fox_v, cache_fox_logf,
           cache_mem_k, cache_mem_v, mem_prompt, w_in, b_fox_f, mla_q_norm, mla_kv_norm, w_uq, w_ukv,
           w_mem_kv, w_out, ln_g, ln_b):
    f = lambda a: np.ascontiguousarray(np.asarray(a, dtype=np.float32))
    x_prompt = f(x_prompt); x_sample = f(x_sample)
    if 'nc' not in _NC_CACHE:
        _NC_CACHE['nc'] = build_program()
    nc = _NC_CACHE['nc']
    shared = dict(w_in=f(w_in), b_fox_f=f(b_fox_f), mla_q_norm=f(mla_q_norm), mla_kv_norm=f(mla_kv_norm),
                  w_uq=f(w_uq), w_ukv=f(w_ukv), w_mem_kv=f(w_mem_kv), w_out=f(w_out), ln_g=f(ln_g), ln_b=f(ln_b))
    ropeS = _rope_tables(PAST + np.arange(DEC)).reshape(DEC, 1, 128)
    in_maps = []
    for c in range(NCORES):
        b, r = c // 2, c % 2
        xp = x_prompt[b].reshape(32, 128, D)[r::2].reshape(TOWN, D)
        pos = ((2 * np.arange(16)[:, None] + r) * 128 + np.arange(128)[None, :])
        ropeP = _rope_tables(pos.reshape(-1)).reshape(16, 128, 128).transpose(1, 0, 2)
        m = dict(shared)
        m.update(
            xp=np.ascontiguousarray(xp), xs=f(x_sample[c]),
            c_lat=f(np.asarray(cache_mla_latent)[:, c]), c_kr=f(np.asarray(cache_mla_krope)[:, c]),
            c_fk=f(np.asarray(cache_fox_k)[:, c]).reshape(DEPTH, PAST, 256),
            c_fv=f(np.asarray(cache_fox_v)[:, c]).reshape(DEPTH, PAST, 256),
            c_logf=f(np.asarray(cache_fox_logf)[:, c]),
            c_mk=f(np.asarray(cache_mem_k)[:, c]).reshape(DEPTH, 256, 256),
            c_mv=f(np.asarray(cache_mem_v)[:, c]).reshape(DEPTH, 256, 256),
            memp=f(np.asarray(mem_prompt)[b]),
            ropeP=np.ascontiguousarray(ropeP), ropeS=np.ascontiguousarray(ropeS), masks=_mask_tiles(r))
        in_maps.append(m)
    res = run_bass_kernel_spmd(nc, in_maps, core_ids=list(range(NCORES)))
    R = res.results
    B = 4
    y_prompt = np.empty((B, SEQ, D), np.float32)
    y_sample = np.empty((NCORES, DEC, D), np.float32)
    p_lat = np.empty((DEPTH, B, SEQ, 256), np.float32); p_kr = np.empty((DEPTH, B, SEQ, 64), np.float32)
    p_fk = np.empty((DEPTH, B, SEQ, 256), np.float32); p_fv = np.empty((DEPTH, B, SEQ, 256), np.float32)
    p_logf = np.empty((DEPTH, B, SEQ, 4), np.float32)
    p_mk = np.empty((DEPTH, B, 256, 256), np.float32); p_mv = np.empty((DEPTH, B, 256, 256), np.float32)
    s_lat = np.empty((DEPTH, NCORES, DEC, 256), np.float32); s_kr = np.empty((DEPTH, NCORES, DEC, 64), np.float32)
    s_fk = np.empty((DEPTH, NCORES, DEC, 256), np.float32); s_fv = np.empty((DEPTH, NCORES, DEC, 256), np.float32)
    s_logf = np.empty((DEPTH, NCORES, DEC, 4), np.float32)
    for c in range(NCORES):
        b, r = c // 2, c % 2
        o = R[c]
        y_prompt[b].reshape(32, 128, D)[r::2] = np.asarray(o["yp"]).reshape(16, 128, D)
        y_sample[c] = np.asarray(o["ys"])
        for (dst, key, w) in ((p_lat, "o_lat", 256), (p_kr, "o_kr", 64), (p_fk, "o_fk", 256), (p_fv, "o_fv", 256),
                              (p_logf, "o_logf", 4)):
            a = np.asarray(o[key]).reshape(DEPTH, 16, 128, w)
            for l in range(DEPTH):
                dst[l, b].reshape(32, 128, w)[r::2] = a[l]
        if r == 0:
            p_mk[:, b] = np.asarray(o["o_mk"]); p_mv[:, b] = np.asarray(o["o_mv"])
        s_lat[:, c] = np.asarray(o["s_lat"]); s_kr[:, c] = np.asarray(o["s_kr"])
        s_fk[:, c] = np.asarray(o["s_fk"]); s_fv[:, c] = np.asarray(o["s_fv"]); s_logf[:, c] = np.asarray(o["s_logf"])
    return (y_prompt, y_sample, p_lat, p_kr, p_fk.reshape(DEPTH, B, SEQ, 4, 64), p_fv.reshape(DEPTH, B, SEQ, 4, 64),
            p_logf, p_mk.reshape(DEPTH, B, 256, 4, 64), p_mv.reshape(DEPTH, B, 256, 4, 64),
            s_lat, s_kr, s_fk.reshape(DEPTH, NCORES, DEC, 4, 64), s_fv.reshape(DEPTH, NCORES, DEC, 4, 64), s_logf)
```
